# Optimizing a Trainium2 kernel written in Bass

```python
import math
import jax, jax.numpy as jnp
from jax import lax
import numpy as np

D_MODEL = 1024
BATCH = 16
SEQ = 4096
DEPTH = 2

N_MIXERS = 2
N_META = 16
N_HEADS = 16
N_KV_HEADS = 4
HEAD_DIM = 64
GQA_GROUP = N_HEADS // N_KV_HEADS
WINDOW = 128
BLOCK = 128
ALIBI_MAX_EXP = 8.0
QKV_DIM = (N_HEADS + 2 * N_KV_HEADS) * HEAD_DIM
SSM_GROUP = 16
SSM_N_GROUPS = D_MODEL // SSM_GROUP
SSM_STATE = 64
DT_MIN = 1e-3
DT_MAX = 1e-1
LAMBDA_RE_MAX = -1e-4
D_FF = 4 * D_MODEL
RMS_EPS = 1e-6
NEG_INF = -1e30
N_ATTN_LAYERS = (DEPTH + 1) // 2
N_SSM_LAYERS = DEPTH // 2

kernel_name = "hybrid_swa_sink_alibi_s5_sqrelu_meta"


def rms_norm(x, w):
    xf = x.astype(jnp.float32)
    y = xf * lax.rsqrt(jnp.mean(xf * xf, axis=-1, keepdims=True) + RMS_EPS)
    return (y * w.astype(jnp.float32)).astype(x.dtype)


def alibi_slopes():
    h = jnp.arange(1, N_HEADS + 1, dtype=jnp.float32)
    return jnp.exp2(-ALIBI_MAX_EXP * h / N_HEADS)


def swa_sink_attention(h, norm_w, w_qkv, sinks, w_o):
    b, l, _ = h.shape
    pad = (-N_META) % BLOCK
    lp = l + pad
    nb = lp // BLOCK
    hn = rms_norm(h, norm_w)
    qkv = hn @ w_qkv
    q, k, v = jnp.split(qkv, [N_HEADS * HEAD_DIM, (N_HEADS + N_KV_HEADS) * HEAD_DIM], axis=-1)
    q = q.reshape(b, l, N_KV_HEADS, GQA_GROUP, HEAD_DIM) * (HEAD_DIM ** -0.5)
    k = k.reshape(b, l, N_KV_HEADS, HEAD_DIM)
    v = v.reshape(b, l, N_KV_HEADS, HEAD_DIM)
    k_meta, v_meta = k[:, :N_META], v[:, :N_META]

    def front_pad(t):
        return jnp.pad(t, [(0, 0), (pad, 0)] + [(0, 0)] * (t.ndim - 2))

    qb = front_pad(q).reshape(b, nb, BLOCK, N_KV_HEADS, GQA_GROUP, HEAD_DIM)
    kb = front_pad(k).reshape(b, nb, BLOCK, N_KV_HEADS, HEAD_DIM)
    vb = front_pad(v).reshape(b, nb, BLOCK, N_KV_HEADS, HEAD_DIM)

    def band(t):
        prev = jnp.pad(t, [(0, 0), (1, 0)] + [(0, 0)] * (t.ndim - 2))[:, :-1]
        return jnp.concatenate([prev, t], axis=2)

    k_band, v_band = band(kb), band(vb)

    blk = jnp.arange(nb)[:, None, None]
    qi = jnp.arange(BLOCK)[None, :, None]
    kj = jnp.arange(2 * BLOCK)[None, None, :]
    q_pos = blk * BLOCK + qi - pad
    k_pos = (blk - 1) * BLOCK + kj - pad
    dist = q_pos - k_pos
    band_ok = (dist >= 0) & (dist < WINDOW) & (k_pos >= N_META)
    meta_ok = jnp.arange(N_META)[None, None, :] <= q_pos

    slopes = alibi_slopes().reshape(N_KV_HEADS, GQA_GROUP)[:, :, None, None]
    alibi = -slopes * dist[:, None, None].astype(jnp.float32)

    s_band = jnp.einsum('bnqkgd,bnskd->bnkgqs', qb, k_band).astype(jnp.float32)
    s_band = jnp.where(band_ok[:, None, None], s_band + alibi, NEG_INF)
    s_meta = jnp.einsum('bnqkgd,bmkd->bnkgqm', qb, k_meta).astype(jnp.float32)
    s_meta = jnp.where(meta_ok[:, None, None], s_meta, NEG_INF)
    sink = jnp.broadcast_to(
        sinks.astype(jnp.float32).reshape(N_KV_HEADS, GQA_GROUP)[None, None, :, :, None, None],
        (b, nb, N_KV_HEADS, GQA_GROUP, BLOCK, 1))
    probs = jax.nn.softmax(jnp.concatenate([s_meta, s_band, sink], axis=-1), axis=-1)
    p_meta = probs[..., :N_META].astype(v.dtype)
    p_band = probs[..., N_META:N_META + 2 * BLOCK].astype(v.dtype)
    out = (jnp.einsum('bnkgqs,bnskd->bnqkgd', p_band, v_band)
           + jnp.einsum('bnkgqm,bmkd->bnqkgd', p_meta, v_meta))
    out = out.reshape(b, lp, N_HEADS * HEAD_DIM)[:, pad:]
    return out @ w_o


def _linear_recurrence_combine(left, right):
    a_i, b_i = left
    a_j, b_j = right
    return (a_j * a_i, a_j * b_i + b_j)


def s5_mixer(h, norm_w, lam_re, lam_im, log_dt, b_re, b_im, c_re, c_im, d_skip, w_glu):
    b, l, _ = h.shape
    u = rms_norm(h, norm_w).astype(jnp.float32).reshape(b, l, SSM_N_GROUPS, SSM_GROUP)
    lam = lax.complex(jnp.minimum(lam_re.astype(jnp.float32), LAMBDA_RE_MAX),
                      lam_im.astype(jnp.float32))
    dt = jnp.exp(log_dt.astype(jnp.float32))[:, None]
    a_bar = jnp.exp(lam * dt)
    b_c = lax.complex(b_re.astype(jnp.float32), b_im.astype(jnp.float32))
    c_c = lax.complex(c_re.astype(jnp.float32), c_im.astype(jnp.float32))
    b_bar = ((a_bar - 1.0) / lam)[:, :, None] * b_c
    bu = jnp.einsum('blgc,gpc->blgp', u.astype(jnp.complex64), b_bar)
    a_seq = jnp.broadcast_to(a_bar[None, None], (1, l, SSM_N_GROUPS, SSM_STATE))
    _, states = lax.associative_scan(_linear_recurrence_combine, (a_seq, bu), axis=1)
    y = jnp.real(jnp.einsum('blgp,gcp->blgc', states, c_c)) \
        + d_skip.astype(jnp.float32).reshape(SSM_N_GROUPS, SSM_GROUP) * u
    y = jax.nn.gelu(y.reshape(b, l, D_MODEL)).astype(h.dtype)
    val, gate = jnp.split(y @ w_glu, 2, axis=-1)
    return val * jax.nn.sigmoid(gate)


def sq_relu_mlp(h, norm_w, w_up, w_down):
    a = jax.nn.relu(rms_norm(h, norm_w) @ w_up)
    return (a * a) @ w_down


def setup_inputs(seed: int = 0) -> dict:
    key = jax.random.key(seed)
    ks = jax.random.split(key, 24)
    f32 = jnp.float32
    na, ns, G, P, C = N_ATTN_LAYERS, N_SSM_LAYERS, SSM_N_GROUPS, SSM_STATE, SSM_GROUP
    x = jax.random.normal(ks[0], (BATCH, SEQ, D_MODEL), f32)
    meta_tokens = jax.random.normal(ks[1], (N_META, D_MODEL), f32)
    attn_norm_w = 1.0 + 0.02 * jax.random.normal(ks[2], (na, D_MODEL), f32)
    attn_w_qkv = jax.random.normal(ks[3], (na, D_MODEL, QKV_DIM), f32) * D_MODEL ** -0.5
    attn_sinks = 0.5 * jax.random.normal(ks[4], (na, N_HEADS), f32)
    attn_w_o = jax.random.normal(ks[5], (na, N_HEADS * HEAD_DIM, D_MODEL), f32) * (N_HEADS * HEAD_DIM) ** -0.5
    ssm_norm_w = 1.0 + 0.02 * jax.random.normal(ks[6], (ns, D_MODEL), f32)
    ssm_lambda_re = -0.5 + 0.01 * jax.random.normal(ks[7], (ns, G, P), f32)
    ssm_lambda_im = jnp.broadcast_to(jnp.pi * jnp.arange(P, dtype=f32), (ns, G, P)) \
        + 0.01 * jax.random.normal(ks[8], (ns, G, P), f32)
    ssm_log_dt = jax.random.uniform(ks[9], (ns, G), f32, math.log(DT_MIN), math.log(DT_MAX))
    ssm_b_re = jax.random.normal(ks[10], (ns, G, P, C), f32) * (2.0 * C) ** -0.5
    ssm_b_im = jax.random.normal(ks[11], (ns, G, P, C), f32) * (2.0 * C) ** -0.5
    ssm_c_re = jax.random.normal(ks[12], (ns, G, C, P), f32) * (2.0 * P) ** -0.5
    ssm_c_im = jax.random.normal(ks[13], (ns, G, C, P), f32) * (2.0 * P) ** -0.5
    ssm_d = jax.random.normal(ks[14], (ns, D_MODEL), f32)
    ssm_w_glu = jax.random.normal(ks[15], (ns, D_MODEL, 2 * D_MODEL), f32) * D_MODEL ** -0.5
    mlp_norm_w = 1.0 + 0.02 * jax.random.normal(ks[16], (DEPTH, D_MODEL), f32)
    mlp_w_up = jax.random.normal(ks[17], (DEPTH, D_MODEL, D_FF), f32) * D_MODEL ** -0.5
    mlp_w_down = jax.random.normal(ks[18], (DEPTH, D_FF, D_MODEL), f32) * D_FF ** -0.5
    final_norm_w = 1.0 + 0.02 * jax.random.normal(ks[19], (D_MODEL,), f32)
    return {"x": x, "meta_tokens": meta_tokens,
            "attn_norm_w": attn_norm_w, "attn_w_qkv": attn_w_qkv,
            "attn_sinks": attn_sinks, "attn_w_o": attn_w_o,
            "ssm_norm_w": ssm_norm_w, "ssm_lambda_re": ssm_lambda_re,
            "ssm_lambda_im": ssm_lambda_im, "ssm_log_dt": ssm_log_dt,
            "ssm_b_re": ssm_b_re, "ssm_b_im": ssm_b_im,
            "ssm_c_re": ssm_c_re, "ssm_c_im": ssm_c_im,
            "ssm_d": ssm_d, "ssm_w_glu": ssm_w_glu,
            "mlp_norm_w": mlp_norm_w, "mlp_w_up": mlp_w_up, "mlp_w_down": mlp_w_down,
            "final_norm_w": final_norm_w}


def reference(x, meta_tokens, attn_norm_w, attn_w_qkv, attn_sinks, attn_w_o,
              ssm_norm_w, ssm_lambda_re, ssm_lambda_im, ssm_log_dt,
              ssm_b_re, ssm_b_im, ssm_c_re, ssm_c_im, ssm_d, ssm_w_glu,
              mlp_norm_w, mlp_w_up, mlp_w_down, final_norm_w):
    b = x.shape[0]
    meta = jnp.broadcast_to(meta_tokens.astype(x.dtype)[None], (b, N_META, D_MODEL))
    h = jnp.concatenate([meta, x], axis=1)
    for i in range(DEPTH):
        j = i // N_MIXERS
        if i % N_MIXERS == 0:
            h = h + swa_sink_attention(h, attn_norm_w[j], attn_w_qkv[j], attn_sinks[j], attn_w_o[j])
        else:
            h = h + s5_mixer(h, ssm_norm_w[j], ssm_lambda_re[j], ssm_lambda_im[j], ssm_log_dt[j],
                             ssm_b_re[j], ssm_b_im[j], ssm_c_re[j], ssm_c_im[j], ssm_d[j], ssm_w_glu[j])
        h = h + sq_relu_mlp(h, mlp_norm_w[i], mlp_w_up[i], mlp_w_down[i])
    return rms_norm(h[:, N_META:], final_norm_w)
```

```python
import numpy as np
from contextlib import ExitStack
import concourse.bass as bass
import concourse.mybir as mybir
from concourse.bass_utils import run_bass_kernel_spmd

F32 = mybir.dt.float32
BF16 = mybir.dt.bfloat16
I32 = mybir.dt.int32
ALU = mybir.AluOpType
AF = mybir.ActivationFunctionType

D = 1024
SEQ = 4096
NCORE = 8
ENGS = ('pe', 'act', 'dve', 'pool', 'sp')
BLOCKNAME = {'pe': 'tensor', 'act': 'scalar', 'dve': 'vector', 'pool': 'gpsimd', 'sp': 'sync'}
PI = float(np.pi)
NSLOT = 5
SLOTE = 4096


class Prog:
    def __init__(self, nc, es):
        self.nc = nc
        self.es = es
        self.ops = {e: [] for e in ENGS}
        self.cnt = {e: 0 for e in ENGS}
        self.known = {e: {} for e in ENGS}
        self.state = {}
        self.dmacnt = {}
        self.sems = {}
        self.regions = {}
        self.groupkeys = {}

    def _waits(self, eng, reads, writes):
        deps = {}

        def addd(d):
            for k, v in d.items():
                if deps.get(k, 0) < v:
                    deps[k] = v
        for key in reads:
            st = self.state.get(key)
            if st:
                addd(st[0])
        for key in writes:
            st = self.state.get(key)
            if st:
                addd(st[0])
                addd(st[1])
        waits = []
        for k, v in deps.items():
            if k == 'pe' and eng == 'pe':
                continue
            if k in ENGS:
                assert v <= self.cnt[k], f"dep on unemitted milestone {k} {v} > {self.cnt[k]}"
            if self.known[eng].get(k, 0) >= v:
                continue
            self.known[eng][k] = v
            waits.append((k, v))
        return waits

    def _update(self, tag, v, reads, writes):
        ws = set(writes)
        for key in ws:
            self.state[key] = [{tag: v}, {}]
        for key in reads:
            if key in ws:
                continue
            st = self.state.setdefault(key, [{}, {}])
            st[1][tag] = v

    def op(self, eng, fn, reads=(), writes=()):
        waits = self._waits(eng, reads, writes)
        self.cnt[eng] += 1
        self.ops[eng].append((waits, fn, eng, 1))
        self._update(eng, self.cnt[eng], reads, writes)

    def dma(self, q, out, in_, semkey, reads=(), writes=(), group=False, **kw):
        waits = self._waits(q, reads, writes)
        v = self.dmacnt.get(semkey, 0) + 16
        self.dmacnt[semkey] = v
        self.ops[q].append((waits, lambda e: e.dma_start(out=out, in_=in_, **kw), semkey, 16))
        self._update(semkey, v, reads, writes)
        if group:
            self.groupkeys.setdefault(semkey, set()).update(writes)

    def group_done(self, semkey):
        tot = self.dmacnt[semkey]
        for key in self.groupkeys.get(semkey, ()):
            st = self.state.get(key)
            if st and semkey in st[0]:
                st[0][semkey] = tot
        self.groupkeys[semkey] = set()

    def claim(self, region, keys):
        reg = self.regions.setdefault(region, set())
        merged = {}
        for k in reg:
            st = self.state.get(k)
            if st:
                for d in (st[0], st[1]):
                    for kk, v in d.items():
                        if merged.get(kk, 0) < v:
                            merged[kk] = v
        for k in keys:
            self.state[k] = [dict(merged), {}]
            reg.add(k)

    def barrier(self):
        tot = {k: v for k, v in self.cnt.items() if v > 0}
        tot.update(self.dmacnt)
        for e in ENGS:
            waits = []
            for k, v in tot.items():
                if k == e:
                    continue
                if self.known[e].get(k, 0) >= v:
                    continue
                self.known[e][k] = v
                waits.append((k, v))
            self.ops[e].append((waits, None, None, 0))

    def emit(self):
        nc = self.nc
        for k in list(ENGS) + list(self.dmacnt.keys()):
            if k not in self.sems:
                self.sems[k] = self.es.enter_context(nc.semaphore("s_" + str(k)))
        sems = self.sems
        with nc.Block() as block:
            for e in ENGS:
                oplist = self.ops[e]

                def body(engobj, oplist=oplist):
                    for waits, fn, inc, amt in oplist:
                        for k, v in waits:
                            engobj.wait_ge(sems[k], v)
                        if fn is None:
                            continue
                        ins = fn(engobj)
                        if inc is not None:
                            ins.then_inc(sems[inc], amt)
                getattr(block, BLOCKNAME[e])(body)
        self.ops = {e: [] for e in ENGS}


def AP(t, p0, npart, off, dims):
    rowlen = 1
    for s in t.shape[1:]:
        rowlen *= s
    return bass.AP(t, p0 * rowlen + off, [[rowlen, npart]] + [list(d) for d in dims])


def DR(t, off, *dims):
    return bass.AP(t, off, [list(d) for d in dims])


def build(debug_stage=0, stop=None):
    import os
    stop = stop or os.environ.get('KSTOP')
    nc = bass.Bass("TRN2", target_bir_lowering=False)
    es = ExitStack()
    P = Prog(nc, es)

    small_only = stop is not None and stop.startswith('p')
    BIGIN = ("x", "attn_w_qkv", "attn_w_o", "ssm_w_glu", "mlp_w_up", "mlp_w_down")

    def din(name, shape):
        if small_only and name in BIGIN:
            return None
        if stop == 'meta' and name in ("x", "ssm_w_glu"):
            return None
        return nc.dram_tensor(name, list(shape), F32, kind="ExternalInput")
    x_d = din("x", [2, SEQ, D])
    meta_d = din("meta_tokens", [16, D])
    anw_d = din("attn_norm_w", [1, D])
    wqkv_d = din("attn_w_qkv", [1, D, 1536])
    sink_d = din("attn_sinks", [1, 16])
    wo_d = din("attn_w_o", [1, D, D])
    snw_d = din("ssm_norm_w", [1, D])
    lre_d = din("ssm_lambda_re", [1, 64, 64])
    lim_d = din("ssm_lambda_im", [1, 64, 64])
    ldt_d = din("ssm_log_dt", [1, 64])
    bre_d = din("ssm_b_re", [1, 64, 64, 16])
    bim_d = din("ssm_b_im", [1, 64, 64, 16])
    cre_d = din("ssm_c_re", [1, 64, 16, 64])
    cim_d = din("ssm_c_im", [1, 64, 16, 64])
    sd_d = din("ssm_d", [1, D])
    wglu_d = din("ssm_w_glu", [1, D, 2048])
    mnw_d = din("mlp_norm_w", [2, D])
    wup_d = din("mlp_w_up", [2, D, 4096])
    wdn_d = din("mlp_w_down", [2, 4096, D])
    fnw_d = din("final_norm_w", [D])
    y_d = nc.dram_tensor("y", [2, SEQ, D], F32, kind="ExternalOutput")
    SA_d = nc.dram_tensor("scrA", [4, 128, 4096], BF16)
    SB_d = nc.dram_tensor("scrB", [4, 128, 4096], BF16)
    SC_d = nc.dram_tensor("scrC", [4, 128, 2048], BF16)

    def sb(name, shape, dt):
        return nc.alloc_sbuf_tensor(name, list(shape), dt)
    scwn = [0]

    def finish():
        P.barrier()
        P.emit()
        es.close()
        return nc

    def scw():
        scwn[0] += 1
        return 'scw%d' % scwn[0]

    def TT(eng, out, in0, in1, op, r, w):
        P.op(eng, lambda e: e.tensor_tensor(out=out, in0=in0, in1=in1, op=op), r, w)

    def TS(eng, out, in0, s1, op0, r, w, s2=None, op1=None):
        if op1 is None:
            P.op(eng, lambda e: e.tensor_scalar(out=out, in0=in0, scalar1=s1, scalar2=None, op0=op0), r, w)
        else:
            P.op(eng, lambda e: e.tensor_scalar(out=out, in0=in0, scalar1=s1, scalar2=s2, op0=op0, op1=op1), r, w)

    def STT(eng, out, in0, sc, in1, op0, op1, r, w):
        P.op(eng, lambda e: e.scalar_tensor_tensor(out=out, in0=in0, scalar=sc, in1=in1, op0=op0, op1=op1), r, w)

    def ACT(out, in_, func, r, w, **kw):
        P.op('act', lambda e: e.activation(out=out, in_=in_, func=func, **kw), r, w)

    def CP(eng, out, in_, r, w):
        if eng == 'act':
            P.op('act', lambda e: e.copy(out=out, in_=in_), r, w)
        else:
            P.op(eng, lambda e: e.tensor_copy(out=out, in_=in_), r, w)

    def MS(eng, out, val, w):
        P.op(eng, lambda e: e.memset(out, val), (), w)

    def MM(out, lhsT, rhs, st, sp, r, w):
        P.op('pe', lambda e: e.matmul(out, lhsT=lhsT, rhs=rhs, start=st, stop=sp), r, w)

    def TR(out, in_, ident, r, w):
        P.op('pe', lambda e: e.transpose(out=out, in_=in_, identity=ident), r, w)

    def RCP(out, in_, r, w):
        P.op('dve', lambda e: e.reciprocal(out=out, in_=in_), r, w)

    ident_f = sb("ident_f", [128, 128], F32)
    ident_b = sb("ident_b", [128, 128], BF16)
    wfm = sb("wfm", [128, 24], F32)
    expsink = sb("expsink", [128, 16], F32)
    epsc = sb("epsc", [128, 1], F32)
    SGN1 = sb("SGN1", [128, 1], F32)
    SGNI = sb("SGNI", [128, 1], F32)
    R8d = sb("R8d", [128, 64], F32)
    Xmeta = sb("Xmeta", [128, 64], F32)
    Xcar = sb("Xcar", [128, 64], F32)
    COSt = sb("COSt", [128, 64 * 129], BF16)
    SINt = sb("SINt", [128, 64 * 129], BF16)
    maskc = sb("maskc", [16, 128], BF16)
    kTm = sb("kTm", [64, 64], BF16)
    Vm = sb("Vm", [16, 260], BF16)
    kTp = sb("kTp", [64, 512], BF16)
    Vp = sb("Vp", [128, 260], BF16)
    ss_t = sb("ss_t", [128, 8], F32)
    rstd_t = sb("rstd_t", [128, 8], F32)
    den_t = sb("den_t", [128, 8], F32)
    rden_t = sb("rden_t", [128, 8], F32)
    vsw_t = sb("vsw_t", [128, 16], F32)
    ct_t = sb("ct_t", [128, 32], F32)
    junk = sb("junk", [128, 1024], BF16)
    BK = [nc.alloc_psum_tensor("B%d" % i, [128, 512], F32) for i in range(8)]
    BKb = [b.bitcast(BF16) for b in BK]

    MS('pool', AP(ident_f, 0, 128, 0, [[1, 128]]), 1.0, ['ident_f'])
    P.op('pool', lambda e: e.affine_select(out=AP(ident_f, 0, 128, 0, [[1, 128]]), in_=AP(ident_f, 0, 128, 0, [[1, 128]]),
                                           pattern=[[-1, 128]], compare_op=ALU.is_equal, fill=0.0, base=0, channel_multiplier=1),
         ['ident_f'], ['ident_f'])
    CP('dve', AP(ident_b, 0, 128, 0, [[1, 128]]), AP(ident_f, 0, 128, 0, [[1, 128]]), ['ident_f'], ['ident_b'])
    MS('pool', AP(epsc, 0, 128, 0, [[1, 1]]), 1e-6, ['epsc'])
    MS('pool', AP(SGN1, 0, 64, 0, [[1, 1]]), 1.0, ['SGN1'])
    MS('pool', AP(SGN1, 64, 64, 0, [[1, 1]]), -1.0, ['SGN1'])
    MS('pool', AP(SGNI, 0, 64, 0, [[1, 1]]), -1.0, ['SGNI'])
    MS('pool', AP(SGNI, 64, 64, 0, [[1, 1]]), 1.0, ['SGNI'])
    mk_f = sb("mk_f", [16, 128], F32)
    MS('pool', AP(mk_f, 0, 16, 0, [[1, 128]]), 1.0, ['mk_f'])
    P.op('pool', lambda e: e.affine_select(out=AP(mk_f, 0, 16, 0, [[1, 128]]), in_=AP(mk_f, 0, 16, 0, [[1, 128]]),
                                           pattern=[[1, 128]], compare_op=ALU.is_ge, fill=0.0, base=0, channel_multiplier=-1),
         ['mk_f'], ['mk_f'])
    CP('dve', AP(maskc, 0, 16, 0, [[1, 128]]), AP(mk_f, 0, 16, 0, [[1, 128]]), ['mk_f'], ['maskc'])
    for j, (dt_, off) in enumerate([(anw_d, 0), (mnw_d, 0), (mnw_d, D)]):
        P.dma('sp', AP(wfm, 0, 128, j * 8, [[1, 8]]), DR(dt_, off, [1, 128], [128, 8]), 'pl', (), ['wfm'], group=True,
              allow_slow_non_contiguous=True)
    P.dma('sp', AP(expsink, 0, 128, 0, [[1, 16]]), DR(sink_d, 0, [0, 128], [1, 16]), 'pl', (), ['expsink'], group=True)

    with ExitStack() as pes:
        def psb(name, shape, dt):
            return pes.enter_context(nc.sbuf_tensor(name, list(shape), dt))
        SCt = psb("SCt", [128, 40 * 64], F32)
        KI = psb("KI", [128, 64], I32)
        PWc = psb("PWc", [128, 64 * 9], F32)
        PWs = psb("PWs", [128, 64 * 9], F32)
        PRr = psb("PRr", [128, 64 * 8], F32)
        PIr = psb("PIr", [128, 64 * 8], F32)
        B_st = psb("B_st", [128, 1024], F32)
        B_sw = psb("B_sw", [128, 1024], F32)
        C_st = psb("C_st", [128, 1024], F32)
        C_sw = psb("C_sw", [128, 1024], F32)
        bb_st = psb("bb_st", [128, 1024], F32)
        bb_sw = psb("bb_sw", [128, 1024], F32)
        bb_sg = psb("bb_sg", [128, 1024], F32)
        WPt = psb("WPt", [128, 1024], F32)
        t1k = psb("t1k", [128, 1024], F32)
        t2k = psb("t2k", [128, 1024], F32)
        DTt = psb("DTt", [16, 1024], F32)
        WNb = psb("WNb", [16, 1024], F32)

        def sc(i):
            return AP(SCt, 0, 128, i * 64, [[1, 64]])
        (LR, LI, LDT, LRC, DTv, X1, ER, TH, TMP, KF, YY, SN, CS, ARr, AIi, NR, DEN, RDEN, FR, FI, T20, T21,
         C8, S8, TH2) = range(25)
        K_ = 'sc'
        for h in range(2):
            P.dma('sp', AP(SCt, 64 * h, 64, LR * 64, [[1, 64]]), DR(lre_d, 0, [1, 64], [64, 64]), 'pl', (), [K_], group=True,
                  allow_slow_non_contiguous=True)
            P.dma('sp', AP(SCt, 64 * h, 64, LI * 64, [[1, 64]]), DR(lim_d, 0, [1, 64], [64, 64]), 'pl', (), [K_], group=True,
                  allow_slow_non_contiguous=True)
        P.dma('sp', sc(LDT), DR(ldt_d, 0, [0, 128], [1, 64]), 'pl', (), [K_], group=True)
        for (dst, top, bot) in ((B_st, bre_d, bim_d), (B_sw, bim_d, bre_d)):
            P.dma('sp', AP(dst, 0, 64, 0, [[16, 64], [1, 16]]), DR(top, 0, [16, 64], [1024, 64], [1, 16]), 'pl', (), [dst.name], group=True)
            P.dma('sp', AP(dst, 64, 64, 0, [[16, 64], [1, 16]]), DR(bot, 0, [16, 64], [1024, 64], [1, 16]), 'pl', (), [dst.name], group=True)
        P.dma('sp', AP(WPt, 0, 128, 0, [[1, 1024]]), DR(snw_d, 0, [0, 128], [1, 1024]), 'pl', (), ['WPt'], group=True)
        P.dma('sp', AP(DTt, 0, 16, 0, [[1, 1024]]), DR(sd_d, 0, [0, 16], [1, 1024]), 'pl', (), ['DTt'], group=True)
        P.dma('sp', AP(WNb, 0, 16, 0, [[1, 1024]]), DR(snw_d, 0, [0, 16], [1, 1024]), 'pl', (), ['WNb'], group=True)
        with ExitStack() as ces:
            CnA = ces.enter_context(nc.sbuf_tensor("CnA", [128, 1024], F32))
            CnB = ces.enter_context(nc.sbuf_tensor("CnB", [128, 1024], F32))
            for (dst, first, second) in ((CnA, cre_d, cim_d), (CnB, cim_d, cre_d)):
                P.dma('sp', AP(dst, 0, 128, 0, [[128, 8], [1, 64]]), DR(first, 0, [64, 128], [8192, 8], [1, 64]), 'pl', (), [dst.name], group=True)
                P.dma('sp', AP(dst, 0, 128, 64, [[128, 8], [1, 64]]), DR(second, 0, [64, 128], [8192, 8], [1, 64]), 'pl', (), [dst.name], group=True)
            P.group_done('pl')
            ACT(AP(expsink, 0, 128, 0, [[1, 16]]), AP(expsink, 0, 128, 0, [[1, 16]]), AF.Exp, ['expsink'], ['expsink'])
            for (src, dst) in ((CnA, C_st), (CnB, C_sw)):
                for j in range(8):
                    bk = BK[j % 2]
                    TR(AP(bk, 0, 128, 0, [[1, 128]]), AP(src, 0, 128, j * 128, [[1, 128]]), AP(ident_f, 0, 128, 0, [[1, 128]]),
                       [src.name, 'ident_f'], ['B%d' % (j % 2)])
                    CP('dve' if j % 2 == 0 else 'act', AP(dst, 0, 128, j * 128, [[1, 128]]), AP(bk, 0, 128, 0, [[1, 128]]),
                       ['B%d' % (j % 2)], [dst.name])
            P.emit()
            if stop == 'p0':
                return finish()

        def sTT(o, a, b, op):
            TT('dve', sc(o), sc(a), sc(b), op, [K_], [K_])

        def sTS(o, a, s1, op0, s2=None, op1=None):
            TS('dve', sc(o), sc(a), s1, op0, [K_], [K_], s2, op1)

        def sinof(o, ang):
            TS('dve', AP(KI, 0, 128, 0, [[1, 64]]), sc(ang), float(1.0 / (2 * PI)), ALU.mult, [K_], ['KI'])
            CP('dve', sc(KF), AP(KI, 0, 128, 0, [[1, 64]]), ['KI'], [K_])
            STT('dve', sc(YY), sc(KF), float(-2 * PI), sc(ang), ALU.mult, ALU.add, [K_], [K_])
            sTS(YY, YY, 3.14159, ALU.min, -3.14159, ALU.max)
            ACT(sc(o), sc(YY), AF.Sin, [K_], [K_])
        sTS(LRC, LR, -1e-4, ALU.min)
        ACT(sc(DTv), sc(LDT), AF.Exp, [K_], [K_])
        sTT(X1, LRC, DTv, ALU.mult)
        ACT(sc(ER), sc(X1), AF.Exp, [K_], [K_])
        ACT(AP(R8d, 0, 128, 0, [[1, 64]]), sc(X1), AF.Exp, [K_], ['R8d'], scale=8.0)
        sTT(TH, LI, DTv, ALU.mult)
        sinof(SN, TH)
        sTS(TH2, TH, float(PI / 2), ALU.add)
        sinof(CS, TH2)
        sTT(ARr, ER, CS, ALU.mult)
        sTT(AIi, ER, SN, ALU.mult)
        sTS(NR, ARr, -1.0, ALU.add)
        sTT(T20, LRC, LRC, ALU.mult)
        sTT(T21, LI, LI, ALU.mult)
        sTT(DEN, T20, T21, ALU.add)
        RCP(sc(RDEN), sc(DEN), [K_], [K_])
        sTT(T20, NR, LRC, ALU.mult)
        sTT(T21, AIi, LI, ALU.mult)
        sTT(T20, T20, T21, ALU.add)
        sTT(FR, T20, RDEN, ALU.mult)
        sTT(T20, AIi, LRC, ALU.mult)
        sTT(T21, NR, LI, ALU.mult)
        sTT(T20, T20, T21, ALU.subtract)
        sTT(FI, T20, RDEN, ALU.mult)

        def pw(t, k, stride=9):
            return AP(t, 0, 128, k, [[stride, 64]])
        MS('dve', pw(PWc, 0), 1.0, ['PW'])
        MS('dve', pw(PWs, 0), 0.0, ['PW'])
        for k in range(8):
            CP('dve', pw(PRr, 7 - k, 8), pw(PWc, k), ['PW'], ['PRr'])
            CP('dve', pw(PIr, 7 - k, 8), pw(PWs, k), ['PW'], ['PRr'])
            TT('dve', sc(T20), pw(PWc, k), sc(ARr), ALU.mult, ['PW', K_], [K_])
            TT('dve', sc(T21), pw(PWs, k), sc(AIi), ALU.mult, ['PW', K_], [K_])
            TT('dve', pw(PWc, k + 1), sc(T20), sc(T21), ALU.subtract, [K_], ['PW'])
            TT('dve', sc(T20), pw(PWc, k), sc(AIi), ALU.mult, ['PW', K_], [K_])
            TT('dve', sc(T21), pw(PWs, k), sc(ARr), ALU.mult, ['PW', K_], [K_])
            TT('dve', pw(PWs, k + 1), sc(T20), sc(T21), ALU.add, [K_], ['PW'])
        CP('dve', sc(C8), sc(CS), [K_], [K_])
        CP('dve', sc(S8), sc(SN), [K_], [K_])
        for _ in range(3):
            sTT(T20, C8, C8, ALU.mult)
            sTT(T21, S8, S8, ALU.mult)
            sTT(TMP, C8, S8, ALU.mult)
            sTT(C8, T20, T21, ALU.subtract)
            sTS(S8, TMP, 2.0, ALU.mult)
        if stop == 'p1':
            return finish()
        with ExitStack() as tes:
            TC = tes.enter_context(nc.sbuf_tensor("TC", [128, 64 * 129], F32))
            TSn = tes.enter_context(nc.sbuf_tensor("TSn", [128, 64 * 129], F32))
            T1 = tes.enter_context(nc.sbuf_tensor("T1", [128, 4096], F32))
            T2 = tes.enter_context(nc.sbuf_tensor("T2", [128, 4096], F32))

            def tb(t, m0, L, bc=False):
                return AP(t, 0, 128, m0, [[129, 64], [0 if bc else 1, L]])
            MS('dve', tb(TC, 0, 1), 1.0, ['TC'])
            MS('dve', tb(TSn, 0, 1), 0.0, ['TS'])
            CP('dve', tb(TC, 1, 1), AP(SCt, 0, 128, C8 * 64, [[1, 64], [1, 1]]), [K_], ['TC'])
            CP('dve', tb(TSn, 1, 1), AP(SCt, 0, 128, S8 * 64, [[1, 64], [1, 1]]), [K_], ['TS'])
            L = 1
            while L <= 64:
                o1 = AP(T1, 0, 128, 0, [[L, 64], [1, L]])
                o2 = AP(T2, 0, 128, 0, [[L, 64], [1, L]])
                TT('dve', o1, tb(TC, 1, L), tb(TC, L, L, True), ALU.mult, ['TC'], ['T1'])
                TT('dve', o2, tb(TSn, 1, L), tb(TSn, L, L, True), ALU.mult, ['TS'], ['T2'])
                TT('dve', tb(TC, L + 1, L), o1, o2, ALU.subtract, ['T1', 'T2'], ['TC'])
                TT('dve', o1, tb(TC, 1, L), tb(TSn, L, L, True), ALU.mult, ['TC', 'TS'], ['T1'])
                TT('dve', o2, tb(TSn, 1, L), tb(TC, L, L, True), ALU.mult, ['TC', 'TS'], ['T2'])
                TT('dve', tb(TSn, L + 1, L), o1, o2, ALU.add, ['T1', 'T2'], ['TS'])
                L *= 2
            CP('dve', AP(COSt, 0, 128, 0, [[1, 64 * 129]]), AP(TC, 0, 128, 0, [[1, 64 * 129]]), ['TC'], ['COSt'])
            CP('act', AP(SINt, 0, 128, 0, [[1, 64 * 129]]), AP(TSn, 0, 128, 0, [[1, 64 * 129]]), ['TS'], ['SINt'])
            P.emit()
        if stop == 'p2':
            return finish()
        g16 = [[16, 64], [1, 16]]
        g16b = [[1, 64], [0, 16]]
        TT('dve', AP(t1k, 0, 128, 0, g16), AP(B_st, 0, 128, 0, g16), AP(SCt, 0, 128, FR * 64, g16b), ALU.mult, ['B_st', K_], ['t1k'])
        TT('dve', AP(t2k, 0, 128, 0, g16), AP(B_sw, 0, 128, 0, g16), AP(SCt, 0, 128, FI * 64, g16b), ALU.mult, ['B_sw', K_], ['t2k'])
        STT('dve', AP(bb_st, 0, 128, 0, [[1, 1024]]), AP(t2k, 0, 128, 0, [[1, 1024]]), AP(SGNI, 0, 128, 0, [[1, 1]]),
            AP(t1k, 0, 128, 0, [[1, 1024]]), ALU.mult, ALU.add, ['t1k', 't2k', 'SGNI'], ['bb_st'])
        TT('dve', AP(t1k, 0, 128, 0, g16), AP(B_sw, 0, 128, 0, g16), AP(SCt, 0, 128, FR * 64, g16b), ALU.mult, ['B_sw', K_], ['t1k'])
        TT('dve', AP(t2k, 0, 128, 0, g16), AP(B_st, 0, 128, 0, g16), AP(SCt, 0, 128, FI * 64, g16b), ALU.mult, ['B_st', K_], ['t2k'])
        STT('dve', AP(bb_sw, 0, 128, 0, [[1, 1024]]), AP(t2k, 0, 128, 0, [[1, 1024]]), AP(SGN1, 0, 128, 0, [[1, 1]]),
            AP(t1k, 0, 128, 0, [[1, 1024]]), ALU.mult, ALU.add, ['t1k', 't2k', 'SGN1'], ['bb_sw'])
        full = [[1, 1024]]
        TT('dve', AP(bb_st, 0, 128, 0, full), AP(bb_st, 0, 128, 0, full), AP(WPt, 0, 128, 0, full), ALU.mult, ['bb_st', 'WPt'], ['bb_st'])
        TT('dve', AP(bb_sw, 0, 128, 0, full), AP(bb_sw, 0, 128, 0, full), AP(WPt, 0, 128, 0, full), ALU.mult, ['bb_sw', 'WPt'], ['bb_sw'])
        TS('dve', AP(bb_sg, 0, 128, 0, full), AP(bb_st, 0, 128, 0, full), AP(SGN1, 0, 128, 0, [[1, 1]]), ALU.mult, ['bb_st', 'SGN1'], ['bb_sg'])
        TT('dve', AP(DTt, 0, 16, 0, full), AP(DTt, 0, 16, 0, full), AP(WNb, 0, 16, 0, full), ALU.mult, ['DTt', 'WNb'], ['DTt'])
        TT('dve', AP(DTt, 0, 16, 0, g16), AP(DTt, 0, 16, 0, g16), AP(ident_f, 0, 16, 0, [[0, 64], [1, 16]]), ALU.mult, ['DTt', 'ident_f'], ['DTt'])
        P.emit()
        if stop == 'p3':
            return finish()
        with ExitStack() as bes:
            G1 = bes.enter_context(nc.sbuf_tensor("G1", [128, 4608], F32))
            G2 = bes.enter_context(nc.sbuf_tensor("G2", [128, 4608], F32))
            G3 = bes.enter_context(nc.sbuf_tensor("G3", [128, 4608], F32))
            OB1 = bes.enter_context(nc.sbuf_tensor("OB1", [128, 4096], BF16))
            OB2 = bes.enter_context(nc.sbuf_tensor("OB2", [128, 4096], BF16))
            M_bf = bes.enter_context(nc.sbuf_tensor("M_bf", [16, 8192], BF16))
            G1b = G1.bitcast(BF16)
            for gh in range(2):
                d4 = [[128, 32], [16, 8], [1, 16]]
                bbv = [[16, 32], [0, 8], [1, 16]]
                prv = [[8, 32], [1, 8], [0, 16]]
                TT('dve', AP(G1, 0, 128, 0, d4), AP(bb_st, 0, 128, gh * 512, bbv), AP(PRr, 0, 128, gh * 256, prv), ALU.mult, ['bb_st', 'PRr'], ['G1'])
                TT('dve', AP(G2, 0, 128, 0, d4), AP(bb_sw, 0, 128, gh * 512, bbv), AP(PIr, 0, 128, gh * 256, prv), ALU.mult, ['bb_sw', 'PRr'], ['G2'])
                STT('dve', AP(G1, 0, 128, 0, [[1, 4096]]), AP(G2, 0, 128, 0, [[1, 4096]]), AP(SGNI, 0, 128, 0, [[1, 1]]),
                    AP(G1, 0, 128, 0, [[1, 4096]]), ALU.mult, ALU.add, ['G1', 'G2', 'SGNI'], ['G1'])
                if stop == 'p3a1':
                    return finish()
                for gl in range(32):
                    bi = (gl // 4) % 2
                    bk = BK[bi]
                    TR(AP(bk, 0, 128, (gl % 4) * 128, [[1, 128]]), AP(G1, 0, 128, gl * 128, [[1, 128]]), AP(ident_f, 0, 128, 0, [[1, 128]]),
                       ['G1', 'ident_f'], ['B%d' % bi])
                    KV = os.environ.get('KVAR', 'abc')
                    if gl % 4 == 3:
                        g0 = gl - 3
                        if 'a' in KV:
                            CP('dve', AP(OB1, 0, 128, g0 * 128, [[1, 512]]), AP(bk, 0, 128, 0, [[1, 512]]), ['B%d' % bi], ['OB1'])
                        if 'b' in KV:
                            CP('dve', AP(OB2, 0, 128, g0 * 128, [[128, 4], [1, 64]]), AP(bk, 0, 128, 64, [[128, 4], [1, 64]]), ['B%d' % bi], ['OB2'])
                        if 'c' in KV:
                            TS('dve', AP(OB2, 0, 128, g0 * 128 + 64, [[128, 4], [1, 64]]), AP(bk, 0, 128, 0, [[128, 4], [1, 64]]), -1.0, ALU.mult,
                               ['B%d' % bi], ['OB2'])
                if stop == 'p3a2':
                    return finish()
                for j in range(2):
                    q = 2 * gh + j
                    P.dma('sp', DR(SA_d, q * 128 * 4096, [4096, 128], [1, 2048]), AP(OB1, 0, 128, j * 2048, [[1, 2048]]), scw(), ['OB1'], ['SA'])
                    P.dma('sp', DR(SA_d, q * 128 * 4096 + 2048, [4096, 128], [1, 2048]), AP(OB2, 0, 128, j * 2048, [[1, 2048]]), scw(), ['OB2'], ['SA'])
                if stop == 'p3a':
                    return finish()
                c4 = [[144, 32], [16, 9], [1, 16]]
                cv = [[16, 32], [0, 9], [1, 16]]
                pv = [[9, 32], [1, 9], [0, 16]]
                TT('dve', AP(G1, 0, 128, 0, c4), AP(C_st, 0, 128, gh * 512, cv), AP(PWc, 0, 128, gh * 288, pv), ALU.mult, ['C_st', 'PW'], ['G1'])
                TT('dve', AP(G2, 0, 128, 0, c4), AP(C_sw, 0, 128, gh * 512, cv), AP(PWs, 0, 128, gh * 288, pv), ALU.mult, ['C_sw', 'PW'], ['G2'])
                STT('dve', AP(G1, 0, 128, 0, [[1, 4608]]), AP(G2, 0, 128, 0, [[1, 4608]]), AP(SGNI, 0, 128, 0, [[1, 1]]),
                    AP(G1, 0, 128, 0, [[1, 4608]]), ALU.mult, ALU.add, ['G1', 'G2', 'SGNI'], ['G1'])
                TT('dve', AP(G3, 0, 128, 0, c4), AP(C_sw, 0, 128, gh * 512, cv), AP(PWc, 0, 128, gh * 288, pv), ALU.mult, ['C_sw', 'PW'], ['G3'])
                TT('dve', AP(G2, 0, 128, 0, c4), AP(C_st, 0, 128, gh * 512, cv), AP(PWs, 0, 128, gh * 288, pv), ALU.mult, ['C_st', 'PW'], ['G2'])
                STT('dve', AP(G3, 0, 128, 0, [[1, 4608]]), AP(G2, 0, 128, 0, [[1, 4608]]), AP(SGN1, 0, 128, 0, [[1, 1]]),
                    AP(G3, 0, 128, 0, [[1, 4608]]), ALU.mult, ALU.add, ['G3', 'G2', 'SGN1'], ['G3'])
                TS('dve', AP(OB1, 0, 128, 0, [[128, 32], [1, 128]]), AP(G1, 0, 128, 16, [[144, 32], [1, 128]]), AP(SGN1, 0, 128, 0, [[1, 1]]), ALU.mult,
                   ['G1', 'SGN1'], ['OB1'])
                TS('dve', AP(OB2, 0, 128, 0, [[128, 32], [1, 128]]), AP(G3, 0, 128, 16, [[144, 32], [1, 128]]), -1.0, ALU.mult, ['G3'], ['OB2'])
                for j in range(2):
                    q = 2 * gh + j
                    P.dma('sp', DR(SB_d, q * 128 * 4096 + 2048, [4096, 128], [1, 2048]), AP(OB1, 0, 128, j * 2048, [[1, 2048]]), scw(), ['OB1'], ['SB'])
                    P.dma('sp', DR(SC_d, q * 128 * 2048, [2048, 128], [1, 2048]), AP(OB2, 0, 128, j * 2048, [[1, 2048]]), scw(), ['OB2'], ['SC'])
                if stop == 'p3b':
                    return finish()
                for gl in range(32):
                    g = gh * 32 + gl
                    bi = 2 + (gl // 4) % 2
                    bk = BK[bi]
                    MM(AP(bk, 0, 16, (gl % 4) * 128, [[1, 128]]), AP(bb_sg, 0, 128, g * 16, [[1, 16]]), AP(G1, 0, 128, gl * 144, [[1, 128]]), True, True,
                       ['bb_sg', 'G1'], ['B%d' % bi])
                    if gl % 4 == 3:
                        g0 = gl - 3
                        CP('act', AP(G2, 0, 16, g0 * 128, [[1, 512]]), AP(bk, 0, 16, 0, [[1, 512]]), ['B%d' % bi], ['G2'])
                TT('dve', AP(G2, 0, 16, 0, [[128, 32], [1, 16]]), AP(G2, 0, 16, 0, [[128, 32], [1, 16]]), AP(DTt, 0, 16, gh * 512, [[16, 32], [1, 16]]), ALU.add,
                   ['G2', 'DTt'], ['G2'])
                CP('dve', AP(M_bf, 0, 16, gh * 4096, [[1, 4096]]), AP(G2, 0, 16, 0, [[1, 4096]]), ['G2'], ['M_bf'])
            if stop == 'p3c':
                return finish()
            MS('dve', AP(G1b, 0, 128, 0, [[1, 8192]]), 0.0, ['G1'])
            for s in range(8):
                n = (8 - s) * 16
                P.dma('sp', AP(G1b, 16 * s, 16, s * 16, [[128, 64], [1, n]]), AP(M_bf, 0, 16, 0, [[128, 64], [1, n]]), 'kpl', ['M_bf'], ['G1'])
            for q in range(4):
                P.dma('sp', DR(SB_d, q * 128 * 4096, [4096, 128], [1, 2048]), AP(G1b, 0, 128, q * 2048, [[1, 2048]]), scw(), ['G1'], ['SB'])
            P.barrier()
            P.emit()
    if stop == 'p4':
        return finish()
    Ecur = sb("Ecur", [128, 2048], BF16)
    Eprev = sb("Eprev", [128, 2048], BF16)
    wfin = sb("wfin", [128, 1024], F32)
    P.dma('sp', AP(wfin, 0, 128, 0, [[1, 1024]]), DR(fnw_d, 0, [0, 128], [1, 1024]), 'pl', (), ['wfin'])
    with ExitStack() as ees:
        di = ees.enter_context(nc.sbuf_tensor("di", [128, 128], I32))
        df = ees.enter_context(nc.sbuf_tensor("df", [128, 128], F32))
        dc = ees.enter_context(nc.sbuf_tensor("dc", [128, 128], F32))
        dp = ees.enter_context(nc.sbuf_tensor("dp", [128, 128], F32))
        Ef = ees.enter_context(nc.sbuf_tensor("Ef", [128, 2048], F32))
        P.op('pool', lambda e: e.iota(AP(di, 0, 128, 0, [[1, 128]]), [[1, 128]], base=0, channel_multiplier=-1), (), ['di'])
        CP('dve', AP(df, 0, 128, 0, [[1, 128]]), AP(di, 0, 128, 0, [[1, 128]]), ['di'], ['df'])
        TS('dve', AP(dc, 0, 128, 0, [[1, 128]]), AP(df, 0, 128, 0, [[1, 128]]), 0.0, ALU.max, ['df'], ['dc'])
        TS('dve', AP(dp, 0, 128, 0, [[1, 128]]), AP(df, 0, 128, 0, [[1, 128]]), 128.0, ALU.add, ['df'], ['dp'])
        TS('dve', AP(dp, 0, 128, 0, [[1, 128]]), AP(dp, 0, 128, 0, [[1, 128]]), 0.0, ALU.max, ['dp'], ['dp'])
        for (src, dstE, pat, base, cm) in ((dc, Ecur, [[0, 16], [1, 128]], 0, -1), (dp, Eprev, [[0, 16], [-1, 128]], -1, 1)):
            for h in range(16):
                slope = float(2.0 ** (-(h + 1) / 2.0))
                ACT(AP(Ef, 0, 128, h * 128, [[1, 128]]), AP(src, 0, 128, 0, [[1, 128]]), AF.Exp, [src.name], ['Ef'], scale=-slope)
            P.op('pool', lambda e, pat=pat, base=base, cm=cm: e.affine_select(
                out=AP(Ef, 0, 128, 0, [[128, 16], [1, 128]]), in_=AP(Ef, 0, 128, 0, [[128, 16], [1, 128]]),
                pattern=pat, compare_op=ALU.is_ge, fill=0.0, base=base, channel_multiplier=cm), ['Ef'], ['Ef'])
            CP('dve', AP(dstE, 0, 128, 0, [[1, 2048]]), AP(Ef, 0, 128, 0, [[1, 2048]]), ['Ef'], [dstE.name])
        P.barrier()
        P.emit()

    if stop == 'p5':
        return finish()
    H8 = sb("H8", [128, 8192], F32)
    Hn8 = sb("Hn8", [128, 8192], BF16)
    hnT = sb("hnT", [128, 8192], BF16)
    RING = sb("RING", [128, NSLOT * SLOTE], BF16)
    RA = sb("RA", [128, 8192], F32)
    RAb = RA.bitcast(BF16)
    TP = sb("TP", [128, 2048], F32)
    PTB = sb("PTB", [128, 3072], BF16)
    Otok = sb("Otok", [128, 1024], BF16)
    QT0, KT0, VA0 = 0, 8192, 12800

    specs = []

    def mlp_specs(l):
        U = lambda qd: [('up', l, qd, 0), ('up', l, qd, 1)]
        Dn = lambda qd: [('dn', l, qd, 0), ('dn', l, qd, 1)]
        return U(0) + U(1) + Dn(0) + U(2) + Dn(1) + U(3) + Dn(2) + Dn(3)

    def tile_specs(meta):
        s = [('wkv',), ('wq', 0), ('wq', 1)]
        if not meta:
            s += [('wq', 0), ('wq', 1)]
        s += [('wo', 0), ('wo', 1)] + mlp_specs(0)
        if meta:
            s += [('A', q) for q in range(4)]
        elif debug_stage != 1:
            s += [('A', 0), ('A', 1), ('B', 0), ('C', 0), ('A', 2), ('B', 1), ('C', 1), ('A', 3), ('B', 2), ('C', 2), ('B', 3), ('C', 3)]
            s += [('glu', 0, 0), ('glu', 1, 0), ('glu', 0, 1), ('glu', 1, 1)] + mlp_specs(1)
        return s
    specs = tile_specs(True)
    ntiles = 8 if debug_stage != 9 else 1
    for _ in range(ntiles):
        specs += tile_specs(False)
    ring = {'next': 0, 'issued': 0, 'done': [False] * len(specs)}

    def issue(i):
        sp = specs[i]
        sl = i % NSLOT
        key = 'R%d' % sl
        base = sl * SLOTE
        kind = sp[0]
        if kind == 'wq':
            src = DR(wqkv_d, sp[1] * 512, [1536, 128], [128 * 1536, 8], [1, 512]); dims = [[512, 8], [1, 512]]; q = 'pool'
        elif kind == 'wkv':
            src = DR(wqkv_d, 1024, [1536, 128], [128 * 1536, 8], [1, 512]); dims = [[512, 8], [1, 512]]; q = 'pool'
        elif kind == 'wo':
            src = DR(wo_d, sp[1] * 512, [1024, 128], [128 * 1024, 8], [1, 512]); dims = [[512, 8], [1, 512]]; q = 'pool'
        elif kind == 'up':
            _, l, qd, j = sp
            src = DR(wup_d, l * D * 4096 + qd * 1024 + j * 512, [4096, 128], [128 * 4096, 8], [1, 512]); dims = [[512, 8], [1, 512]]; q = 'pool'
        elif kind == 'dn':
            _, l, qd, j = sp
            src = DR(wdn_d, l * 4096 * D + (qd * 1024 + j * 512) * D, [1024, 128], [128 * 1024, 4], [1, 1024]); dims = [[1024, 4], [1, 1024]]; q = 'pool'
        elif kind == 'glu':
            _, part, j = sp
            src = DR(wglu_d, part * 1024 + j * 512, [2048, 128], [128 * 2048, 8], [1, 512]); dims = [[512, 8], [1, 512]]; q = 'pool'
        elif kind == 'A':
            src = DR(SA_d, sp[1] * 128 * 4096, [4096, 128], [1, 4096]); dims = [[1, 4096]]; q = 'sp'
        elif kind == 'B':
            src = DR(SB_d, sp[1] * 128 * 4096, [4096, 128], [1, 4096]); dims = [[1, 4096]]; q = 'sp'
        elif kind == 'C':
            src = DR(SC_d, sp[1] * 128 * 2048, [2048, 128], [1, 2048]); dims = [[1, 2048]]; q = 'sp'
        P.dma(q, AP(RING, 0, 128, base, dims), src, key, (), [key])

    def pump():
        while ring['issued'] < len(specs) and ring['issued'] < ring['next'] + NSLOT:
            i = ring['issued']
            if i >= NSLOT and not ring['done'][i - NSLOT]:
                break
            issue(i)
            ring['issued'] += 1

    def rget(*sp):
        i = ring['next']
        assert specs[i] == tuple(sp), (i, specs[i], sp)
        ring['next'] += 1
        pump()
        assert ring['issued'] > i
        return i

    def rdone(i):
        ring['done'][i] = True
        pump()

    def rk(i):
        return 'R%d' % (i % NSLOT)

    def rb(i):
        return (i % NSLOT) * SLOTE

    ctr = {'t': 0, 'pt': 0, 'ev': 0, 'bk': 0}

    def tmp():
        i = ctr['t'] % 4
        ctr['t'] += 1
        return i

    def evq():
        ctr['ev'] += 1
        return 'act' if ctr['ev'] % 2 == 0 else 'dve'

    HK = ['H8.%d' % s for s in range(8)]
    HNK = ['Hn8.%d' % s for s in range(8)]
    TK = ['hnT.%d' % k for k in range(8)]

    def rmsnorm(nt, order):
        P.claim('Hn8', HNK)
        MS('dve', AP(ss_t, 0, nt, 0, [[1, 8]]), 0.0, ['ss'])
        for s in range(8):
            ACT(AP(junk, 0, nt, 0, [[1, 1024]]), AP(H8, 0, nt, s * 1024, [[1, 1024]]), AF.Square, [HK[s], 'ss'], ['ss'],
                accum_out=AP(ss_t, 0, nt, s, [[1, 1]]))
        ACT(AP(rstd_t, 0, nt, 0, [[1, 8]]), AP(ss_t, 0, nt, 0, [[1, 8]]), AF.Sqrt, ['ss', 'epsc'], ['rstd'],
            bias=AP(epsc, 0, nt, 0, [[1, 1]]), scale=float(1.0 / D))
        RCP(AP(rstd_t, 0, nt, 0, [[1, 8]]), AP(rstd_t, 0, nt, 0, [[1, 8]]), ['rstd'], ['rstd'])
        for s in range(8):
            if order == 'std':
                o = AP(Hn8, 0, nt, s * 1024, [[1, 1024]])
                i = AP(H8, 0, nt, s * 1024, [[1, 1024]])
            else:
                o = AP(Hn8, 0, nt, s * 16, [[128, 64], [1, 16]])
                i = AP(H8, 0, nt, s * 1024, [[16, 64], [1, 16]])
            TS('dve', o, i, AP(rstd_t, 0, nt, s, [[1, 1]]), ALU.mult, [HK[s], 'rstd'], [HNK[s]])

    def to_featmajor(nt, wj, srckeys):
        P.claim('hnT', TK)
        for kc in range(8):
            bi = kc % 2
            for s in range(8):
                TR(AP(BKb[bi], 0, 128, s * 128, [[1, nt]]), AP(Hn8, 0, nt, s * 1024 + kc * 128, [[1, 128]]), AP(ident_b, 0, nt, 0, [[1, nt]]),
                   srckeys + ['ident_b'], ['B%d' % bi])
            o = AP(hnT, 0, 128, kc * 1024, [[1, 8], [8, nt]])
            i = AP(BKb[bi], 0, 128, 0, [[128, 8], [1, nt]])
            if wj is None:
                CP(evq(), o, i, ['B%d' % bi], [TK[kc]])
            else:
                e = evq()
                wc = AP(wfm, 0, 128, wj * 8 + kc, [[1, 1]])
                if e == 'dve':
                    TS('dve', o, i, wc, ALU.mult, ['B%d' % bi, 'wfm'], [TK[kc]])
                else:
                    ACT(o, i, AF.Copy, ['B%d' % bi, 'wfm'], [TK[kc]], scale=wc)

    def resid_add(nt, s, half, bi):
        hs = AP(H8, 0, nt, s * 1024 + half * 512, [[1, 512]])
        TT('dve', hs, AP(BK[bi], 0, nt, 0, [[1, 512]]), hs, ALU.add, ['B%d' % bi, HK[s]], [HK[s]])

    def attention(nt, meta, first):
        ntok = nt * 8
        nblk = max(1, ntok // 128)
        QB = min(128, ntok)
        nth = max(1, ntok // 512)
        tn = min(512, ntok)
        P.claim('RA', ['qT', 'kT', 'Vatt'])
        P.claim('Hn8', ['OT'])
        MS('dve', AP(RAb, 0, 128, VA0, [[1, 2340]]), 1.0, ['Vatt'])
        if not meta and not first:
            CP('dve', AP(RAb, 0, 128, VA0, [[1, 260]]), AP(Vp, 0, 128, 0, [[1, 260]]), ['Vp'], ['Vatt'])
            CP('dve', AP(RAb, 0, 64, KT0, [[1152, 4], [1, 128]]), AP(kTp, 0, 64, 0, [[128, 4], [1, 128]]), ['kTp'], ['kT'])
        iw = rget('wkv')
        for kv in range(4):
            for th in range(nth):
                bi = 2 + (ctr['bk'] % 2); ctr['bk'] += 1
                for kc in range(8):
                    MM(AP(BK[bi], 0, 64, 0, [[1, tn]]), AP(RING, 0, 128, rb(iw) + kc * 512 + kv * 64, [[1, 64]]),
                       AP(hnT, 0, 128, kc * 1024 + th * 512, [[1, tn]]), kc == 0, kc == 7, [rk(iw), TK[kc]], ['B%d' % bi])
                CP(evq(), AP(RAb, 0, 64, KT0 + kv * 1152 + 128 + th * 512, [[1, tn]]), AP(BK[bi], 0, 64, 0, [[1, tn]]), ['B%d' % bi], ['kT'])
        for blk in range(nblk):
            bi = 2 + (ctr['bk'] % 2); ctr['bk'] += 1
            for kc in range(8):
                MM(AP(BK[bi], 0, QB, 0, [[1, 256]]), AP(hnT, 0, 128, kc * 1024 + blk * 128, [[1, QB]]),
                   AP(RING, 0, 128, rb(iw) + kc * 512 + 256, [[1, 256]]), kc == 0, kc == 7, [rk(iw), TK[kc]], ['B%d' % bi])
            CP(evq(), AP(RAb, 0, QB, VA0 + (blk + 1) * 260, [[65, 4], [1, 64]]), AP(BK[bi], 0, QB, 0, [[64, 4], [1, 64]]), ['B%d' % bi], ['Vatt'])
        rdone(iw)
        if meta:
            CP('dve', AP(kTm, 0, 64, 0, [[16, 4], [1, 16]]), AP(RAb, 0, 64, KT0 + 128, [[1152, 4], [1, 16]]), ['kT'], ['kTm'])
            CP('dve', AP(Vm, 0, 16, 0, [[1, 260]]), AP(RAb, 0, 16, VA0 + 260, [[1, 260]]), ['Vatt'], ['Vm'])
            MS('dve', AP(RAb, 0, 64, QT0, [[1, 8192]]), 0.0, ['qT'])
        nhalf = 1 if meta else 2
        for hf in range(nhalf):
            bl0 = hf * 4
            nb_h = min(4, nblk)
            iq = [rget('wq', 0), rget('wq', 1)]
            tnq = min(512, ntok)
            for h in range(16):
                bi = 2 + (ctr['bk'] % 2); ctr['bk'] += 1
                ip = iq[h // 8]
                for kc in range(8):
                    MM(AP(BK[bi], 0, 64, 0, [[1, tnq]]), AP(RING, 0, 128, rb(ip) + kc * 512 + (h % 8) * 64, [[1, 64]]),
                       AP(hnT, 0, 128, kc * 1024 + hf * 512, [[1, tnq]]), kc == 0, kc == 7, [rk(ip), TK[kc]], ['B%d' % bi])
                if meta:
                    CP(evq(), AP(RAb, 0, 64, QT0 + h * 128, [[1, tnq]]), AP(BK[bi], 0, 64, 0, [[1, tnq]]), ['B%d' % bi], ['qT'])
                else:
                    CP(evq(), AP(RAb, 0, 64, QT0 + h * 128, [[2048, 4], [1, 128]]), AP(BK[bi], 0, 64, 0, [[128, 4], [1, 128]]), ['B%d' % bi], ['qT'])
            rdone(iq[0]); rdone(iq[1])
            for bl in range(nb_h):
                b = bl0 + bl
                has_prev = (not meta) and (b > 0 or not first)
                for kv in range(4):
                    rq = AP(RAb, 0, 64, QT0 + bl * 2048 + kv * 512, [[1, 512]])
                    pset = ctr['pt'] % 2; ctr['pt'] += 1
                    PTc = pset * 1536
                    PTp = PTc + 512
                    PTm = PTc + 1024
                    kc_, kp_, km_ = 'PTc%d' % pset, 'PTp%d' % pset, 'PTm%d' % pset
                    if not meta:
                        MM(AP(BK[4], 0, 128, 0, [[1, 512]]), AP(RAb, 0, 64, KT0 + kv * 1152 + 128 + b * 128, [[1, 128]]), rq, True, True, ['kT', 'qT'], ['B4'])
                        if has_prev:
                            MM(AP(BK[5], 0, 128, 0, [[1, 512]]), AP(RAb, 0, 64, KT0 + kv * 1152 + b * 128, [[1, 128]]), rq, True, True, ['kT', 'qT'], ['B5'])
                        MM(AP(BK[6], 0, 16, 0, [[1, 512]]), AP(kTm, 0, 64, kv * 16, [[1, 16]]), rq, True, True, ['kTm', 'qT'], ['B6'])
                        t = tmp()
                        ACT(AP(TP, 0, 128, t * 512, [[1, 512]]), AP(BK[4], 0, 128, 0, [[1, 512]]), AF.Exp, ['B4'], ['T%d' % t], scale=0.125)
                        TT('dve', AP(PTB, 0, 128, PTc, [[1, 512]]), AP(TP, 0, 128, t * 512, [[1, 512]]), AP(Ecur, 0, 128, kv * 512, [[1, 512]]), ALU.mult,
                           ['T%d' % t, 'Ecur'], [kc_])
                        if has_prev:
                            t = tmp()
                            ACT(AP(TP, 0, 128, t * 512, [[1, 512]]), AP(BK[5], 0, 128, 0, [[1, 512]]), AF.Exp, ['B5'], ['T%d' % t], scale=0.125)
                            TT('dve', AP(PTB, 0, 128, PTp, [[1, 512]]), AP(TP, 0, 128, t * 512, [[1, 512]]), AP(Eprev, 0, 128, kv * 512, [[1, 512]]), ALU.mult,
                               ['T%d' % t, 'Eprev'], [kp_])
                        ACT(AP(PTB, 0, 16, PTm, [[1, 512]]), AP(BK[6], 0, 16, 0, [[1, 512]]), AF.Exp, ['B6'], [km_], scale=0.125)
                    else:
                        MM(AP(BK[6], 0, 16, 0, [[1, 512]]), AP(kTm, 0, 64, kv * 16, [[1, 16]]), rq, True, True, ['kTm', 'qT'], ['B6'])
                        t = tmp()
                        ACT(AP(TP, 0, 16, t * 512, [[1, 512]]), AP(BK[6], 0, 16, 0, [[1, 512]]), AF.Exp, ['B6'], ['T%d' % t], scale=0.125)
                        TT('dve', AP(PTB, 0, 16, PTm, [[128, 4], [1, 128]]), AP(TP, 0, 16, t * 512, [[128, 4], [1, 128]]),
                           AP(maskc, 0, 16, 0, [[0, 4], [1, 128]]), ALU.mult, ['T%d' % t, 'maskc'], [km_])
                    for hl in range(4):
                        lst = []
                        if not meta:
                            lst.append((AP(PTB, 0, 128, PTc + hl * 128, [[1, QB]]), AP(RAb, 0, 128, VA0 + (b + 1) * 260 + kv * 65, [[1, 65]]), [kc_, 'Vatt']))
                            if has_prev:
                                lst.append((AP(PTB, 0, 128, PTp + hl * 128, [[1, QB]]), AP(RAb, 0, 128, VA0 + b * 260 + kv * 65, [[1, 65]]), [kp_, 'Vatt']))
                        lst.append((AP(PTB, 0, 16, PTm + hl * 128, [[1, QB]]), AP(Vm, 0, 16, kv * 65, [[1, 65]]), [km_, 'Vm']))
                        for ii, (l_, r_, ks) in enumerate(lst):
                            MM(AP(BK[7], 0, QB, hl * 128, [[1, 65]]), l_, r_, ii == 0, ii == len(lst) - 1, ks, ['B7'])
                    dk = 'den%d' % (kv % 2)
                    dn = AP(den_t, 0, QB, (kv % 2) * 4, [[1, 4]])
                    rd = AP(rden_t, 0, QB, (kv % 2) * 4, [[1, 4]])
                    TT('dve', dn, AP(BK[7], 0, QB, 64, [[128, 4]]), AP(expsink, 0, QB, kv * 4, [[1, 4]]), ALU.add, ['B7', 'expsink'], [dk])
                    RCP(rd, dn, [dk], [dk])
                    TT('dve', AP(Otok, 0, QB, kv * 256, [[64, 4], [1, 64]]), AP(BK[7], 0, QB, 0, [[128, 4], [1, 64]]),
                       AP(rden_t, 0, QB, (kv % 2) * 4, [[1, 4], [0, 64]]), ALU.mult, ['B7', dk], ['Otok'])
                bi = b % 2
                for kc in range(8):
                    TR(AP(BKb[bi], 0, 128, kc * 128, [[1, QB]]), AP(Otok, 0, QB, kc * 128, [[1, 128]]), AP(ident_b, 0, QB, 0, [[1, QB]]),
                       ['Otok', 'ident_b'], ['B%d' % bi])
                CP(evq(), AP(Hn8, 0, 128, b * 128, [[1024, 8], [1, QB]]), AP(BKb[bi], 0, 128, 0, [[128, 8], [1, QB]]), ['B%d' % bi], ['OT'])
        if not meta:
            CP('dve', AP(kTp, 0, 64, 0, [[128, 4], [1, 128]]), AP(RAb, 0, 64, KT0 + 1024, [[1152, 4], [1, 128]]), ['kT'], ['kTp'])
            CP('dve', AP(Vp, 0, 128, 0, [[1, 260]]), AP(RAb, 0, 128, VA0 + 8 * 260, [[1, 260]]), ['Vatt'], ['Vp'])
        for half in range(2):
            io = rget('wo', half)
            for s in range(8):
                bi = 2 + (ctr['bk'] % 2); ctr['bk'] += 1
                for kc in range(8):
                    MM(AP(BK[bi], 0, nt, 0, [[1, 512]]), AP(Hn8, 0, 128, kc * 1024 + s, [[8, nt]]), AP(RING, 0, 128, rb(io) + kc * 512, [[1, 512]]),
                       kc == 0, kc == 7, ['OT', rk(io)], ['B%d' % bi])
                resid_add(nt, s, half, bi)
            rdone(io)

    def mlp(nt, l):
        ntok = nt * 8
        nth = max(1, ntok // 512)
        tn = min(512, ntok)
        aTk = [[['aT%d.%d.%d' % (b_, fc, th) for th in range(2)] for fc in range(8)] for b_ in range(2)]
        P.claim('RA', [k for a in aTk for bb_ in a for k in bb_])

        def up(qd):
            buf = qd % 2
            for j in range(2):
                iu = rget('up', l, qd, j)
                for fcl in range(4):
                    fc = j * 4 + fcl
                    for th in range(nth):
                        bi = 2 + (ctr['bk'] % 2); ctr['bk'] += 1
                        for kc in range(8):
                            MM(AP(BK[bi], 0, 128, 0, [[1, tn]]), AP(RING, 0, 128, rb(iu) + kc * 512 + fcl * 128, [[1, 128]]),
                               AP(hnT, 0, 128, kc * 1024 + th * 512, [[1, tn]]), kc == 0, kc == 7, [rk(iu), TK[kc]], ['B%d' % bi])
                        t = tmp()
                        tv = AP(TP, 0, 128, t * 512, [[1, tn]])
                        ACT(tv, AP(BK[bi], 0, 128, 0, [[1, tn]]), AF.Relu, ['B%d' % bi], ['T%d' % t])
                        TT('dve', AP(RAb, 0, 128, buf * 8192 + fc * 1024 + th * 512, [[1, tn]]), tv, tv, ALU.mult, ['T%d' % t], [aTk[buf][fc][th]])
                rdone(iu)

        def down(qd):
            buf = qd % 2
            idn = [rget('dn', l, qd, 0), rget('dn', l, qd, 1)]
            rkeys = [k for fc in range(8) for k in aTk[buf][fc][:nth]]
            for s in range(8):
                for half in range(2):
                    bi = 4 + (ctr['bk'] % 2); ctr['bk'] += 1
                    for fc in range(8):
                        ii = idn[fc // 4]
                        MM(AP(BK[bi], 0, nt, 0, [[1, 512]]), AP(RAb, 0, 128, buf * 8192 + fc * 1024 + s, [[8, nt]]),
                           AP(RING, 0, 128, rb(ii) + (fc % 4) * 1024 + half * 512, [[1, 512]]), fc == 0, fc == 7, rkeys + [rk(ii)], ['B%d' % bi])
                    resid_add(nt, s, half, bi)
            rdone(idn[0]); rdone(idn[1])
        up(0)
        up(1)
        down(0)
        up(2)
        down(1)
        up(3)
        down(2)
        down(3)

    def ssm(nt, meta):
        UK = ['U.%d' % i for i in range(8)]
        P.claim('hnT', UK)
        SKEYS = ['Wb0', 'Vb0', 'Vb1'] + ['P1b%d' % i for i in range(4)] + ['P2b%d' % i for i in range(4)]
        P.claim('RA', SKEYS)
        P.claim('Hn8', HNK)
        for g8 in range(8):
            bi = g8 % 2
            for gl in range(8):
                g = g8 * 8 + gl
                TR(AP(BKb[bi], 0, 128, gl * 128, [[1, nt]]), AP(Hn8, 0, nt, g * 128, [[1, 128]]), AP(ident_b, 0, nt, 0, [[1, nt]]), HNK + ['ident_b'], ['B%d' % bi])
            CP(evq(), AP(hnT, 0, 128, g8 * 1024, [[128, 8], [1, nt]]), AP(BKb[bi], 0, 128, 0, [[128, 8], [1, nt]]), ['B%d' % bi], [UK[g8]])
        if not meta:
            P.claim('Hn8', ['Y8.%d' % i for i in range(8)])
        WB = [0, 0]
        VB = [1024, 2056]
        P1B = [6176 + i * 1024 for i in range(4)]
        P2B = [6176 + 4096 + i * 1024 for i in range(4)]
        pend = {}

        def stage1(q, ia):
            for bb_ in range(2):
                gb = 2 * q + bb_
                i2 = gb % 2
                g0 = gb * 8
                for gl in range(8):
                    gq = bb_ * 8 + gl
                    b1, b2 = 4 + gl // 4, 6 + gl // 4
                    rhs = AP(hnT, 0, 128, (g0 + gl) * 128, [[1, nt]])
                    MM(AP(BK[b1], 0, 128, (gl % 4) * 128, [[1, nt]]), AP(RING, 0, 128, rb(ia) + gq * 128, [[1, 128]]), rhs, True, True, [rk(ia), UK[gb]], ['B%d' % b1])
                    MM(AP(BK[b2], 0, 128, (gl % 4) * 128, [[1, nt]]), AP(RING, 0, 128, rb(ia) + 2048 + gq * 128, [[1, 128]]), rhs, True, True, [rk(ia), UK[gb]], ['B%d' % b2])
                i4 = gb % 4
                wk, vk, p1k, p2k = 'Wb0', 'Vb%d' % i2, 'P1b%d' % i4, 'P2b%d' % i4
                for hb in range(2):
                    b1, b2 = 4 + hb, 6 + hb
                    d3 = [[128, 4], [1, nt]]
                    cosv = AP(COSt, 0, 128, (g0 + hb * 4) * 129 + 1, [[129, 4], [1, nt]])
                    sinv = AP(SINt, 0, 128, (g0 + hb * 4) * 129 + 1, [[129, 4], [1, nt]])
                    wv = AP(RA, 0, 128, WB[i2] + hb * 512, d3)
                    t = tmp()
                    tv = AP(TP, 0, 128, t * 512, d3)
                    TT('dve', wv, AP(BK[b1], 0, 128, 0, d3), cosv, ALU.mult, ['B%d' % b1, 'COSt'], [wk])
                    TT('dve', tv, AP(BK[b2], 0, 128, 0, d3), sinv, ALU.mult, ['B%d' % b2, 'SINt'], ['T%d' % t])
                    TT('dve', wv, wv, tv, ALU.add, [wk, 'T%d' % t], [wk])
                CP('dve', AP(RA, 0, 128, VB[i2], [[129, 8]]), AP(Xcar, 0, 128, g0, [[1, 8]]), ['Xcar.%d' % gb], [vk])
                for gl in range(8):
                    P.op('dve', lambda e, gl=gl, i2=i2, g0=g0: e.tensor_tensor_scan(
                        out=AP(RA, 0, 128, VB[i2] + gl * 129 + 1, [[1, nt]]), data0=AP(R8d, 0, 128, g0 + gl, [[0, nt]]),
                        data1=AP(RA, 0, 128, WB[i2] + gl * 128, [[1, nt]]), initial=AP(RA, 0, 128, VB[i2] + gl * 129, [[1, 1]]),
                        op0=ALU.mult, op1=ALU.add), [wk, vk, 'R8d'], [vk])
                if not meta:
                    vv = AP(RA, 0, 128, VB[i2], [[129, 8], [1, nt]])
                    TT('dve', AP(RAb, 0, 128, P1B[i4], [[128, 8], [1, nt]]), vv, AP(COSt, 0, 128, g0 * 129, [[129, 8], [1, nt]]), ALU.mult, [vk, 'COSt'], [p1k])
                    TT('dve', AP(RAb, 0, 128, P2B[i4], [[128, 8], [1, nt]]), vv, AP(SINt, 0, 128, g0 * 129, [[129, 8], [1, nt]]), ALU.mult, [vk, 'SINt'], [p2k])
                vl = lambda p0: AP(RA, p0, 64, VB[i2] + nt, [[129, 8]])
                sk = 'vsw%d' % i2
                P.dma('sp', AP(vsw_t, 64, 64, i2 * 8, [[1, 8]]), vl(0), sk, [vk], [sk], allow_slow_non_contiguous=True)
                P.dma('sp', AP(vsw_t, 0, 64, i2 * 8, [[1, 8]]), vl(64), sk, [vk], [sk], allow_slow_non_contiguous=True)
                pend[gb] = (i2, g0, vk, sk)

        def carry(gb):
            i2, g0, vk, sk = pend.pop(gb)
            c1 = AP(ct_t, 0, 128, i2 * 16, [[1, 8]])
            c2 = AP(ct_t, 0, 128, i2 * 16 + 8, [[1, 8]])
            ck = 'ct%d' % i2
            TT('dve', c1, AP(RA, 0, 128, VB[i2] + nt, [[129, 8]]), AP(COSt, 0, 128, g0 * 129 + nt, [[129, 8]]), ALU.mult, [vk, 'COSt'], [ck])
            TT('dve', c2, AP(vsw_t, 0, 128, i2 * 8, [[1, 8]]), AP(SINt, 0, 128, g0 * 129 + nt, [[129, 8]]), ALU.mult, [sk, 'SINt'], [ck])
            STT('dve', AP(Xcar, 0, 128, g0, [[1, 8]]), c2, AP(SGNI, 0, 128, 0, [[1, 1]]), c1, ALU.mult, ALU.add, [ck, 'SGNI'], ['Xcar.%d' % gb])

        def stage2(q, ib, ic):
            for bb_ in range(2):
                gb = 2 * q + bb_
                i2 = gb % 2
                g0 = gb * 8
                i4 = gb % 4
                p1k, p2k = 'P1b%d' % i4, 'P2b%d' % i4
                for hb in range(2):
                    bi = 2 + (ctr['bk'] % 2); ctr['bk'] += 1
                    for g4 in range(4):
                        gl = hb * 4 + g4
                        gq = bb_ * 8 + gl
                        o = AP(BK[bi], 0, nt, g4 * 128, [[1, 128]])
                        MM(o, AP(hnT, 0, 128, (g0 + gl) * 128, [[1, nt]]), AP(RING, 0, 128, rb(ib) + gq * 128, [[1, 128]]), True, False, ['U.%d' % gb, rk(ib)], ['B%d' % bi])
                        MM(o, AP(RAb, 0, 128, P1B[i4] + gl * 128, [[1, nt]]), AP(RING, 0, 128, rb(ib) + 2048 + gq * 128, [[1, 128]]), False, False, [p1k, rk(ib)], ['B%d' % bi])
                        MM(o, AP(RAb, 0, 128, P2B[i4] + gl * 128, [[1, nt]]), AP(RING, 0, 128, rb(ic) + gq * 128, [[1, 128]]), False, True, [p2k, rk(ic)], ['B%d' % bi])
                    ACT(AP(Hn8, 0, nt, (g0 + hb * 4) * 16, [[16, 4], [1024, 8], [1, 16]]), AP(BK[bi], 0, nt, 0, [[128, 4], [16, 8], [1, 16]]), AF.Gelu_apprx_tanh,
                        ['B%d' % bi], ['Y8.%d' % gb])

        if meta:
            ias = []
            for q in range(4):
                ia = rget('A', q)
                stage1(q, ia)
                rdone(ia)
                carry(2 * q)
                carry(2 * q + 1)
            return
        ia = rget('A', 0)
        stage1(0, ia)
        rdone(ia)
        for q in range(4):
            carry(2 * q)
            carry(2 * q + 1)
            if q + 1 < 4:
                ia = rget('A', q + 1)
                stage1(q + 1, ia)
                rdone(ia)
            ib = rget('B', q)
            ic = rget('C', q)
            stage2(q, ib, ic)
            rdone(ib); rdone(ic)
        YK = ['Y8.%d' % i for i in range(8)]
        to_featmajor(nt, None, YK)
        for half in range(2):
            iv = rget('glu', 0, half)
            ig = rget('glu', 1, half)
            for s in range(8):
                bv = 2 + (ctr['bk'] % 2); ctr['bk'] += 1
                bg = 4 + (ctr['bk'] % 2)
                for kc in range(8):
                    lt = AP(hnT, 0, 128, kc * 1024 + s, [[8, nt]])
                    MM(AP(BK[bv], 0, nt, 0, [[1, 512]]), lt, AP(RING, 0, 128, rb(iv) + kc * 512, [[1, 512]]), kc == 0, kc == 7, [TK[kc], rk(iv)], ['B%d' % bv])
                for kc in range(8):
                    lt = AP(hnT, 0, 128, kc * 1024 + s, [[8, nt]])
                    MM(AP(BK[bg], 0, nt, 0, [[1, 512]]), lt, AP(RING, 0, 128, rb(ig) + kc * 512, [[1, 512]]), kc == 0, kc == 7, [TK[kc], rk(ig)], ['B%d' % bg])
                t = tmp()
                tv = AP(TP, 0, nt, t * 512, [[1, 512]])
                ACT(tv, AP(BK[bg], 0, nt, 0, [[1, 512]]), AF.Sigmoid, ['B%d' % bg], ['T%d' % t])
                TT('dve', tv, AP(BK[bv], 0, nt, 0, [[1, 512]]), tv, ALU.mult, ['B%d' % bv, 'T%d' % t], ['T%d' % t])
                hs = AP(H8, 0, nt, s * 1024 + half * 512, [[1, 512]])
                TT('dve', hs, hs, tv, ALU.add, [HK[s], 'T%d' % t], [HK[s]])
            rdone(iv); rdone(ig)

    def layer0(nt, meta, first):
        rmsnorm(nt, 'std')
        to_featmajor(nt, 0, HNK)
        attention(nt, meta, first)
        rmsnorm(nt, 'std')
        to_featmajor(nt, 1, HNK)
        mlp(nt, 0)

    def layer1(nt):
        rmsnorm(nt, 'gsc')
        ssm(nt, False)
        rmsnorm(nt, 'std')
        to_featmajor(nt, 2, HNK)
        mlp(nt, 1)

    MS('dve', AP(Xcar, 0, 128, 0, [[1, 64]]), 0.0, ['Xcar.%d' % i for i in range(8)])
    P.dma('sp', AP(H8, 0, 2, 0, [[1, 8192]]), DR(meta_d, 0, [8192, 2], [1, 8192]), 'xin', (), HK)
    layer0(2, True, True)
    rmsnorm(2, 'gsc')
    ssm(2, True)
    CP('dve', AP(Xmeta, 0, 128, 0, [[1, 64]]), AP(Xcar, 0, 128, 0, [[1, 64]]), ['Xcar.%d' % i for i in range(8)], ['Xmeta'])
    P.emit()

    if stop == 'meta':
        return finish()
    for ti in range(ntiles):
        sq, tl = ti // 4, ti % 4
        first = (tl == 0)
        off = (sq * SEQ + tl * 1024) * D
        P.dma('sp', AP(H8, 0, 128, 0, [[1, 8192]]), DR(x_d, off, [8192, 128], [1, 8192]), 'xin', (), HK)
        if first:
            CP('dve', AP(Xcar, 0, 128, 0, [[1, 64]]), AP(Xmeta, 0, 128, 0, [[1, 64]]), ['Xmeta'], ['Xcar.%d' % i for i in range(8)])
        layer0(128, False, first)
        if debug_stage != 1:
            layer1(128)
            MS('dve', AP(ss_t, 0, 128, 0, [[1, 8]]), 0.0, ['ss'])
            for s in range(8):
                ACT(AP(junk, 0, 128, 0, [[1, 1024]]), AP(H8, 0, 128, s * 1024, [[1, 1024]]), AF.Square, [HK[s], 'ss'], ['ss'],
                    accum_out=AP(ss_t, 0, 128, s, [[1, 1]]))
            ACT(AP(rstd_t, 0, 128, 0, [[1, 8]]), AP(ss_t, 0, 128, 0, [[1, 8]]), AF.Sqrt, ['ss', 'epsc'], ['rstd'],
                bias=AP(epsc, 0, 128, 0, [[1, 1]]), scale=float(1.0 / D))
            RCP(AP(rstd_t, 0, 128, 0, [[1, 8]]), AP(rstd_t, 0, 128, 0, [[1, 8]]), ['rstd'], ['rstd'])
            for s in range(8):
                hs = AP(H8, 0, 128, s * 1024, [[1, 1024]])
                STT('dve', hs, hs, AP(rstd_t, 0, 128, s, [[1, 1]]), AP(wfin, 0, 128, 0, [[1, 1024]]), ALU.mult, ALU.mult, [HK[s], 'rstd', 'wfin'], [HK[s]])
        P.dma('sp', DR(y_d, off, [8192, 128], [1, 8192]), AP(H8, 0, 128, 0, [[1, 8192]]), 'yout', HK, ())
        P.emit()
    waits = P._waits('sp', (), HK)
    P.ops['sp'].append((waits, None, None, 0))
    P.emit()
    es.close()
    return nc


_INPUT_NAMES = ["meta_tokens", "attn_norm_w", "attn_w_qkv", "attn_sinks", "attn_w_o", "ssm_norm_w", "ssm_lambda_re",
                "ssm_lambda_im", "ssm_log_dt", "ssm_b_re", "ssm_b_im", "ssm_c_re", "ssm_c_im", "ssm_d", "ssm_w_glu",
                "mlp_norm_w", "mlp_w_up", "mlp_w_down", "final_norm_w"]


def kernel(**inputs):
    x = np.ascontiguousarray(np.asarray(inputs["x"], dtype=np.float32))
    shared = {k: np.ascontiguousarray(np.asarray(inputs[k], dtype=np.float32)) for k in _INPUT_NAMES}
    nc = build()
    in_maps = []
    for c in range(NCORE):
        m = dict(shared)
        m["x"] = x[2 * c:2 * c + 2]
        in_maps.append(m)
    res = run_bass_kernel_spmd(nc, in_maps, core_ids=list(range(NCORE)))
    out = np.concatenate([np.asarray(r["y"], dtype=np.float32) for r in res.results], axis=0)
    return out
```

```python
import numpy as np
from contextlib import ExitStack
import concourse.bass as bass
import concourse.mybir as mybir
from concourse.bass_utils import run_bass_kernel_spmd

F32 = mybir.dt.float32
BF16 = mybir.dt.bfloat16
I32 = mybir.dt.int32
ALU = mybir.AluOpType
AF = mybir.ActivationFunctionType

D = 1024
SEQ = 4096
NCORE = 8
ENGS = ('pe', 'act', 'dve', 'pool', 'sp')
BLOCKNAME = {'pe': 'tensor', 'act': 'scalar', 'dve': 'vector', 'pool': 'gpsimd', 'sp': 'sync'}
PI = float(np.pi)
NSLOT = 5
SLOTE = 4096


class Prog:
    def __init__(self, nc, es):
        self.nc = nc
        self.es = es
        self.ops = {e: [] for e in ENGS}
        self.cnt = {e: 0 for e in ENGS}
        self.known = {e: {} for e in ENGS}
        self.state = {}
        self.dmacnt = {}
        self.sems = {}
        self.regions = {}
        self.groupkeys = {}

    def _waits(self, eng, reads, writes):
        deps = {}

        def addd(d):
            for k, v in d.items():
                if deps.get(k, 0) < v:
                    deps[k] = v
        for key in reads:
            st = self.state.get(key)
            if st:
                addd(st[0])
        for key in writes:
            st = self.state.get(key)
            if st:
                addd(st[0])
                addd(st[1])
        waits = []
        for k, v in deps.items():
            if k == 'pe' and eng == 'pe':
                continue
            if k in ENGS:
                assert v <= self.cnt[k], f"dep on unemitted milestone {k} {v} > {self.cnt[k]}"
            if self.known[eng].get(k, 0) >= v:
                continue
            self.known[eng][k] = v
            waits.append((k, v))
        return waits

    def _update(self, tag, v, reads, writes):
        ws = set(writes)
        for key in ws:
            self.state[key] = [{tag: v}, {}]
        for key in reads:
            if key in ws:
                continue
            st = self.state.setdefault(key, [{}, {}])
            st[1][tag] = v

    def op(self, eng, fn, reads=(), writes=(), ms=True):
        waits = self._waits(eng, reads, writes)
        if ms:
            self.cnt[eng] += 1
            self.ops[eng].append((waits, fn, eng, 1))
            self._update(eng, self.cnt[eng], reads, writes)
        else:
            self.ops[eng].append((waits, fn, None, 0))
            self._update(eng, self.cnt[eng] + 1, reads, writes)

    def dma(self, q, out, in_, semkey, reads=(), writes=(), group=False, **kw):
        waits = self._waits(q, reads, writes)
        v = self.dmacnt.get(semkey, 0) + 16
        self.dmacnt[semkey] = v
        self.ops[q].append((waits, lambda e: e.dma_start(out=out, in_=in_, **kw), semkey, 16))
        self._update(semkey, v, reads, writes)
        if group:
            self.groupkeys.setdefault(semkey, set()).update(writes)

    def group_done(self, semkey):
        tot = self.dmacnt[semkey]
        for key in self.groupkeys.get(semkey, ()):
            st = self.state.get(key)
            if st and semkey in st[0]:
                st[0][semkey] = tot
        self.groupkeys[semkey] = set()

    def claim(self, region, keys):
        reg = self.regions.setdefault(region, set())
        merged = {}
        for k in reg:
            st = self.state.get(k)
            if st:
                for d in (st[0], st[1]):
                    for kk, v in d.items():
                        if merged.get(kk, 0) < v:
                            merged[kk] = v
        for k in keys:
            self.state[k] = [dict(merged), {}]
            reg.add(k)

    def barrier(self):
        tot = {k: v for k, v in self.cnt.items() if v > 0}
        tot.update(self.dmacnt)
        for e in ENGS:
            waits = []
            for k, v in tot.items():
                if k == e:
                    continue
                if self.known[e].get(k, 0) >= v:
                    continue
                self.known[e][k] = v
                waits.append((k, v))
            self.ops[e].append((waits, None, None, 0))

    def emit(self):
        nc = self.nc
        for k in list(ENGS) + list(self.dmacnt.keys()):
            if k not in self.sems:
                self.sems[k] = self.es.enter_context(nc.semaphore("s_" + str(k)))
        sems = self.sems
        with nc.Block() as block:
            for e in ENGS:
                oplist = self.ops[e]

                def body(engobj, oplist=oplist):
                    for waits, fn, inc, amt in oplist:
                        for k, v in waits:
                            engobj.wait_ge(sems[k], v)
                        if fn is None:
                            continue
                        ins = fn(engobj)
                        if inc is not None:
                            ins.then_inc(sems[inc], amt)
                getattr(block, BLOCKNAME[e])(body)
        self.ops = {e: [] for e in ENGS}


def AP(t, p0, npart, off, dims):
    rowlen = 1
    for s in t.shape[1:]:
        rowlen *= s
    return bass.AP(t, p0 * rowlen + off, [[rowlen, npart]] + [list(d) for d in dims])


def DR(t, off, *dims):
    return bass.AP(t, off, [list(d) for d in dims])


def build(debug_stage=0, stop=None):
    import os
    stop = stop or os.environ.get('KSTOP')
    nc = bass.Bass("TRN2", target_bir_lowering=False)
    es = ExitStack()
    P = Prog(nc, es)

    small_only = stop is not None and stop.startswith('p')
    BIGIN = ("x", "attn_w_qkv", "attn_w_o", "ssm_w_glu", "mlp_w_up", "mlp_w_down")

    def din(name, shape):
        if small_only and name in BIGIN:
            return None
        if stop == 'meta' and name in ("x", "ssm_w_glu"):
            return None
        return nc.dram_tensor(name, list(shape), F32, kind="ExternalInput")
    x_d = din("x", [2, SEQ, D])
    meta_d = din("meta_tokens", [16, D])
    anw_d = din("attn_norm_w", [1, D])
    wqkv_d = din("attn_w_qkv", [1, D, 1536])
    sink_d = din("attn_sinks", [1, 16])
    wo_d = din("attn_w_o", [1, D, D])
    snw_d = din("ssm_norm_w", [1, D])
    lre_d = din("ssm_lambda_re", [1, 64, 64])
    lim_d = din("ssm_lambda_im", [1, 64, 64])
    ldt_d = din("ssm_log_dt", [1, 64])
    bre_d = din("ssm_b_re", [1, 64, 64, 16])
    bim_d = din("ssm_b_im", [1, 64, 64, 16])
    cre_d = din("ssm_c_re", [1, 64, 16, 64])
    cim_d = din("ssm_c_im", [1, 64, 16, 64])
    sd_d = din("ssm_d", [1, D])
    wglu_d = din("ssm_w_glu", [1, D, 2048])
    mnw_d = din("mlp_norm_w", [2, D])
    wup_d = din("mlp_w_up", [2, D, 4096])
    wdn_d = din("mlp_w_down", [2, 4096, D])
    fnw_d = din("final_norm_w", [D])
    y_d = nc.dram_tensor("y", [2, SEQ, D], F32, kind="ExternalOutput")
    SA_d = nc.dram_tensor("scrA", [4, 128, 4096], BF16)
    SB_d = nc.dram_tensor("scrB", [4, 128, 4096], BF16)
    SC_d = nc.dram_tensor("scrC", [4, 128, 2048], BF16)

    def sb(name, shape, dt):
        return nc.alloc_sbuf_tensor(name, list(shape), dt)
    scwn = [0]

    def finish():
        P.barrier()
        P.emit()
        es.close()
        return nc

    def scw():
        scwn[0] += 1
        return 'scw%d' % scwn[0]

    def TT(eng, out, in0, in1, op, r, w):
        P.op(eng, lambda e: e.tensor_tensor(out=out, in0=in0, in1=in1, op=op), r, w)

    def TS(eng, out, in0, s1, op0, r, w, s2=None, op1=None):
        if op1 is None:
            P.op(eng, lambda e: e.tensor_scalar(out=out, in0=in0, scalar1=s1, scalar2=None, op0=op0), r, w)
        else:
            P.op(eng, lambda e: e.tensor_scalar(out=out, in0=in0, scalar1=s1, scalar2=s2, op0=op0, op1=op1), r, w)

    def STT(eng, out, in0, sc, in1, op0, op1, r, w):
        P.op(eng, lambda e: e.scalar_tensor_tensor(out=out, in0=in0, scalar=sc, in1=in1, op0=op0, op1=op1), r, w)

    def ACT(out, in_, func, r, w, **kw):
        P.op('act', lambda e: e.activation(out=out, in_=in_, func=func, **kw), r, w)

    def CP(eng, out, in_, r, w):
        if eng == 'act':
            P.op('act', lambda e: e.copy(out=out, in_=in_), r, w)
        else:
            P.op(eng, lambda e: e.tensor_copy(out=out, in_=in_), r, w)

    def MS(eng, out, val, w):
        P.op(eng, lambda e: e.memset(out, val), (), w)

    def MM(out, lhsT, rhs, st, sp, r, w, ms=None):
        P.op('pe', lambda e: e.matmul(out, lhsT=lhsT, rhs=rhs, start=st, stop=sp), r, w, ms=(sp if ms is None else ms))

    def TR(out, in_, ident, r, w, ms=True):
        P.op('pe', lambda e: e.transpose(out=out, in_=in_, identity=ident), r, w, ms=ms)

    def RCP(out, in_, r, w):
        P.op('dve', lambda e: e.reciprocal(out=out, in_=in_), r, w)

    ident_f = sb("ident_f", [128, 128], F32)
    ident_b = sb("ident_b", [128, 128], BF16)
    wfm = sb("wfm", [128, 24], F32)
    expsink = sb("expsink", [128, 16], F32)
    epsc = sb("epsc", [128, 1], F32)
    SGN1 = sb("SGN1", [128, 1], F32)
    SGNI = sb("SGNI", [128, 1], F32)
    R8d = sb("R8d", [128, 64], F32)
    Xmeta = sb("Xmeta", [128, 64], F32)
    Xcar = sb("Xcar", [128, 64], F32)
    COSt = sb("COSt", [128, 64 * 129], BF16)
    SINt = sb("SINt", [128, 64 * 129], BF16)
    maskc = sb("maskc", [16, 128], BF16)
    kTm = sb("kTm", [64, 64], BF16)
    Vm = sb("Vm", [16, 260], BF16)
    kTp = sb("kTp", [64, 512], BF16)
    Vp = sb("Vp", [128, 260], BF16)
    ss_t = sb("ss_t", [128, 8], F32)
    rstd_t = sb("rstd_t", [128, 8], F32)
    den_t = sb("den_t", [128, 8], F32)
    rden_t = sb("rden_t", [128, 8], F32)
    vsw_t = sb("vsw_t", [128, 16], F32)
    ct_t = sb("ct_t", [128, 32], F32)
    junk = sb("junk", [128, 1024], BF16)
    BK = [nc.alloc_psum_tensor("B%d" % i, [128, 512], F32) for i in range(8)]
    BKb = [b.bitcast(BF16) for b in BK]

    MS('pool', AP(ident_f, 0, 128, 0, [[1, 128]]), 1.0, ['ident_f'])
    P.op('pool', lambda e: e.affine_select(out=AP(ident_f, 0, 128, 0, [[1, 128]]), in_=AP(ident_f, 0, 128, 0, [[1, 128]]),
                                           pattern=[[-1, 128]], compare_op=ALU.is_equal, fill=0.0, base=0, channel_multiplier=1),
         ['ident_f'], ['ident_f'])
    CP('dve', AP(ident_b, 0, 128, 0, [[1, 128]]), AP(ident_f, 0, 128, 0, [[1, 128]]), ['ident_f'], ['ident_b'])
    MS('pool', AP(epsc, 0, 128, 0, [[1, 1]]), 1e-6, ['epsc'])
    MS('pool', AP(SGN1, 0, 64, 0, [[1, 1]]), 1.0, ['SGN1'])
    MS('pool', AP(SGN1, 64, 64, 0, [[1, 1]]), -1.0, ['SGN1'])
    MS('pool', AP(SGNI, 0, 64, 0, [[1, 1]]), -1.0, ['SGNI'])
    MS('pool', AP(SGNI, 64, 64, 0, [[1, 1]]), 1.0, ['SGNI'])
    mk_f = sb("mk_f", [16, 128], F32)
    MS('pool', AP(mk_f, 0, 16, 0, [[1, 128]]), 1.0, ['mk_f'])
    P.op('pool', lambda e: e.affine_select(out=AP(mk_f, 0, 16, 0, [[1, 128]]), in_=AP(mk_f, 0, 16, 0, [[1, 128]]),
                                           pattern=[[1, 128]], compare_op=ALU.is_ge, fill=0.0, base=0, channel_multiplier=-1),
         ['mk_f'], ['mk_f'])
    CP('dve', AP(maskc, 0, 16, 0, [[1, 128]]), AP(mk_f, 0, 16, 0, [[1, 128]]), ['mk_f'], ['maskc'])
    for j, (dt_, off) in enumerate([(anw_d, 0), (mnw_d, 0), (mnw_d, D)]):
        P.dma('sp', AP(wfm, 0, 128, j * 8, [[1, 8]]), DR(dt_, off, [1, 128], [128, 8]), 'pl', (), ['wfm'], group=True,
              allow_slow_non_contiguous=True)
    P.dma('sp', AP(expsink, 0, 128, 0, [[1, 16]]), DR(sink_d, 0, [0, 128], [1, 16]), 'pl', (), ['expsink'], group=True)

    with ExitStack() as pes:
        def psb(name, shape, dt):
            return pes.enter_context(nc.sbuf_tensor(name, list(shape), dt))
        SCt = psb("SCt", [128, 40 * 64], F32)
        KI = psb("KI", [128, 64], I32)
        PWc = psb("PWc", [128, 64 * 9], F32)
        PWs = psb("PWs", [128, 64 * 9], F32)
        PRr = psb("PRr", [128, 64 * 8], F32)
        PIr = psb("PIr", [128, 64 * 8], F32)
        B_st = psb("B_st", [128, 1024], F32)
        B_sw = psb("B_sw", [128, 1024], F32)
        C_st = psb("C_st", [128, 1024], F32)
        C_sw = psb("C_sw", [128, 1024], F32)
        bb_st = psb("bb_st", [128, 1024], F32)
        bb_sw = psb("bb_sw", [128, 1024], F32)
        bb_sg = psb("bb_sg", [128, 1024], F32)
        WPt = psb("WPt", [128, 1024], F32)
        t1k = psb("t1k", [128, 1024], F32)
        t2k = psb("t2k", [128, 1024], F32)
        DTt = psb("DTt", [16, 1024], F32)
        WNb = psb("WNb", [16, 1024], F32)

        def sc(i):
            return AP(SCt, 0, 128, i * 64, [[1, 64]])
        (LR, LI, LDT, LRC, DTv, X1, ER, TH, TMP, KF, YY, SN, CS, ARr, AIi, NR, DEN, RDEN, FR, FI, T20, T21,
         C8, S8, TH2) = range(25)
        K_ = 'sc'
        for h in range(2):
            P.dma('sp', AP(SCt, 64 * h, 64, LR * 64, [[1, 64]]), DR(lre_d, 0, [1, 64], [64, 64]), 'pl', (), [K_], group=True,
                  allow_slow_non_contiguous=True)
            P.dma('sp', AP(SCt, 64 * h, 64, LI * 64, [[1, 64]]), DR(lim_d, 0, [1, 64], [64, 64]), 'pl', (), [K_], group=True,
                  allow_slow_non_contiguous=True)
        P.dma('sp', sc(LDT), DR(ldt_d, 0, [0, 128], [1, 64]), 'pl', (), [K_], group=True)
        for (dst, top, bot) in ((B_st, bre_d, bim_d), (B_sw, bim_d, bre_d)):
            P.dma('sp', AP(dst, 0, 64, 0, [[16, 64], [1, 16]]), DR(top, 0, [16, 64], [1024, 64], [1, 16]), 'pl', (), [dst.name], group=True)
            P.dma('sp', AP(dst, 64, 64, 0, [[16, 64], [1, 16]]), DR(bot, 0, [16, 64], [1024, 64], [1, 16]), 'pl', (), [dst.name], group=True)
        P.dma('sp', AP(WPt, 0, 128, 0, [[1, 1024]]), DR(snw_d, 0, [0, 128], [1, 1024]), 'pl', (), ['WPt'], group=True)
        P.dma('sp', AP(DTt, 0, 16, 0, [[1, 1024]]), DR(sd_d, 0, [0, 16], [1, 1024]), 'pl', (), ['DTt'], group=True)
        P.dma('sp', AP(WNb, 0, 16, 0, [[1, 1024]]), DR(snw_d, 0, [0, 16], [1, 1024]), 'pl', (), ['WNb'], group=True)
        with ExitStack() as ces:
            CnA = ces.enter_context(nc.sbuf_tensor("CnA", [128, 1024], F32))
            CnB = ces.enter_context(nc.sbuf_tensor("CnB", [128, 1024], F32))
            for (dst, first, second) in ((CnA, cre_d, cim_d), (CnB, cim_d, cre_d)):
                P.dma('sp', AP(dst, 0, 128, 0, [[128, 8], [1, 64]]), DR(first, 0, [64, 128], [8192, 8], [1, 64]), 'pl', (), [dst.name], group=True)
                P.dma('sp', AP(dst, 0, 128, 64, [[128, 8], [1, 64]]), DR(second, 0, [64, 128], [8192, 8], [1, 64]), 'pl', (), [dst.name], group=True)
            P.group_done('pl')
            ACT(AP(expsink, 0, 128, 0, [[1, 16]]), AP(expsink, 0, 128, 0, [[1, 16]]), AF.Exp, ['expsink'], ['expsink'])
            for (src, dst) in ((CnA, C_st), (CnB, C_sw)):
                for j in range(8):
                    bk = BK[j % 2]
                    TR(AP(bk, 0, 128, 0, [[1, 128]]), AP(src, 0, 128, j * 128, [[1, 128]]), AP(ident_f, 0, 128, 0, [[1, 128]]),
                       [src.name, 'ident_f'], ['B%d' % (j % 2)])
                    CP('dve' if j % 2 == 0 else 'act', AP(dst, 0, 128, j * 128, [[1, 128]]), AP(bk, 0, 128, 0, [[1, 128]]),
                       ['B%d' % (j % 2)], [dst.name])
            P.emit()
            if stop == 'p0':
                return finish()

        def sTT(o, a, b, op):
            TT('dve', sc(o), sc(a), sc(b), op, [K_], [K_])

        def sTS(o, a, s1, op0, s2=None, op1=None):
            TS('dve', sc(o), sc(a), s1, op0, [K_], [K_], s2, op1)

        def sinof(o, ang):
            TS('dve', AP(KI, 0, 128, 0, [[1, 64]]), sc(ang), float(1.0 / (2 * PI)), ALU.mult, [K_], ['KI'])
            CP('dve', sc(KF), AP(KI, 0, 128, 0, [[1, 64]]), ['KI'], [K_])
            STT('dve', sc(YY), sc(KF), float(-2 * PI), sc(ang), ALU.mult, ALU.add, [K_], [K_])
            sTS(YY, YY, 3.14159, ALU.min, -3.14159, ALU.max)
            ACT(sc(o), sc(YY), AF.Sin, [K_], [K_])
        sTS(LRC, LR, -1e-4, ALU.min)
        ACT(sc(DTv), sc(LDT), AF.Exp, [K_], [K_])
        sTT(X1, LRC, DTv, ALU.mult)
        ACT(sc(ER), sc(X1), AF.Exp, [K_], [K_])
        ACT(AP(R8d, 0, 128, 0, [[1, 64]]), sc(X1), AF.Exp, [K_], ['R8d'], scale=8.0)
        sTT(TH, LI, DTv, ALU.mult)
        sinof(SN, TH)
        sTS(TH2, TH, float(PI / 2), ALU.add)
        sinof(CS, TH2)
        sTT(ARr, ER, CS, ALU.mult)
        sTT(AIi, ER, SN, ALU.mult)
        sTS(NR, ARr, -1.0, ALU.add)
        sTT(T20, LRC, LRC, ALU.mult)
        sTT(T21, LI, LI, ALU.mult)
        sTT(DEN, T20, T21, ALU.add)
        RCP(sc(RDEN), sc(DEN), [K_], [K_])
        sTT(T20, NR, LRC, ALU.mult)
        sTT(T21, AIi, LI, ALU.mult)
        sTT(T20, T20, T21, ALU.add)
        sTT(FR, T20, RDEN, ALU.mult)
        sTT(T20, AIi, LRC, ALU.mult)
        sTT(T21, NR, LI, ALU.mult)
        sTT(T20, T20, T21, ALU.subtract)
        sTT(FI, T20, RDEN, ALU.mult)

        def pw(t, k, stride=9):
            return AP(t, 0, 128, k, [[stride, 64]])
        MS('dve', pw(PWc, 0), 1.0, ['PW'])
        MS('dve', pw(PWs, 0), 0.0, ['PW'])
        for k in range(8):
            CP('dve', pw(PRr, 7 - k, 8), pw(PWc, k), ['PW'], ['PRr'])
            CP('dve', pw(PIr, 7 - k, 8), pw(PWs, k), ['PW'], ['PRr'])
            TT('dve', sc(T20), pw(PWc, k), sc(ARr), ALU.mult, ['PW', K_], [K_])
            TT('dve', sc(T21), pw(PWs, k), sc(AIi), ALU.mult, ['PW', K_], [K_])
            TT('dve', pw(PWc, k + 1), sc(T20), sc(T21), ALU.subtract, [K_], ['PW'])
            TT('dve', sc(T20), pw(PWc, k), sc(AIi), ALU.mult, ['PW', K_], [K_])
            TT('dve', sc(T21), pw(PWs, k), sc(ARr), ALU.mult, ['PW', K_], [K_])
            TT('dve', pw(PWs, k + 1), sc(T20), sc(T21), ALU.add, [K_], ['PW'])
        CP('dve', sc(C8), sc(CS), [K_], [K_])
        CP('dve', sc(S8), sc(SN), [K_], [K_])
        for _ in range(3):
            sTT(T20, C8, C8, ALU.mult)
            sTT(T21, S8, S8, ALU.mult)
            sTT(TMP, C8, S8, ALU.mult)
            sTT(C8, T20, T21, ALU.subtract)
            sTS(S8, TMP, 2.0, ALU.mult)
        if stop == 'p1':
            return finish()
        with ExitStack() as tes:
            TC = tes.enter_context(nc.sbuf_tensor("TC", [128, 64 * 129], F32))
            TSn = tes.enter_context(nc.sbuf_tensor("TSn", [128, 64 * 129], F32))
            T1 = tes.enter_context(nc.sbuf_tensor("T1", [128, 4096], F32))
            T2 = tes.enter_context(nc.sbuf_tensor("T2", [128, 4096], F32))

            def tb(t, m0, L, bc=False):
                return AP(t, 0, 128, m0, [[129, 64], [0 if bc else 1, L]])
            MS('dve', tb(TC, 0, 1), 1.0, ['TC'])
            MS('dve', tb(TSn, 0, 1), 0.0, ['TS'])
            CP('dve', tb(TC, 1, 1), AP(SCt, 0, 128, C8 * 64, [[1, 64], [1, 1]]), [K_], ['TC'])
            CP('dve', tb(TSn, 1, 1), AP(SCt, 0, 128, S8 * 64, [[1, 64], [1, 1]]), [K_], ['TS'])
            L = 1
            while L <= 64:
                o1 = AP(T1, 0, 128, 0, [[L, 64], [1, L]])
                o2 = AP(T2, 0, 128, 0, [[L, 64], [1, L]])
                TT('dve', o1, tb(TC, 1, L), tb(TC, L, L, True), ALU.mult, ['TC'], ['T1'])
                TT('dve', o2, tb(TSn, 1, L), tb(TSn, L, L, True), ALU.mult, ['TS'], ['T2'])
                TT('dve', tb(TC, L + 1, L), o1, o2, ALU.subtract, ['T1', 'T2'], ['TC'])
                TT('dve', o1, tb(TC, 1, L), tb(TSn, L, L, True), ALU.mult, ['TC', 'TS'], ['T1'])
                TT('dve', o2, tb(TSn, 1, L), tb(TC, L, L, True), ALU.mult, ['TC', 'TS'], ['T2'])
                TT('dve', tb(TSn, L + 1, L), o1, o2, ALU.add, ['T1', 'T2'], ['TS'])
                L *= 2
            CP('dve', AP(COSt, 0, 128, 0, [[1, 64 * 129]]), AP(TC, 0, 128, 0, [[1, 64 * 129]]), ['TC'], ['COSt'])
            CP('act', AP(SINt, 0, 128, 0, [[1, 64 * 129]]), AP(TSn, 0, 128, 0, [[1, 64 * 129]]), ['TS'], ['SINt'])
            P.emit()
        if stop == 'p2':
            return finish()
        g16 = [[16, 64], [1, 16]]
        g16b = [[1, 64], [0, 16]]
        TT('dve', AP(t1k, 0, 128, 0, g16), AP(B_st, 0, 128, 0, g16), AP(SCt, 0, 128, FR * 64, g16b), ALU.mult, ['B_st', K_], ['t1k'])
        TT('dve', AP(t2k, 0, 128, 0, g16), AP(B_sw, 0, 128, 0, g16), AP(SCt, 0, 128, FI * 64, g16b), ALU.mult, ['B_sw', K_], ['t2k'])
        STT('dve', AP(bb_st, 0, 128, 0, [[1, 1024]]), AP(t2k, 0, 128, 0, [[1, 1024]]), AP(SGNI, 0, 128, 0, [[1, 1]]),
            AP(t1k, 0, 128, 0, [[1, 1024]]), ALU.mult, ALU.add, ['t1k', 't2k', 'SGNI'], ['bb_st'])
        TT('dve', AP(t1k, 0, 128, 0, g16), AP(B_sw, 0, 128, 0, g16), AP(SCt, 0, 128, FR * 64, g16b), ALU.mult, ['B_sw', K_], ['t1k'])
        TT('dve', AP(t2k, 0, 128, 0, g16), AP(B_st, 0, 128, 0, g16), AP(SCt, 0, 128, FI * 64, g16b), ALU.mult, ['B_st', K_], ['t2k'])
        STT('dve', AP(bb_sw, 0, 128, 0, [[1, 1024]]), AP(t2k, 0, 128, 0, [[1, 1024]]), AP(SGN1, 0, 128, 0, [[1, 1]]),
            AP(t1k, 0, 128, 0, [[1, 1024]]), ALU.mult, ALU.add, ['t1k', 't2k', 'SGN1'], ['bb_sw'])
        full = [[1, 1024]]
        TT('dve', AP(bb_st, 0, 128, 0, full), AP(bb_st, 0, 128, 0, full), AP(WPt, 0, 128, 0, full), ALU.mult, ['bb_st', 'WPt'], ['bb_st'])
        TT('dve', AP(bb_sw, 0, 128, 0, full), AP(bb_sw, 0, 128, 0, full), AP(WPt, 0, 128, 0, full), ALU.mult, ['bb_sw', 'WPt'], ['bb_sw'])
        TS('dve', AP(bb_sg, 0, 128, 0, full), AP(bb_st, 0, 128, 0, full), AP(SGN1, 0, 128, 0, [[1, 1]]), ALU.mult, ['bb_st', 'SGN1'], ['bb_sg'])
        TT('dve', AP(DTt, 0, 16, 0, full), AP(DTt, 0, 16, 0, full), AP(WNb, 0, 16, 0, full), ALU.mult, ['DTt', 'WNb'], ['DTt'])
        TT('dve', AP(DTt, 0, 16, 0, g16), AP(DTt, 0, 16, 0, g16), AP(ident_f, 0, 16, 0, [[0, 64], [1, 16]]), ALU.mult, ['DTt', 'ident_f'], ['DTt'])
        P.emit()
        if stop == 'p3':
            return finish()
        with ExitStack() as bes:
            G1 = bes.enter_context(nc.sbuf_tensor("G1", [128, 4608], F32))
            G2 = bes.enter_context(nc.sbuf_tensor("G2", [128, 4608], F32))
            G3 = bes.enter_context(nc.sbuf_tensor("G3", [128, 4608], F32))
            OB1 = bes.enter_context(nc.sbuf_tensor("OB1", [128, 4096], BF16))
            OB2 = bes.enter_context(nc.sbuf_tensor("OB2", [128, 4096], BF16))
            M_bf = bes.enter_context(nc.sbuf_tensor("M_bf", [16, 8192], BF16))
            G1b = G1.bitcast(BF16)
            for gh in range(2):
                d4 = [[128, 32], [16, 8], [1, 16]]
                bbv = [[16, 32], [0, 8], [1, 16]]
                prv = [[8, 32], [1, 8], [0, 16]]
                TT('dve', AP(G1, 0, 128, 0, d4), AP(bb_st, 0, 128, gh * 512, bbv), AP(PRr, 0, 128, gh * 256, prv), ALU.mult, ['bb_st', 'PRr'], ['G1'])
                TT('dve', AP(G2, 0, 128, 0, d4), AP(bb_sw, 0, 128, gh * 512, bbv), AP(PIr, 0, 128, gh * 256, prv), ALU.mult, ['bb_sw', 'PRr'], ['G2'])
                STT('dve', AP(G1, 0, 128, 0, [[1, 4096]]), AP(G2, 0, 128, 0, [[1, 4096]]), AP(SGNI, 0, 128, 0, [[1, 1]]),
                    AP(G1, 0, 128, 0, [[1, 4096]]), ALU.mult, ALU.add, ['G1', 'G2', 'SGNI'], ['G1'])
                if stop == 'p3a1':
                    return finish()
                for gl in range(32):
                    bi = (gl // 4) % 2
                    bk = BK[bi]
                    TR(AP(bk, 0, 128, (gl % 4) * 128, [[1, 128]]), AP(G1, 0, 128, gl * 128, [[1, 128]]), AP(ident_f, 0, 128, 0, [[1, 128]]),
                       ['G1', 'ident_f'], ['B%d' % bi])
                    KV = os.environ.get('KVAR', 'abc')
                    if gl % 4 == 3:
                        g0 = gl - 3
                        if 'a' in KV:
                            CP('dve', AP(OB1, 0, 128, g0 * 128, [[1, 512]]), AP(bk, 0, 128, 0, [[1, 512]]), ['B%d' % bi], ['OB1'])
                        if 'b' in KV:
                            CP('dve', AP(OB2, 0, 128, g0 * 128, [[128, 4], [1, 64]]), AP(bk, 0, 128, 64, [[128, 4], [1, 64]]), ['B%d' % bi], ['OB2'])
                        if 'c' in KV:
                            TS('dve', AP(OB2, 0, 128, g0 * 128 + 64, [[128, 4], [1, 64]]), AP(bk, 0, 128, 0, [[128, 4], [1, 64]]), -1.0, ALU.mult,
                               ['B%d' % bi], ['OB2'])
                if stop == 'p3a2':
                    return finish()
                for j in range(2):
                    q = 2 * gh + j
                    P.dma('sp', DR(SA_d, q * 128 * 4096, [4096, 128], [1, 2048]), AP(OB1, 0, 128, j * 2048, [[1, 2048]]), scw(), ['OB1'], ['SA'])
                    P.dma('sp', DR(SA_d, q * 128 * 4096 + 2048, [4096, 128], [1, 2048]), AP(OB2, 0, 128, j * 2048, [[1, 2048]]), scw(), ['OB2'], ['SA'])
                if stop == 'p3a':
                    return finish()
                c4 = [[144, 32], [16, 9], [1, 16]]
                cv = [[16, 32], [0, 9], [1, 16]]
                pv = [[9, 32], [1, 9], [0, 16]]
                TT('dve', AP(G1, 0, 128, 0, c4), AP(C_st, 0, 128, gh * 512, cv), AP(PWc, 0, 128, gh * 288, pv), ALU.mult, ['C_st', 'PW'], ['G1'])
                TT('dve', AP(G2, 0, 128, 0, c4), AP(C_sw, 0, 128, gh * 512, cv), AP(PWs, 0, 128, gh * 288, pv), ALU.mult, ['C_sw', 'PW'], ['G2'])
                STT('dve', AP(G1, 0, 128, 0, [[1, 4608]]), AP(G2, 0, 128, 0, [[1, 4608]]), AP(SGNI, 0, 128, 0, [[1, 1]]),
                    AP(G1, 0, 128, 0, [[1, 4608]]), ALU.mult, ALU.add, ['G1', 'G2', 'SGNI'], ['G1'])
                TT('dve', AP(G3, 0, 128, 0, c4), AP(C_sw, 0, 128, gh * 512, cv), AP(PWc, 0, 128, gh * 288, pv), ALU.mult, ['C_sw', 'PW'], ['G3'])
                TT('dve', AP(G2, 0, 128, 0, c4), AP(C_st, 0, 128, gh * 512, cv), AP(PWs, 0, 128, gh * 288, pv), ALU.mult, ['C_st', 'PW'], ['G2'])
                STT('dve', AP(G3, 0, 128, 0, [[1, 4608]]), AP(G2, 0, 128, 0, [[1, 4608]]), AP(SGN1, 0, 128, 0, [[1, 1]]),
                    AP(G3, 0, 128, 0, [[1, 4608]]), ALU.mult, ALU.add, ['G3', 'G2', 'SGN1'], ['G3'])
                TS('dve', AP(OB1, 0, 128, 0, [[128, 32], [1, 128]]), AP(G1, 0, 128, 16, [[144, 32], [1, 128]]), AP(SGN1, 0, 128, 0, [[1, 1]]), ALU.mult,
                   ['G1', 'SGN1'], ['OB1'])
                TS('dve', AP(OB2, 0, 128, 0, [[128, 32], [1, 128]]), AP(G3, 0, 128, 16, [[144, 32], [1, 128]]), -1.0, ALU.mult, ['G3'], ['OB2'])
                for j in range(2):
                    q = 2 * gh + j
                    P.dma('sp', DR(SB_d, q * 128 * 4096 + 2048, [4096, 128], [1, 2048]), AP(OB1, 0, 128, j * 2048, [[1, 2048]]), scw(), ['OB1'], ['SB'])
                    P.dma('sp', DR(SC_d, q * 128 * 2048, [2048, 128], [1, 2048]), AP(OB2, 0, 128, j * 2048, [[1, 2048]]), scw(), ['OB2'], ['SC'])
                if stop == 'p3b':
                    return finish()
                for gl in range(32):
                    g = gh * 32 + gl
                    bi = 2 + (gl // 4) % 2
                    bk = BK[bi]
                    MM(AP(bk, 0, 16, (gl % 4) * 128, [[1, 128]]), AP(bb_sg, 0, 128, g * 16, [[1, 16]]), AP(G1, 0, 128, gl * 144, [[1, 128]]), True, True,
                       ['bb_sg', 'G1'], ['B%d' % bi])
                    if gl % 4 == 3:
                        g0 = gl - 3
                        CP('act', AP(G2, 0, 16, g0 * 128, [[1, 512]]), AP(bk, 0, 16, 0, [[1, 512]]), ['B%d' % bi], ['G2'])
                TT('dve', AP(G2, 0, 16, 0, [[128, 32], [1, 16]]), AP(G2, 0, 16, 0, [[128, 32], [1, 16]]), AP(DTt, 0, 16, gh * 512, [[16, 32], [1, 16]]), ALU.add,
                   ['G2', 'DTt'], ['G2'])
                CP('dve', AP(M_bf, 0, 16, gh * 4096, [[1, 4096]]), AP(G2, 0, 16, 0, [[1, 4096]]), ['G2'], ['M_bf'])
            if stop == 'p3c':
                return finish()
            MS('dve', AP(G1b, 0, 128, 0, [[1, 8192]]), 0.0, ['G1'])
            for s in range(8):
                n = (8 - s) * 16
                P.dma('sp', AP(G1b, 16 * s, 16, s * 16, [[128, 64], [1, n]]), AP(M_bf, 0, 16, 0, [[128, 64], [1, n]]), 'kpl', ['M_bf'], ['G1'])
            for q in range(4):
                P.dma('sp', DR(SB_d, q * 128 * 4096, [4096, 128], [1, 2048]), AP(G1b, 0, 128, q * 2048, [[1, 2048]]), scw(), ['G1'], ['SB'])
            P.barrier()
            P.emit()
    if stop == 'p4':
        return finish()
    Ecur = sb("Ecur", [128, 2048], BF16)
    Eprev = sb("Eprev", [128, 2048], BF16)
    wfin = sb("wfin", [128, 1024], F32)
    P.dma('sp', AP(wfin, 0, 128, 0, [[1, 1024]]), DR(fnw_d, 0, [0, 128], [1, 1024]), 'pl', (), ['wfin'])
    with ExitStack() as ees:
        di = ees.enter_context(nc.sbuf_tensor("di", [128, 128], I32))
        df = ees.enter_context(nc.sbuf_tensor("df", [128, 128], F32))
        dc = ees.enter_context(nc.sbuf_tensor("dc", [128, 128], F32))
        dp = ees.enter_context(nc.sbuf_tensor("dp", [128, 128], F32))
        Ef = ees.enter_context(nc.sbuf_tensor("Ef", [128, 2048], F32))
        P.op('pool', lambda e: e.iota(AP(di, 0, 128, 0, [[1, 128]]), [[1, 128]], base=0, channel_multiplier=-1), (), ['di'])
        CP('dve', AP(df, 0, 128, 0, [[1, 128]]), AP(di, 0, 128, 0, [[1, 128]]), ['di'], ['df'])
        TS('dve', AP(dc, 0, 128, 0, [[1, 128]]), AP(df, 0, 128, 0, [[1, 128]]), 0.0, ALU.max, ['df'], ['dc'])
        TS('dve', AP(dp, 0, 128, 0, [[1, 128]]), AP(df, 0, 128, 0, [[1, 128]]), 128.0, ALU.add, ['df'], ['dp'])
        TS('dve', AP(dp, 0, 128, 0, [[1, 128]]), AP(dp, 0, 128, 0, [[1, 128]]), 0.0, ALU.max, ['dp'], ['dp'])
        for (src, dstE, pat, base, cm) in ((dc, Ecur, [[0, 16], [1, 128]], 0, -1), (dp, Eprev, [[0, 16], [-1, 128]], -1, 1)):
            for h in range(16):
                slope = float(2.0 ** (-(h + 1) / 2.0))
                ACT(AP(Ef, 0, 128, h * 128, [[1, 128]]), AP(src, 0, 128, 0, [[1, 128]]), AF.Exp, [src.name], ['Ef'], scale=-slope)
            P.op('pool', lambda e, pat=pat, base=base, cm=cm: e.affine_select(
                out=AP(Ef, 0, 128, 0, [[128, 16], [1, 128]]), in_=AP(Ef, 0, 128, 0, [[128, 16], [1, 128]]),
                pattern=pat, compare_op=ALU.is_ge, fill=0.0, base=base, channel_multiplier=cm), ['Ef'], ['Ef'])
            CP('dve', AP(dstE, 0, 128, 0, [[1, 2048]]), AP(Ef, 0, 128, 0, [[1, 2048]]), ['Ef'], [dstE.name])
        P.barrier()
        P.emit()

    if stop == 'p5':
        return finish()
    H8 = sb("H8", [128, 8192], F32)
    Hn8 = sb("Hn8", [128, 8192], BF16)
    hnT = sb("hnT", [128, 8192], BF16)
    RING = sb("RING", [128, NSLOT * SLOTE], BF16)
    RA = sb("RA", [128, 8192], F32)
    RAb = RA.bitcast(BF16)
    TP = sb("TP", [128, 2048], F32)
    PTB = sb("PTB", [128, 3072], BF16)
    Otok = sb("Otok", [128, 1024], BF16)
    QT0, KT0, VA0 = 0, 8192, 12800

    specs = []

    def mlp_specs(l):
        U = lambda qd: [('up', l, qd, 0), ('up', l, qd, 1)]
        Dn = lambda qd: [('dn', l, qd, 0), ('dn', l, qd, 1)]
        return U(0) + U(1) + Dn(0) + U(2) + Dn(1) + U(3) + Dn(2) + Dn(3)

    def tile_specs(meta):
        s = [('wkv',), ('wq', 0), ('wq', 1)]
        if not meta:
            s += [('wq', 0), ('wq', 1)]
        s += [('wo', 0), ('wo', 1)] + mlp_specs(0)
        if meta:
            s += [('A', q) for q in range(4)]
        elif debug_stage != 1:
            s += [('A', 0), ('A', 1), ('B', 0), ('C', 0), ('A', 2), ('B', 1), ('C', 1), ('A', 3), ('B', 2), ('C', 2), ('B', 3), ('C', 3)]
            s += [('glu', 0, 0), ('glu', 1, 0), ('glu', 0, 1), ('glu', 1, 1)] + mlp_specs(1)
        return s
    specs = tile_specs(True)
    ntiles = 8 if debug_stage != 9 else 1
    for _ in range(ntiles):
        specs += tile_specs(False)
    ring = {'next': 0, 'issued': 0, 'done': [False] * len(specs)}

    def issue(i):
        sp = specs[i]
        sl = i % NSLOT
        key = 'R%d' % sl
        base = sl * SLOTE
        kind = sp[0]
        if kind == 'wq':
            src = DR(wqkv_d, sp[1] * 512, [1536, 128], [128 * 1536, 8], [1, 512]); dims = [[512, 8], [1, 512]]; q = 'pool'
        elif kind == 'wkv':
            src = DR(wqkv_d, 1024, [1536, 128], [128 * 1536, 8], [1, 512]); dims = [[512, 8], [1, 512]]; q = 'pool'
        elif kind == 'wo':
            src = DR(wo_d, sp[1] * 512, [1024, 128], [128 * 1024, 8], [1, 512]); dims = [[512, 8], [1, 512]]; q = 'pool'
        elif kind == 'up':
            _, l, qd, j = sp
            src = DR(wup_d, l * D * 4096 + qd * 1024 + j * 512, [4096, 128], [128 * 4096, 8], [1, 512]); dims = [[512, 8], [1, 512]]; q = 'pool'
        elif kind == 'dn':
            _, l, qd, j = sp
            src = DR(wdn_d, l * 4096 * D + (qd * 1024 + j * 512) * D, [1024, 128], [128 * 1024, 4], [1, 1024]); dims = [[1024, 4], [1, 1024]]; q = 'pool'
        elif kind == 'glu':
            _, part, j = sp
            src = DR(wglu_d, part * 1024 + j * 512, [2048, 128], [128 * 2048, 8], [1, 512]); dims = [[512, 8], [1, 512]]; q = 'pool'
        elif kind == 'A':
            src = DR(SA_d, sp[1] * 128 * 4096, [4096, 128], [1, 4096]); dims = [[1, 4096]]; q = 'sp'
        elif kind == 'B':
            src = DR(SB_d, sp[1] * 128 * 4096, [4096, 128], [1, 4096]); dims = [[1, 4096]]; q = 'sp'
        elif kind == 'C':
            src = DR(SC_d, sp[1] * 128 * 2048, [2048, 128], [1, 2048]); dims = [[1, 2048]]; q = 'sp'
        P.dma(q, AP(RING, 0, 128, base, dims), src, key, (), [key])

    def pump():
        while ring['issued'] < len(specs) and ring['issued'] < ring['next'] + NSLOT:
            i = ring['issued']
            if i >= NSLOT and not ring['done'][i - NSLOT]:
                break
            issue(i)
            ring['issued'] += 1

    def rget(*sp):
        i = ring['next']
        assert specs[i] == tuple(sp), (i, specs[i], sp)
        ring['next'] += 1
        pump()
        assert ring['issued'] > i
        return i

    def rdone(i):
        ring['done'][i] = True
        pump()

    def rk(i):
        return 'R%d' % (i % NSLOT)

    def rb(i):
        return (i % NSLOT) * SLOTE

    ctr = {'t': 0, 'pt': 0, 'ev': 0, 'bk': 0}

    def tmp():
        i = ctr['t'] % 4
        ctr['t'] += 1
        return i

    def evq():
        ctr['ev'] += 1
        return 'act' if ctr['ev'] % 2 == 0 else 'dve'

    HK = ['H8.%d' % s for s in range(8)]
    HNK = ['Hn8.%d' % s for s in range(8)]
    TK = ['hnT.%d' % k for k in range(8)]

    def rmsnorm(nt, order):
        P.claim('Hn8', HNK)
        SSK = ['ss.%d' % s for s in range(8)]
        MS('dve', AP(ss_t, 0, nt, 0, [[1, 8]]), 0.0, SSK)
        for s in range(8):
            ACT(AP(junk, 0, nt, 0, [[1, 1024]]), AP(H8, 0, nt, s * 1024, [[1, 1024]]), AF.Square, [HK[s], SSK[s]], [SSK[s]],
                accum_out=AP(ss_t, 0, nt, s, [[1, 1]]))
        ACT(AP(rstd_t, 0, nt, 0, [[1, 8]]), AP(ss_t, 0, nt, 0, [[1, 8]]), AF.Sqrt, SSK + ['epsc'], ['rstd'],
            bias=AP(epsc, 0, nt, 0, [[1, 1]]), scale=float(1.0 / D))
        RCP(AP(rstd_t, 0, nt, 0, [[1, 8]]), AP(rstd_t, 0, nt, 0, [[1, 8]]), ['rstd'], ['rstd'])
        for s in range(8):
            if order == 'std':
                o = AP(Hn8, 0, nt, s * 1024, [[1, 1024]])
                i = AP(H8, 0, nt, s * 1024, [[1, 1024]])
            else:
                o = AP(Hn8, 0, nt, s * 16, [[128, 64], [1, 16]])
                i = AP(H8, 0, nt, s * 1024, [[16, 64], [1, 16]])
            if s % 2 == 0:
                TS('dve', o, i, AP(rstd_t, 0, nt, s, [[1, 1]]), ALU.mult, [HK[s], 'rstd'], [HNK[s]])
            else:
                ACT(o, i, AF.Copy, [HK[s], 'rstd'], [HNK[s]], scale=AP(rstd_t, 0, nt, s, [[1, 1]]))

    def to_featmajor(nt, wj, srckeys):
        P.claim('hnT', TK)
        for kc in range(8):
            bi = kc % 2
            for s in range(8):
                TR(AP(BKb[bi], 0, 128, s * 128, [[1, nt]]), AP(Hn8, 0, nt, s * 1024 + kc * 128, [[1, 128]]), AP(ident_b, 0, nt, 0, [[1, nt]]),
                   srckeys + ['ident_b'], ['B%d' % bi], ms=(s == 7))
            o = AP(hnT, 0, 128, kc * 1024, [[1, 8], [8, nt]])
            i = AP(BKb[bi], 0, 128, 0, [[128, 8], [1, nt]])
            if wj is None:
                CP(evq(), o, i, ['B%d' % bi], [TK[kc]])
            else:
                e = evq()
                wc = AP(wfm, 0, 128, wj * 8 + kc, [[1, 1]])
                if e == 'dve':
                    TS('dve', o, i, wc, ALU.mult, ['B%d' % bi, 'wfm'], [TK[kc]])
                else:
                    ACT(o, i, AF.Copy, ['B%d' % bi, 'wfm'], [TK[kc]], scale=wc)

    def resid_add(nt, s, half, bi):
        hs = AP(H8, 0, nt, s * 1024 + half * 512, [[1, 512]])
        TT('dve', hs, AP(BK[bi], 0, nt, 0, [[1, 512]]), hs, ALU.add, ['B%d' % bi, HK[s]], [HK[s]])

    def attention(nt, meta, first):
        ntok = nt * 8
        nblk = max(1, ntok // 128)
        QB = min(128, ntok)
        nth = max(1, ntok // 512)
        tn = min(512, ntok)
        P.claim('RA', ['qT', 'kT', 'Vatt'])
        P.claim('Hn8', ['OT'])
        MS('dve', AP(RAb, 0, 128, VA0, [[1, 2340]]), 1.0, ['Vatt'])
        if not meta and not first:
            CP('dve', AP(RAb, 0, 128, VA0, [[1, 260]]), AP(Vp, 0, 128, 0, [[1, 260]]), ['Vp'], ['Vatt'])
            CP('dve', AP(RAb, 0, 64, KT0, [[1152, 4], [1, 128]]), AP(kTp, 0, 64, 0, [[128, 4], [1, 128]]), ['kTp'], ['kT'])
        iw = rget('wkv')
        for kv in range(4):
            for th in range(nth):
                bi = 2 + (ctr['bk'] % 2); ctr['bk'] += 1
                for kc in range(8):
                    MM(AP(BK[bi], 0, 64, 0, [[1, tn]]), AP(RING, 0, 128, rb(iw) + kc * 512 + kv * 64, [[1, 64]]),
                       AP(hnT, 0, 128, kc * 1024 + th * 512, [[1, tn]]), kc == 0, kc == 7, [rk(iw), TK[kc]], ['B%d' % bi])
                CP(evq(), AP(RAb, 0, 64, KT0 + kv * 1152 + 128 + th * 512, [[1, tn]]), AP(BK[bi], 0, 64, 0, [[1, tn]]), ['B%d' % bi], ['kT'])
        for blk in range(nblk):
            bi = 2 + (ctr['bk'] % 2); ctr['bk'] += 1
            for kc in range(8):
                MM(AP(BK[bi], 0, QB, 0, [[1, 256]]), AP(hnT, 0, 128, kc * 1024 + blk * 128, [[1, QB]]),
                   AP(RING, 0, 128, rb(iw) + kc * 512 + 256, [[1, 256]]), kc == 0, kc == 7, [rk(iw), TK[kc]], ['B%d' % bi])
            CP(evq(), AP(RAb, 0, QB, VA0 + (blk + 1) * 260, [[65, 4], [1, 64]]), AP(BK[bi], 0, QB, 0, [[64, 4], [1, 64]]), ['B%d' % bi], ['Vatt'])
        rdone(iw)
        if meta:
            CP('dve', AP(kTm, 0, 64, 0, [[16, 4], [1, 16]]), AP(RAb, 0, 64, KT0 + 128, [[1152, 4], [1, 16]]), ['kT'], ['kTm'])
            CP('dve', AP(Vm, 0, 16, 0, [[1, 260]]), AP(RAb, 0, 16, VA0 + 260, [[1, 260]]), ['Vatt'], ['Vm'])
            MS('dve', AP(RAb, 0, 64, QT0, [[1, 8192]]), 0.0, ['qT'])
        nhalf = 1 if meta else 2
        for hf in range(nhalf):
            bl0 = hf * 4
            nb_h = min(4, nblk)
            iq = [rget('wq', 0), rget('wq', 1)]
            tnq = min(512, ntok)
            for h in range(16):
                bi = 2 + (ctr['bk'] % 2); ctr['bk'] += 1
                ip = iq[h // 8]
                for kc in range(8):
                    MM(AP(BK[bi], 0, 64, 0, [[1, tnq]]), AP(RING, 0, 128, rb(ip) + kc * 512 + (h % 8) * 64, [[1, 64]]),
                       AP(hnT, 0, 128, kc * 1024 + hf * 512, [[1, tnq]]), kc == 0, kc == 7, [rk(ip), TK[kc]], ['B%d' % bi])
                if meta:
                    CP(evq(), AP(RAb, 0, 64, QT0 + h * 128, [[1, tnq]]), AP(BK[bi], 0, 64, 0, [[1, tnq]]), ['B%d' % bi], ['qT'])
                else:
                    CP(evq(), AP(RAb, 0, 64, QT0 + h * 128, [[2048, 4], [1, 128]]), AP(BK[bi], 0, 64, 0, [[128, 4], [1, 128]]), ['B%d' % bi], ['qT'])
            rdone(iq[0]); rdone(iq[1])
            SETS = ((4, 5, 6), (2, 3, 1))
            its = [(bl, kv) for bl in range(nb_h) for kv in range(4)]

            def hp(bl):
                return (not meta) and ((bl0 + bl) > 0 or not first)

            def scores(i):
                bl, kv = its[i]
                b = bl0 + bl
                bc, bp, bm = SETS[i % 2]
                rq = AP(RAb, 0, 64, QT0 + bl * 2048 + kv * 512, [[1, 512]])
                if not meta:
                    MM(AP(BK[bc], 0, 128, 0, [[1, 512]]), AP(RAb, 0, 64, KT0 + kv * 1152 + 128 + b * 128, [[1, 128]]), rq, True, True, ['kT', 'qT'], ['B%d' % bc], ms=False)
                    if hp(bl):
                        MM(AP(BK[bp], 0, 128, 0, [[1, 512]]), AP(RAb, 0, 64, KT0 + kv * 1152 + b * 128, [[1, 128]]), rq, True, True, ['kT', 'qT'], ['B%d' % bp], ms=False)
                MM(AP(BK[bm], 0, 16, 0, [[1, 512]]), AP(kTm, 0, 64, kv * 16, [[1, 16]]), rq, True, True, ['kTm', 'qT'], ['B%d' % bm], ms=True)

            def softmax_pv(i):
                bl, kv = its[i]
                b = bl0 + bl
                bc, bp, bm = SETS[i % 2]
                has_prev = hp(bl)
                pset = i % 2
                PTc = pset * 1536
                PTp = PTc + 512
                PTm = PTc + 1024
                kc_, kp_, km_ = 'PTc%d' % pset, 'PTp%d' % pset, 'PTm%d' % pset
                if not meta:
                    t = tmp()
                    ACT(AP(TP, 0, 128, t * 512, [[1, 512]]), AP(BK[bc], 0, 128, 0, [[1, 512]]), AF.Exp, ['B%d' % bc], ['T%d' % t], scale=0.125)
                    TT('dve', AP(PTB, 0, 128, PTc, [[1, 512]]), AP(TP, 0, 128, t * 512, [[1, 512]]), AP(Ecur, 0, 128, kv * 512, [[1, 512]]), ALU.mult,
                       ['T%d' % t, 'Ecur'], [kc_])
                    if has_prev:
                        t = tmp()
                        ACT(AP(TP, 0, 128, t * 512, [[1, 512]]), AP(BK[bp], 0, 128, 0, [[1, 512]]), AF.Exp, ['B%d' % bp], ['T%d' % t], scale=0.125)
                        TT('dve', AP(PTB, 0, 128, PTp, [[1, 512]]), AP(TP, 0, 128, t * 512, [[1, 512]]), AP(Eprev, 0, 128, kv * 512, [[1, 512]]), ALU.mult,
                           ['T%d' % t, 'Eprev'], [kp_])
                    ACT(AP(PTB, 0, 16, PTm, [[1, 512]]), AP(BK[bm], 0, 16, 0, [[1, 512]]), AF.Exp, ['B%d' % bm], [km_], scale=0.125)
                else:
                    t = tmp()
                    ACT(AP(TP, 0, 16, t * 512, [[1, 512]]), AP(BK[bm], 0, 16, 0, [[1, 512]]), AF.Exp, ['B%d' % bm], ['T%d' % t], scale=0.125)
                    TT('dve', AP(PTB, 0, 16, PTm, [[128, 4], [1, 128]]), AP(TP, 0, 16, t * 512, [[128, 4], [1, 128]]),
                       AP(maskc, 0, 16, 0, [[0, 4], [1, 128]]), ALU.mult, ['T%d' % t, 'maskc'], [km_])
                for hl in range(4):
                    lst = []
                    if not meta:
                        lst.append((AP(PTB, 0, 128, PTc + hl * 128, [[1, QB]]), AP(RAb, 0, 128, VA0 + (b + 1) * 260 + kv * 65, [[1, 65]]), [kc_, 'Vatt']))
                        if has_prev:
                            lst.append((AP(PTB, 0, 128, PTp + hl * 128, [[1, QB]]), AP(RAb, 0, 128, VA0 + b * 260 + kv * 65, [[1, 65]]), [kp_, 'Vatt']))
                    lst.append((AP(PTB, 0, 16, PTm + hl * 128, [[1, QB]]), AP(Vm, 0, 16, kv * 65, [[1, 65]]), [km_, 'Vm']))
                    for ii, (l_, r_, ks) in enumerate(lst):
                        MM(AP(BK[7], 0, QB, hl * 128, [[1, 65]]), l_, r_, ii == 0, ii == len(lst) - 1, ks, ['B7'],
                           ms=(hl == 3 and ii == len(lst) - 1))
                dk = 'den%d' % (kv % 2)
                dn = AP(den_t, 0, QB, (kv % 2) * 4, [[1, 4]])
                rd = AP(rden_t, 0, QB, (kv % 2) * 4, [[1, 4]])
                TT('dve', dn, AP(BK[7], 0, QB, 64, [[128, 4]]), AP(expsink, 0, QB, kv * 4, [[1, 4]]), ALU.add, ['B7', 'expsink'], [dk])
                RCP(rd, dn, [dk], [dk])
                TT('dve', AP(Otok, 0, QB, kv * 256, [[64, 4], [1, 64]]), AP(BK[7], 0, QB, 0, [[128, 4], [1, 64]]),
                   AP(rden_t, 0, QB, (kv % 2) * 4, [[1, 4], [0, 64]]), ALU.mult, ['B7', dk], ['Otok'])
                if kv == 3:
                    for kc in range(8):
                        TR(AP(BKb[0], 0, 128, kc * 128, [[1, QB]]), AP(Otok, 0, QB, kc * 128, [[1, 128]]), AP(ident_b, 0, QB, 0, [[1, QB]]),
                           ['Otok', 'ident_b'], ['B0'], ms=(kc == 7))
                    CP(evq(), AP(Hn8, 0, 128, b * 128, [[1024, 8], [1, QB]]), AP(BKb[0], 0, 128, 0, [[128, 8], [1, QB]]), ['B0'], ['OT'])

            scores(0)
            for i in range(len(its)):
                if i + 1 < len(its):
                    scores(i + 1)
                softmax_pv(i)
        if not meta:
            CP('dve', AP(kTp, 0, 64, 0, [[128, 4], [1, 128]]), AP(RAb, 0, 64, KT0 + 1024, [[1152, 4], [1, 128]]), ['kT'], ['kTp'])
            CP('dve', AP(Vp, 0, 128, 0, [[1, 260]]), AP(RAb, 0, 128, VA0 + 8 * 260, [[1, 260]]), ['Vatt'], ['Vp'])
        for half in range(2):
            io = rget('wo', half)
            for s in range(8):
                bi = 2 + (ctr['bk'] % 2); ctr['bk'] += 1
                for kc in range(8):
                    MM(AP(BK[bi], 0, nt, 0, [[1, 512]]), AP(Hn8, 0, 128, kc * 1024 + s, [[8, nt]]), AP(RING, 0, 128, rb(io) + kc * 512, [[1, 512]]),
                       kc == 0, kc == 7, ['OT', rk(io)], ['B%d' % bi])
                resid_add(nt, s, half, bi)
            rdone(io)

    def mlp(nt, l):
        ntok = nt * 8
        nth = max(1, ntok // 512)
        tn = min(512, ntok)
        aTk = [[['aT%d.%d.%d' % (b_, fc, th) for th in range(2)] for fc in range(8)] for b_ in range(2)]
        P.claim('RA', [k for a in aTk for bb_ in a for k in bb_])

        def up(qd):
            buf = qd % 2
            for j in range(2):
                iu = rget('up', l, qd, j)
                for fcl in range(4):
                    fc = j * 4 + fcl
                    for th in range(nth):
                        bi = 2 + (ctr['bk'] % 2); ctr['bk'] += 1
                        for kc in range(8):
                            MM(AP(BK[bi], 0, 128, 0, [[1, tn]]), AP(RING, 0, 128, rb(iu) + kc * 512 + fcl * 128, [[1, 128]]),
                               AP(hnT, 0, 128, kc * 1024 + th * 512, [[1, tn]]), kc == 0, kc == 7, [rk(iu), TK[kc]], ['B%d' % bi])
                        t = tmp()
                        tv = AP(TP, 0, 128, t * 512, [[1, tn]])
                        ACT(tv, AP(BK[bi], 0, 128, 0, [[1, tn]]), AF.Relu, ['B%d' % bi], ['T%d' % t])
                        TT('dve', AP(RAb, 0, 128, buf * 8192 + fc * 1024 + th * 512, [[1, tn]]), tv, tv, ALU.mult, ['T%d' % t], [aTk[buf][fc][th]])
                rdone(iu)

        def down(qd):
            buf = qd % 2
            idn = [rget('dn', l, qd, 0), rget('dn', l, qd, 1)]
            rkeys = [k for fc in range(8) for k in aTk[buf][fc][:nth]]
            for s in range(8):
                for half in range(2):
                    bi = 4 + (ctr['bk'] % 2); ctr['bk'] += 1
                    for fc in range(8):
                        ii = idn[fc // 4]
                        MM(AP(BK[bi], 0, nt, 0, [[1, 512]]), AP(RAb, 0, 128, buf * 8192 + fc * 1024 + s, [[8, nt]]),
                           AP(RING, 0, 128, rb(ii) + (fc % 4) * 1024 + half * 512, [[1, 512]]), fc == 0, fc == 7, rkeys + [rk(ii)], ['B%d' % bi])
                    resid_add(nt, s, half, bi)
            rdone(idn[0]); rdone(idn[1])
        up(0)
        up(1)
        down(0)
        up(2)
        down(1)
        up(3)
        down(2)
        down(3)

    def ssm(nt, meta):
        UK = ['U.%d' % i for i in range(8)]
        P.claim('hnT', UK)
        SKEYS = ['Wb0', 'Vb0', 'Vb1'] + ['P1b%d' % i for i in range(4)] + ['P2b%d' % i for i in range(4)]
        P.claim('RA', SKEYS)
        P.claim('Hn8', HNK)
        for g8 in range(8):
            bi = g8 % 2
            for gl in range(8):
                g = g8 * 8 + gl
                TR(AP(BKb[bi], 0, 128, gl * 128, [[1, nt]]), AP(Hn8, 0, nt, g * 128, [[1, 128]]), AP(ident_b, 0, nt, 0, [[1, nt]]), HNK + ['ident_b'], ['B%d' % bi], ms=(gl == 7))
            CP(evq(), AP(hnT, 0, 128, g8 * 1024, [[128, 8], [1, nt]]), AP(BKb[bi], 0, 128, 0, [[128, 8], [1, nt]]), ['B%d' % bi], [UK[g8]])
        if not meta:
            P.claim('Hn8', ['Y8.%d' % i for i in range(8)])
        WB = [0, 0]
        VB = [1024, 2056]
        P1B = [6176 + i * 1024 for i in range(4)]
        P2B = [6176 + 4096 + i * 1024 for i in range(4)]
        pend = {}

        def stage1(q, ia):
            for bb_ in range(2):
                gb = 2 * q + bb_
                i2 = gb % 2
                g0 = gb * 8
                for gl in range(8):
                    gq = bb_ * 8 + gl
                    b1, b2 = 4 + gl // 4, 6 + gl // 4
                    rhs = AP(hnT, 0, 128, (g0 + gl) * 128, [[1, nt]])
                    MM(AP(BK[b1], 0, 128, (gl % 4) * 128, [[1, nt]]), AP(RING, 0, 128, rb(ia) + gq * 128, [[1, 128]]), rhs, True, True, [rk(ia), UK[gb]], ['B%d' % b1], ms=False)
                    MM(AP(BK[b2], 0, 128, (gl % 4) * 128, [[1, nt]]), AP(RING, 0, 128, rb(ia) + 2048 + gq * 128, [[1, 128]]), rhs, True, True, [rk(ia), UK[gb]], ['B%d' % b2], ms=(gl == 7))
                i4 = gb % 4
                wk, vk, p1k, p2k = 'Wb0', 'Vb%d' % i2, 'P1b%d' % i4, 'P2b%d' % i4
                for hb in range(2):
                    b1, b2 = 4 + hb, 6 + hb
                    d3 = [[128, 4], [1, nt]]
                    cosv = AP(COSt, 0, 128, (g0 + hb * 4) * 129 + 1, [[129, 4], [1, nt]])
                    sinv = AP(SINt, 0, 128, (g0 + hb * 4) * 129 + 1, [[129, 4], [1, nt]])
                    wv = AP(RA, 0, 128, WB[i2] + hb * 512, d3)
                    t = tmp()
                    tv = AP(TP, 0, 128, t * 512, d3)
                    TT('dve', wv, AP(BK[b1], 0, 128, 0, d3), cosv, ALU.mult, ['B%d' % b1, 'COSt'], [wk])
                    TT('dve', tv, AP(BK[b2], 0, 128, 0, d3), sinv, ALU.mult, ['B%d' % b2, 'SINt'], ['T%d' % t])
                    TT('dve', wv, wv, tv, ALU.add, [wk, 'T%d' % t], [wk])
                CP('dve', AP(RA, 0, 128, VB[i2], [[129, 8]]), AP(Xcar, 0, 128, g0, [[1, 8]]), ['Xcar.%d' % gb], [vk])
                for gl in range(8):
                    P.op('dve', lambda e, gl=gl, i2=i2, g0=g0: e.tensor_tensor_scan(
                        out=AP(RA, 0, 128, VB[i2] + gl * 129 + 1, [[1, nt]]), data0=AP(R8d, 0, 128, g0 + gl, [[0, nt]]),
                        data1=AP(RA, 0, 128, WB[i2] + gl * 128, [[1, nt]]), initial=AP(RA, 0, 128, VB[i2] + gl * 129, [[1, 1]]),
                        op0=ALU.mult, op1=ALU.add), [wk, vk, 'R8d'], [vk])
                if not meta:
                    vv = AP(RA, 0, 128, VB[i2], [[129, 8], [1, nt]])
                    TT('dve', AP(RAb, 0, 128, P1B[i4], [[128, 8], [1, nt]]), vv, AP(COSt, 0, 128, g0 * 129, [[129, 8], [1, nt]]), ALU.mult, [vk, 'COSt'], [p1k])
                    TT('dve', AP(RAb, 0, 128, P2B[i4], [[128, 8], [1, nt]]), vv, AP(SINt, 0, 128, g0 * 129, [[129, 8], [1, nt]]), ALU.mult, [vk, 'SINt'], [p2k])
                vl = lambda p0: AP(RA, p0, 64, VB[i2] + nt, [[129, 8]])
                sk = 'vsw%d' % i2
                P.dma('sp', AP(vsw_t, 64, 64, i2 * 8, [[1, 8]]), vl(0), sk, [vk], [sk], allow_slow_non_contiguous=True)
                P.dma('sp', AP(vsw_t, 0, 64, i2 * 8, [[1, 8]]), vl(64), sk, [vk], [sk], allow_slow_non_contiguous=True)
                pend[gb] = (i2, g0, vk, sk)

        def carry(gb):
            i2, g0, vk, sk = pend.pop(gb)
            c1 = AP(ct_t, 0, 128, i2 * 16, [[1, 8]])
            c2 = AP(ct_t, 0, 128, i2 * 16 + 8, [[1, 8]])
            ck = 'ct%d' % i2
            TT('dve', c1, AP(RA, 0, 128, VB[i2] + nt, [[129, 8]]), AP(COSt, 0, 128, g0 * 129 + nt, [[129, 8]]), ALU.mult, [vk, 'COSt'], [ck])
            TT('dve', c2, AP(vsw_t, 0, 128, i2 * 8, [[1, 8]]), AP(SINt, 0, 128, g0 * 129 + nt, [[129, 8]]), ALU.mult, [sk, 'SINt'], [ck])
            STT('dve', AP(Xcar, 0, 128, g0, [[1, 8]]), c2, AP(SGNI, 0, 128, 0, [[1, 1]]), c1, ALU.mult, ALU.add, [ck, 'SGNI'], ['Xcar.%d' % gb])

        def stage2(q, ib, ic):
            for bb_ in range(2):
                gb = 2 * q + bb_
                i2 = gb % 2
                g0 = gb * 8
                i4 = gb % 4
                p1k, p2k = 'P1b%d' % i4, 'P2b%d' % i4
                for hb in range(2):
                    bi = 2 + (ctr['bk'] % 2); ctr['bk'] += 1
                    for g4 in range(4):
                        gl = hb * 4 + g4
                        gq = bb_ * 8 + gl
                        o = AP(BK[bi], 0, nt, g4 * 128, [[1, 128]])
                        MM(o, AP(hnT, 0, 128, (g0 + gl) * 128, [[1, nt]]), AP(RING, 0, 128, rb(ib) + gq * 128, [[1, 128]]), True, False, ['U.%d' % gb, rk(ib)], ['B%d' % bi])
                        MM(o, AP(RAb, 0, 128, P1B[i4] + gl * 128, [[1, nt]]), AP(RING, 0, 128, rb(ib) + 2048 + gq * 128, [[1, 128]]), False, False, [p1k, rk(ib)], ['B%d' % bi])
                        MM(o, AP(RAb, 0, 128, P2B[i4] + gl * 128, [[1, nt]]), AP(RING, 0, 128, rb(ic) + gq * 128, [[1, 128]]), False, True, [p2k, rk(ic)], ['B%d' % bi], ms=(g4 == 3))
                    ACT(AP(Hn8, 0, nt, (g0 + hb * 4) * 16, [[16, 4], [1024, 8], [1, 16]]), AP(BK[bi], 0, nt, 0, [[128, 4], [16, 8], [1, 16]]), AF.Gelu_apprx_tanh,
                        ['B%d' % bi], ['Y8.%d' % gb])

        if meta:
            ias = []
            for q in range(4):
                ia = rget('A', q)
                stage1(q, ia)
                rdone(ia)
                carry(2 * q)
                carry(2 * q + 1)
            return
        ia = rget('A', 0)
        stage1(0, ia)
        rdone(ia)
        for q in range(4):
            carry(2 * q)
            carry(2 * q + 1)
            if q + 1 < 4:
                ia = rget('A', q + 1)
                stage1(q + 1, ia)
                rdone(ia)
            ib = rget('B', q)
            ic = rget('C', q)
            stage2(q, ib, ic)
            rdone(ib); rdone(ic)
        YK = ['Y8.%d' % i for i in range(8)]
        to_featmajor(nt, None, YK)
        for half in range(2):
            iv = rget('glu', 0, half)
            ig = rget('glu', 1, half)
            for s in range(8):
                bv = 2 + (ctr['bk'] % 2); ctr['bk'] += 1
                bg = 4 + (ctr['bk'] % 2)
                for kc in range(8):
                    lt = AP(hnT, 0, 128, kc * 1024 + s, [[8, nt]])
                    MM(AP(BK[bv], 0, nt, 0, [[1, 512]]), lt, AP(RING, 0, 128, rb(iv) + kc * 512, [[1, 512]]), kc == 0, kc == 7, [TK[kc], rk(iv)], ['B%d' % bv])
                for kc in range(8):
                    lt = AP(hnT, 0, 128, kc * 1024 + s, [[8, nt]])
                    MM(AP(BK[bg], 0, nt, 0, [[1, 512]]), lt, AP(RING, 0, 128, rb(ig) + kc * 512, [[1, 512]]), kc == 0, kc == 7, [TK[kc], rk(ig)], ['B%d' % bg])
                t = tmp()
                tv = AP(TP, 0, nt, t * 512, [[1, 512]])
                ACT(tv, AP(BK[bg], 0, nt, 0, [[1, 512]]), AF.Sigmoid, ['B%d' % bg], ['T%d' % t])
                TT('dve', tv, AP(BK[bv], 0, nt, 0, [[1, 512]]), tv, ALU.mult, ['B%d' % bv, 'T%d' % t], ['T%d' % t])
                hs = AP(H8, 0, nt, s * 1024 + half * 512, [[1, 512]])
                TT('dve', hs, hs, tv, ALU.add, [HK[s], 'T%d' % t], [HK[s]])
            rdone(iv); rdone(ig)

    def layer0(nt, meta, first):
        rmsnorm(nt, 'std')
        to_featmajor(nt, 0, HNK)
        attention(nt, meta, first)
        rmsnorm(nt, 'std')
        to_featmajor(nt, 1, HNK)
        mlp(nt, 0)

    def layer1(nt):
        rmsnorm(nt, 'gsc')
        ssm(nt, False)
        rmsnorm(nt, 'std')
        to_featmajor(nt, 2, HNK)
        mlp(nt, 1)

    MS('dve', AP(Xcar, 0, 128, 0, [[1, 64]]), 0.0, ['Xcar.%d' % i for i in range(8)])
    P.dma('sp', AP(H8, 0, 2, 0, [[1, 8192]]), DR(meta_d, 0, [8192, 2], [1, 8192]), 'xin', (), HK)
    layer0(2, True, True)
    rmsnorm(2, 'gsc')
    ssm(2, True)
    CP('dve', AP(Xmeta, 0, 128, 0, [[1, 64]]), AP(Xcar, 0, 128, 0, [[1, 64]]), ['Xcar.%d' % i for i in range(8)], ['Xmeta'])
    P.emit()

    if stop == 'meta':
        return finish()
    for ti in range(ntiles):
        sq, tl = ti // 4, ti % 4
        first = (tl == 0)
        off = (sq * SEQ + tl * 1024) * D
        for s in range(8):
            P.dma('sp', AP(H8, 0, 128, s * 1024, [[1, 1024]]), DR(x_d, off + s * 1024, [8192, 128], [1, 1024]), 'xin%d' % s, (), [HK[s]])
        if first:
            CP('dve', AP(Xcar, 0, 128, 0, [[1, 64]]), AP(Xmeta, 0, 128, 0, [[1, 64]]), ['Xmeta'], ['Xcar.%d' % i for i in range(8)])
        layer0(128, False, first)
        if debug_stage != 1:
            layer1(128)
            SSK = ['ss.%d' % s for s in range(8)]
            MS('dve', AP(ss_t, 0, 128, 0, [[1, 8]]), 0.0, SSK)
            for s in range(8):
                ACT(AP(junk, 0, 128, 0, [[1, 1024]]), AP(H8, 0, 128, s * 1024, [[1, 1024]]), AF.Square, [HK[s], SSK[s]], [SSK[s]],
                    accum_out=AP(ss_t, 0, 128, s, [[1, 1]]))
            ACT(AP(rstd_t, 0, 128, 0, [[1, 8]]), AP(ss_t, 0, 128, 0, [[1, 8]]), AF.Sqrt, SSK + ['epsc'], ['rstd'],
                bias=AP(epsc, 0, 128, 0, [[1, 1]]), scale=float(1.0 / D))
            RCP(AP(rstd_t, 0, 128, 0, [[1, 8]]), AP(rstd_t, 0, 128, 0, [[1, 8]]), ['rstd'], ['rstd'])
            for s in range(8):
                hs = AP(H8, 0, 128, s * 1024, [[1, 1024]])
                STT('dve', hs, hs, AP(rstd_t, 0, 128, s, [[1, 1]]), AP(wfin, 0, 128, 0, [[1, 1024]]), ALU.mult, ALU.mult, [HK[s], 'rstd', 'wfin'], [HK[s]])
        for s in range(8):
            P.dma('sp', DR(y_d, off + s * 1024, [8192, 128], [1, 1024]), AP(H8, 0, 128, s * 1024, [[1, 1024]]), 'yout%d' % s, [HK[s]], ())
        P.emit()
    waits = P._waits('sp', (), HK)
    P.ops['sp'].append((waits, None, None, 0))
    P.emit()
    es.close()
    return nc


_INPUT_NAMES = ["meta_tokens", "attn_norm_w", "attn_w_qkv", "attn_sinks", "attn_w_o", "ssm_norm_w", "ssm_lambda_re",
                "ssm_lambda_im", "ssm_log_dt", "ssm_b_re", "ssm_b_im", "ssm_c_re", "ssm_c_im", "ssm_d", "ssm_w_glu",
                "mlp_norm_w", "mlp_w_up", "mlp_w_down", "final_norm_w"]


def kernel(**inputs):
    x = np.ascontiguousarray(np.asarray(inputs["x"], dtype=np.float32))
    shared = {k: np.ascontiguousarray(np.asarray(inputs[k], dtype=np.float32)) for k in _INPUT_NAMES}
    nc = build()
    in_maps = []
    for c in range(NCORE):
        m = dict(shared)
        m["x"] = x[2 * c:2 * c + 2]
        in_maps.append(m)
    res = run_bass_kernel_spmd(nc, in_maps, core_ids=list(range(NCORE)))
    out = np.concatenate([np.asarray(r["y"], dtype=np.float32) for r in res.results], axis=0)
    return out
```

```python
import numpy as np
from contextlib import ExitStack
import concourse.bass as bass
import concourse.mybir as mybir
from concourse.bass_utils import run_bass_kernel_spmd

F32 = mybir.dt.float32
BF16 = mybir.dt.bfloat16
I32 = mybir.dt.int32
ALU = mybir.AluOpType
AF = mybir.ActivationFunctionType

D = 1024
SEQ = 4096
NCORE = 8
ENGS = ('pe', 'act', 'dve', 'pool', 'sp')
BLOCKNAME = {'pe': 'tensor', 'act': 'scalar', 'dve': 'vector', 'pool': 'gpsimd', 'sp': 'sync'}
PI = float(np.pi)
NSLOT = 5
SLOTE = 4096


class Prog:
    def __init__(self, nc, es):
        self.nc = nc
        self.es = es
        self.ops = {e: [] for e in ENGS}
        self.cnt = {e: 0 for e in ENGS}
        self.known = {e: {} for e in ENGS}
        self.state = {}
        self.dmacnt = {}
        self.sems = {}
        self.regions = {}
        self.groupkeys = {}

    def _waits(self, eng, reads, writes):
        deps = {}

        def addd(d):
            for k, v in d.items():
                if deps.get(k, 0) < v:
                    deps[k] = v
        for key in reads:
            st = self.state.get(key)
            if st:
                addd(st[0])
        for key in writes:
            st = self.state.get(key)
            if st:
                addd(st[0])
                addd(st[1])
        waits = []
        for k, v in deps.items():
            if k == 'pe' and eng == 'pe':
                continue
            if k in ENGS:
                assert v <= self.cnt[k], f"dep on unemitted milestone {k} {v} > {self.cnt[k]}"
            if self.known[eng].get(k, 0) >= v:
                continue
            self.known[eng][k] = v
            waits.append((k, v))
        return waits

    def _update(self, tag, v, reads, writes):
        ws = set(writes)
        for key in ws:
            self.state[key] = [{tag: v}, {}]
        for key in reads:
            if key in ws:
                continue
            st = self.state.setdefault(key, [{}, {}])
            st[1][tag] = v

    def op(self, eng, fn, reads=(), writes=(), ms=True):
        waits = self._waits(eng, reads, writes)
        if ms:
            self.cnt[eng] += 1
            self.ops[eng].append((waits, fn, eng, 1))
            self._update(eng, self.cnt[eng], reads, writes)
        else:
            self.ops[eng].append((waits, fn, None, 0))
            self._update(eng, self.cnt[eng] + 1, reads, writes)

    def dma(self, q, out, in_, semkey, reads=(), writes=(), group=False, **kw):
        waits = self._waits(q, reads, writes)
        v = self.dmacnt.get(semkey, 0) + 16
        self.dmacnt[semkey] = v
        self.ops[q].append((waits, lambda e: e.dma_start(out=out, in_=in_, **kw), semkey, 16))
        self._update(semkey, v, reads, writes)
        if group:
            self.groupkeys.setdefault(semkey, set()).update(writes)

    def group_done(self, semkey):
        tot = self.dmacnt[semkey]
        for key in self.groupkeys.get(semkey, ()):
            st = self.state.get(key)
            if st and semkey in st[0]:
                st[0][semkey] = tot
        self.groupkeys[semkey] = set()

    def claim(self, region, keys):
        reg = self.regions.setdefault(region, set())
        merged = {}
        for k in reg:
            st = self.state.get(k)
            if st:
                for d in (st[0], st[1]):
                    for kk, v in d.items():
                        if merged.get(kk, 0) < v:
                            merged[kk] = v
        for k in keys:
            self.state[k] = [dict(merged), {}]
            reg.add(k)

    def barrier(self):
        tot = {k: v for k, v in self.cnt.items() if v > 0}
        tot.update(self.dmacnt)
        for e in ENGS:
            waits = []
            for k, v in tot.items():
                if k == e:
                    continue
                if self.known[e].get(k, 0) >= v:
                    continue
                self.known[e][k] = v
                waits.append((k, v))
            self.ops[e].append((waits, None, None, 0))

    def emit(self):
        nc = self.nc
        for k in list(ENGS) + list(self.dmacnt.keys()):
            if k not in self.sems:
                self.sems[k] = self.es.enter_context(nc.semaphore("s_" + str(k)))
        sems = self.sems
        with nc.Block() as block:
            for e in ENGS:
                oplist = self.ops[e]

                def body(engobj, oplist=oplist):
                    for waits, fn, inc, amt in oplist:
                        for k, v in waits:
                            engobj.wait_ge(sems[k], v)
                        if fn is None:
                            continue
                        ins = fn(engobj)
                        if inc is not None:
                            ins.then_inc(sems[inc], amt)
                getattr(block, BLOCKNAME[e])(body)
        self.ops = {e: [] for e in ENGS}


def AP(t, p0, npart, off, dims):
    rowlen = 1
    for s in t.shape[1:]:
        rowlen *= s
    return bass.AP(t, p0 * rowlen + off, [[rowlen, npart]] + [list(d) for d in dims])


def DR(t, off, *dims):
    return bass.AP(t, off, [list(d) for d in dims])


def build(debug_stage=0, stop=None):
    import os
    stop = stop or os.environ.get('KSTOP')
    nc = bass.Bass("TRN2", target_bir_lowering=False)
    es = ExitStack()
    P = Prog(nc, es)

    small_only = stop is not None and stop.startswith('p')
    BIGIN = ("x", "attn_w_qkv", "attn_w_o", "ssm_w_glu", "mlp_w_up", "mlp_w_down")

    def din(name, shape):
        if small_only and name in BIGIN:
            return None
        if stop == 'meta' and name in ("x", "ssm_w_glu"):
            return None
        return nc.dram_tensor(name, list(shape), F32, kind="ExternalInput")
    x_d = din("x", [2, SEQ, D])
    meta_d = din("meta_tokens", [16, D])
    anw_d = din("attn_norm_w", [1, D])
    wqkv_d = din("attn_w_qkv", [1, D, 1536])
    sink_d = din("attn_sinks", [1, 16])
    wo_d = din("attn_w_o", [1, D, D])
    snw_d = din("ssm_norm_w", [1, D])
    lre_d = din("ssm_lambda_re", [1, 64, 64])
    lim_d = din("ssm_lambda_im", [1, 64, 64])
    ldt_d = din("ssm_log_dt", [1, 64])
    bre_d = din("ssm_b_re", [1, 64, 64, 16])
    bim_d = din("ssm_b_im", [1, 64, 64, 16])
    cre_d = din("ssm_c_re", [1, 64, 16, 64])
    cim_d = din("ssm_c_im", [1, 64, 16, 64])
    sd_d = din("ssm_d", [1, D])
    wglu_d = din("ssm_w_glu", [1, D, 2048])
    mnw_d = din("mlp_norm_w", [2, D])
    wup_d = din("mlp_w_up", [2, D, 4096])
    wdn_d = din("mlp_w_down", [2, 4096, D])
    fnw_d = din("final_norm_w", [D])
    y_d = nc.dram_tensor("y", [2, SEQ, D], F32, kind="ExternalOutput")
    SA_d = nc.dram_tensor("scrA", [4, 128, 4096], BF16)
    SB_d = nc.dram_tensor("scrB", [4, 128, 4096], BF16)
    SC_d = nc.dram_tensor("scrC", [4, 128, 2048], BF16)

    def sb(name, shape, dt):
        return nc.alloc_sbuf_tensor(name, list(shape), dt)
    scwn = [0]

    def finish():
        P.barrier()
        P.emit()
        es.close()
        return nc

    def scw():
        scwn[0] += 1
        return 'scw%d' % scwn[0]

    def TT(eng, out, in0, in1, op, r, w):
        P.op(eng, lambda e: e.tensor_tensor(out=out, in0=in0, in1=in1, op=op), r, w)

    def TS(eng, out, in0, s1, op0, r, w, s2=None, op1=None):
        if op1 is None:
            P.op(eng, lambda e: e.tensor_scalar(out=out, in0=in0, scalar1=s1, scalar2=None, op0=op0), r, w)
        else:
            P.op(eng, lambda e: e.tensor_scalar(out=out, in0=in0, scalar1=s1, scalar2=s2, op0=op0, op1=op1), r, w)

    def STT(eng, out, in0, sc, in1, op0, op1, r, w):
        P.op(eng, lambda e: e.scalar_tensor_tensor(out=out, in0=in0, scalar=sc, in1=in1, op0=op0, op1=op1), r, w)

    def ACT(out, in_, func, r, w, **kw):
        P.op('act', lambda e: e.activation(out=out, in_=in_, func=func, **kw), r, w)

    def CP(eng, out, in_, r, w):
        if eng == 'act':
            P.op('act', lambda e: e.copy(out=out, in_=in_), r, w)
        else:
            P.op(eng, lambda e: e.tensor_copy(out=out, in_=in_), r, w)

    def MS(eng, out, val, w):
        P.op(eng, lambda e: e.memset(out, val), (), w)

    def MM(out, lhsT, rhs, st, sp, r, w, ms=None):
        P.op('pe', lambda e: e.matmul(out, lhsT=lhsT, rhs=rhs, start=st, stop=sp), r, w, ms=(sp if ms is None else ms))

    def TR(out, in_, ident, r, w, ms=True):
        P.op('pe', lambda e: e.transpose(out=out, in_=in_, identity=ident), r, w, ms=ms)

    def RCP(out, in_, r, w):
        P.op('dve', lambda e: e.reciprocal(out=out, in_=in_), r, w)

    ident_f = sb("ident_f", [128, 128], F32)
    ident_b = sb("ident_b", [128, 128], BF16)
    wfm = sb("wfm", [128, 24], F32)
    expsink = sb("expsink", [128, 16], F32)
    epsc = sb("epsc", [128, 1], F32)
    SGN1 = sb("SGN1", [128, 1], F32)
    SGNI = sb("SGNI", [128, 1], F32)
    R8d = sb("R8d", [128, 64], F32)
    Xmeta = sb("Xmeta", [128, 64], F32)
    Xcar = sb("Xcar", [128, 64], F32)
    COSt = sb("COSt", [128, 64 * 129], BF16)
    SINt = sb("SINt", [128, 64 * 129], BF16)
    maskc = sb("maskc", [16, 128], BF16)
    kTm = sb("kTm", [64, 64], BF16)
    Vm = sb("Vm", [16, 260], BF16)
    kTp = sb("kTp", [64, 512], BF16)
    Vp = sb("Vp", [128, 260], BF16)
    ss_t = sb("ss_t", [128, 8], F32)
    rstd_t = sb("rstd_t", [128, 8], F32)
    den_t = sb("den_t", [128, 8], F32)
    rden_t = sb("rden_t", [128, 8], F32)
    vsw_t = sb("vsw_t", [128, 16], F32)
    ct_t = sb("ct_t", [128, 32], F32)
    junk = sb("junk", [128, 1024], BF16)
    BK = [nc.alloc_psum_tensor("B%d" % i, [128, 512], F32) for i in range(8)]
    BKb = [b.bitcast(BF16) for b in BK]

    MS('pool', AP(ident_f, 0, 128, 0, [[1, 128]]), 1.0, ['ident_f'])
    P.op('pool', lambda e: e.affine_select(out=AP(ident_f, 0, 128, 0, [[1, 128]]), in_=AP(ident_f, 0, 128, 0, [[1, 128]]),
                                           pattern=[[-1, 128]], compare_op=ALU.is_equal, fill=0.0, base=0, channel_multiplier=1),
         ['ident_f'], ['ident_f'])
    CP('dve', AP(ident_b, 0, 128, 0, [[1, 128]]), AP(ident_f, 0, 128, 0, [[1, 128]]), ['ident_f'], ['ident_b'])
    MS('pool', AP(epsc, 0, 128, 0, [[1, 1]]), 1e-6, ['epsc'])
    MS('pool', AP(SGN1, 0, 64, 0, [[1, 1]]), 1.0, ['SGN1'])
    MS('pool', AP(SGN1, 64, 64, 0, [[1, 1]]), -1.0, ['SGN1'])
    MS('pool', AP(SGNI, 0, 64, 0, [[1, 1]]), -1.0, ['SGNI'])
    MS('pool', AP(SGNI, 64, 64, 0, [[1, 1]]), 1.0, ['SGNI'])
    mk_f = sb("mk_f", [16, 128], F32)
    MS('pool', AP(mk_f, 0, 16, 0, [[1, 128]]), 1.0, ['mk_f'])
    P.op('pool', lambda e: e.affine_select(out=AP(mk_f, 0, 16, 0, [[1, 128]]), in_=AP(mk_f, 0, 16, 0, [[1, 128]]),
                                           pattern=[[1, 128]], compare_op=ALU.is_ge, fill=0.0, base=0, channel_multiplier=-1),
         ['mk_f'], ['mk_f'])
    CP('dve', AP(maskc, 0, 16, 0, [[1, 128]]), AP(mk_f, 0, 16, 0, [[1, 128]]), ['mk_f'], ['maskc'])
    for j, (dt_, off) in enumerate([(anw_d, 0), (mnw_d, 0), (mnw_d, D)]):
        P.dma('sp', AP(wfm, 0, 128, j * 8, [[1, 8]]), DR(dt_, off, [1, 128], [128, 8]), 'pl', (), ['wfm'], group=True,
              allow_slow_non_contiguous=True)
    P.dma('sp', AP(expsink, 0, 128, 0, [[1, 16]]), DR(sink_d, 0, [0, 128], [1, 16]), 'pl', (), ['expsink'], group=True)

    with ExitStack() as pes:
        def psb(name, shape, dt):
            return pes.enter_context(nc.sbuf_tensor(name, list(shape), dt))
        SCt = psb("SCt", [128, 40 * 64], F32)
        KI = psb("KI", [128, 64], I32)
        PWc = psb("PWc", [128, 64 * 9], F32)
        PWs = psb("PWs", [128, 64 * 9], F32)
        PRr = psb("PRr", [128, 64 * 8], F32)
        PIr = psb("PIr", [128, 64 * 8], F32)
        B_st = psb("B_st", [128, 1024], F32)
        B_sw = psb("B_sw", [128, 1024], F32)
        C_st = psb("C_st", [128, 1024], F32)
        C_sw = psb("C_sw", [128, 1024], F32)
        bb_st = psb("bb_st", [128, 1024], F32)
        bb_sw = psb("bb_sw", [128, 1024], F32)
        bb_sg = psb("bb_sg", [128, 1024], F32)
        WPt = psb("WPt", [128, 1024], F32)
        t1k = psb("t1k", [128, 1024], F32)
        t2k = psb("t2k", [128, 1024], F32)
        DTt = psb("DTt", [16, 1024], F32)
        WNb = psb("WNb", [16, 1024], F32)

        def sc(i):
            return AP(SCt, 0, 128, i * 64, [[1, 64]])
        (LR, LI, LDT, LRC, DTv, X1, ER, TH, TMP, KF, YY, SN, CS, ARr, AIi, NR, DEN, RDEN, FR, FI, T20, T21,
         C8, S8, TH2) = range(25)
        K_ = 'sc'
        for h in range(2):
            P.dma('sp', AP(SCt, 64 * h, 64, LR * 64, [[1, 64]]), DR(lre_d, 0, [1, 64], [64, 64]), 'pl', (), [K_], group=True,
                  allow_slow_non_contiguous=True)
            P.dma('sp', AP(SCt, 64 * h, 64, LI * 64, [[1, 64]]), DR(lim_d, 0, [1, 64], [64, 64]), 'pl', (), [K_], group=True,
                  allow_slow_non_contiguous=True)
        P.dma('sp', sc(LDT), DR(ldt_d, 0, [0, 128], [1, 64]), 'pl', (), [K_], group=True)
        for (dst, top, bot) in ((B_st, bre_d, bim_d), (B_sw, bim_d, bre_d)):
            P.dma('sp', AP(dst, 0, 64, 0, [[16, 64], [1, 16]]), DR(top, 0, [16, 64], [1024, 64], [1, 16]), 'pl', (), [dst.name], group=True)
            P.dma('sp', AP(dst, 64, 64, 0, [[16, 64], [1, 16]]), DR(bot, 0, [16, 64], [1024, 64], [1, 16]), 'pl', (), [dst.name], group=True)
        P.dma('sp', AP(WPt, 0, 128, 0, [[1, 1024]]), DR(snw_d, 0, [0, 128], [1, 1024]), 'pl', (), ['WPt'], group=True)
        P.dma('sp', AP(DTt, 0, 16, 0, [[1, 1024]]), DR(sd_d, 0, [0, 16], [1, 1024]), 'pl', (), ['DTt'], group=True)
        P.dma('sp', AP(WNb, 0, 16, 0, [[1, 1024]]), DR(snw_d, 0, [0, 16], [1, 1024]), 'pl', (), ['WNb'], group=True)
        with ExitStack() as ces:
            CnA = ces.enter_context(nc.sbuf_tensor("CnA", [128, 1024], F32))
            CnB = ces.enter_context(nc.sbuf_tensor("CnB", [128, 1024], F32))
            for (dst, first, second) in ((CnA, cre_d, cim_d), (CnB, cim_d, cre_d)):
                P.dma('sp', AP(dst, 0, 128, 0, [[128, 8], [1, 64]]), DR(first, 0, [64, 128], [8192, 8], [1, 64]), 'pl', (), [dst.name], group=True)
                P.dma('sp', AP(dst, 0, 128, 64, [[128, 8], [1, 64]]), DR(second, 0, [64, 128], [8192, 8], [1, 64]), 'pl', (), [dst.name], group=True)
            P.group_done('pl')
            ACT(AP(expsink, 0, 128, 0, [[1, 16]]), AP(expsink, 0, 128, 0, [[1, 16]]), AF.Exp, ['expsink'], ['expsink'])
            for (src, dst) in ((CnA, C_st), (CnB, C_sw)):
                for j in range(8):
                    bk = BK[j % 2]
                    TR(AP(bk, 0, 128, 0, [[1, 128]]), AP(src, 0, 128, j * 128, [[1, 128]]), AP(ident_f, 0, 128, 0, [[1, 128]]),
                       [src.name, 'ident_f'], ['B%d' % (j % 2)])
                    CP('dve' if j % 2 == 0 else 'act', AP(dst, 0, 128, j * 128, [[1, 128]]), AP(bk, 0, 128, 0, [[1, 128]]),
                       ['B%d' % (j % 2)], [dst.name])
            P.emit()
            if stop == 'p0':
                return finish()

        def sTT(o, a, b, op):
            TT('dve', sc(o), sc(a), sc(b), op, [K_], [K_])

        def sTS(o, a, s1, op0, s2=None, op1=None):
            TS('dve', sc(o), sc(a), s1, op0, [K_], [K_], s2, op1)

        def sinof(o, ang):
            TS('dve', AP(KI, 0, 128, 0, [[1, 64]]), sc(ang), float(1.0 / (2 * PI)), ALU.mult, [K_], ['KI'])
            CP('dve', sc(KF), AP(KI, 0, 128, 0, [[1, 64]]), ['KI'], [K_])
            STT('dve', sc(YY), sc(KF), float(-2 * PI), sc(ang), ALU.mult, ALU.add, [K_], [K_])
            sTS(YY, YY, 3.14159, ALU.min, -3.14159, ALU.max)
            ACT(sc(o), sc(YY), AF.Sin, [K_], [K_])
        sTS(LRC, LR, -1e-4, ALU.min)
        ACT(sc(DTv), sc(LDT), AF.Exp, [K_], [K_])
        sTT(X1, LRC, DTv, ALU.mult)
        ACT(sc(ER), sc(X1), AF.Exp, [K_], [K_])
        ACT(AP(R8d, 0, 128, 0, [[1, 64]]), sc(X1), AF.Exp, [K_], ['R8d'], scale=8.0)
        sTT(TH, LI, DTv, ALU.mult)
        sinof(SN, TH)
        sTS(TH2, TH, float(PI / 2), ALU.add)
        sinof(CS, TH2)
        sTT(ARr, ER, CS, ALU.mult)
        sTT(AIi, ER, SN, ALU.mult)
        sTS(NR, ARr, -1.0, ALU.add)
        sTT(T20, LRC, LRC, ALU.mult)
        sTT(T21, LI, LI, ALU.mult)
        sTT(DEN, T20, T21, ALU.add)
        RCP(sc(RDEN), sc(DEN), [K_], [K_])
        sTT(T20, NR, LRC, ALU.mult)
        sTT(T21, AIi, LI, ALU.mult)
        sTT(T20, T20, T21, ALU.add)
        sTT(FR, T20, RDEN, ALU.mult)
        sTT(T20, AIi, LRC, ALU.mult)
        sTT(T21, NR, LI, ALU.mult)
        sTT(T20, T20, T21, ALU.subtract)
        sTT(FI, T20, RDEN, ALU.mult)

        def pw(t, k, stride=9):
            return AP(t, 0, 128, k, [[stride, 64]])
        MS('dve', pw(PWc, 0), 1.0, ['PW'])
        MS('dve', pw(PWs, 0), 0.0, ['PW'])
        for k in range(8):
            CP('dve', pw(PRr, 7 - k, 8), pw(PWc, k), ['PW'], ['PRr'])
            CP('dve', pw(PIr, 7 - k, 8), pw(PWs, k), ['PW'], ['PRr'])
            TT('dve', sc(T20), pw(PWc, k), sc(ARr), ALU.mult, ['PW', K_], [K_])
            TT('dve', sc(T21), pw(PWs, k), sc(AIi), ALU.mult, ['PW', K_], [K_])
            TT('dve', pw(PWc, k + 1), sc(T20), sc(T21), ALU.subtract, [K_], ['PW'])
            TT('dve', sc(T20), pw(PWc, k), sc(AIi), ALU.mult, ['PW', K_], [K_])
            TT('dve', sc(T21), pw(PWs, k), sc(ARr), ALU.mult, ['PW', K_], [K_])
            TT('dve', pw(PWs, k + 1), sc(T20), sc(T21), ALU.add, [K_], ['PW'])
        CP('dve', sc(C8), sc(CS), [K_], [K_])
        CP('dve', sc(S8), sc(SN), [K_], [K_])
        for _ in range(3):
            sTT(T20, C8, C8, ALU.mult)
            sTT(T21, S8, S8, ALU.mult)
            sTT(TMP, C8, S8, ALU.mult)
            sTT(C8, T20, T21, ALU.subtract)
            sTS(S8, TMP, 2.0, ALU.mult)
        if stop == 'p1':
            return finish()
        with ExitStack() as tes:
            TC = tes.enter_context(nc.sbuf_tensor("TC", [128, 64 * 129], F32))
            TSn = tes.enter_context(nc.sbuf_tensor("TSn", [128, 64 * 129], F32))
            T1 = tes.enter_context(nc.sbuf_tensor("T1", [128, 4096], F32))
            T2 = tes.enter_context(nc.sbuf_tensor("T2", [128, 4096], F32))

            def tb(t, m0, L, bc=False):
                return AP(t, 0, 128, m0, [[129, 64], [0 if bc else 1, L]])
            MS('dve', tb(TC, 0, 1), 1.0, ['TC'])
            MS('dve', tb(TSn, 0, 1), 0.0, ['TS'])
            CP('dve', tb(TC, 1, 1), AP(SCt, 0, 128, C8 * 64, [[1, 64], [1, 1]]), [K_], ['TC'])
            CP('dve', tb(TSn, 1, 1), AP(SCt, 0, 128, S8 * 64, [[1, 64], [1, 1]]), [K_], ['TS'])
            L = 1
            while L <= 64:
                o1 = AP(T1, 0, 128, 0, [[L, 64], [1, L]])
                o2 = AP(T2, 0, 128, 0, [[L, 64], [1, L]])
                TT('dve', o1, tb(TC, 1, L), tb(TC, L, L, True), ALU.mult, ['TC'], ['T1'])
                TT('dve', o2, tb(TSn, 1, L), tb(TSn, L, L, True), ALU.mult, ['TS'], ['T2'])
                TT('dve', tb(TC, L + 1, L), o1, o2, ALU.subtract, ['T1', 'T2'], ['TC'])
                TT('dve', o1, tb(TC, 1, L), tb(TSn, L, L, True), ALU.mult, ['TC', 'TS'], ['T1'])
                TT('dve', o2, tb(TSn, 1, L), tb(TC, L, L, True), ALU.mult, ['TC', 'TS'], ['T2'])
                TT('dve', tb(TSn, L + 1, L), o1, o2, ALU.add, ['T1', 'T2'], ['TS'])
                L *= 2
            CP('dve', AP(COSt, 0, 128, 0, [[1, 64 * 129]]), AP(TC, 0, 128, 0, [[1, 64 * 129]]), ['TC'], ['COSt'])
            CP('act', AP(SINt, 0, 128, 0, [[1, 64 * 129]]), AP(TSn, 0, 128, 0, [[1, 64 * 129]]), ['TS'], ['SINt'])
            P.emit()
        if stop == 'p2':
            return finish()
        g16 = [[16, 64], [1, 16]]
        g16b = [[1, 64], [0, 16]]
        TT('dve', AP(t1k, 0, 128, 0, g16), AP(B_st, 0, 128, 0, g16), AP(SCt, 0, 128, FR * 64, g16b), ALU.mult, ['B_st', K_], ['t1k'])
        TT('dve', AP(t2k, 0, 128, 0, g16), AP(B_sw, 0, 128, 0, g16), AP(SCt, 0, 128, FI * 64, g16b), ALU.mult, ['B_sw', K_], ['t2k'])
        STT('dve', AP(bb_st, 0, 128, 0, [[1, 1024]]), AP(t2k, 0, 128, 0, [[1, 1024]]), AP(SGNI, 0, 128, 0, [[1, 1]]),
            AP(t1k, 0, 128, 0, [[1, 1024]]), ALU.mult, ALU.add, ['t1k', 't2k', 'SGNI'], ['bb_st'])
        TT('dve', AP(t1k, 0, 128, 0, g16), AP(B_sw, 0, 128, 0, g16), AP(SCt, 0, 128, FR * 64, g16b), ALU.mult, ['B_sw', K_], ['t1k'])
        TT('dve', AP(t2k, 0, 128, 0, g16), AP(B_st, 0, 128, 0, g16), AP(SCt, 0, 128, FI * 64, g16b), ALU.mult, ['B_st', K_], ['t2k'])
        STT('dve', AP(bb_sw, 0, 128, 0, [[1, 1024]]), AP(t2k, 0, 128, 0, [[1, 1024]]), AP(SGN1, 0, 128, 0, [[1, 1]]),
            AP(t1k, 0, 128, 0, [[1, 1024]]), ALU.mult, ALU.add, ['t1k', 't2k', 'SGN1'], ['bb_sw'])
        full = [[1, 1024]]
        TT('dve', AP(bb_st, 0, 128, 0, full), AP(bb_st, 0, 128, 0, full), AP(WPt, 0, 128, 0, full), ALU.mult, ['bb_st', 'WPt'], ['bb_st'])
        TT('dve', AP(bb_sw, 0, 128, 0, full), AP(bb_sw, 0, 128, 0, full), AP(WPt, 0, 128, 0, full), ALU.mult, ['bb_sw', 'WPt'], ['bb_sw'])
        TS('dve', AP(bb_sg, 0, 128, 0, full), AP(bb_st, 0, 128, 0, full), AP(SGN1, 0, 128, 0, [[1, 1]]), ALU.mult, ['bb_st', 'SGN1'], ['bb_sg'])
        TT('dve', AP(DTt, 0, 16, 0, full), AP(DTt, 0, 16, 0, full), AP(WNb, 0, 16, 0, full), ALU.mult, ['DTt', 'WNb'], ['DTt'])
        TT('dve', AP(DTt, 0, 16, 0, g16), AP(DTt, 0, 16, 0, g16), AP(ident_f, 0, 16, 0, [[0, 64], [1, 16]]), ALU.mult, ['DTt', 'ident_f'], ['DTt'])
        P.emit()
        if stop == 'p3':
            return finish()
        with ExitStack() as bes:
            G1 = bes.enter_context(nc.sbuf_tensor("G1", [128, 4608], F32))
            G2 = bes.enter_context(nc.sbuf_tensor("G2", [128, 4608], F32))
            G3 = bes.enter_context(nc.sbuf_tensor("G3", [128, 4608], F32))
            OB1 = bes.enter_context(nc.sbuf_tensor("OB1", [128, 4096], BF16))
            OB2 = bes.enter_context(nc.sbuf_tensor("OB2", [128, 4096], BF16))
            M_bf = bes.enter_context(nc.sbuf_tensor("M_bf", [16, 8192], BF16))
            G1b = G1.bitcast(BF16)
            for gh in range(2):
                d4 = [[128, 32], [16, 8], [1, 16]]
                bbv = [[16, 32], [0, 8], [1, 16]]
                prv = [[8, 32], [1, 8], [0, 16]]
                TT('dve', AP(G1, 0, 128, 0, d4), AP(bb_st, 0, 128, gh * 512, bbv), AP(PRr, 0, 128, gh * 256, prv), ALU.mult, ['bb_st', 'PRr'], ['G1'])
                TT('dve', AP(G2, 0, 128, 0, d4), AP(bb_sw, 0, 128, gh * 512, bbv), AP(PIr, 0, 128, gh * 256, prv), ALU.mult, ['bb_sw', 'PRr'], ['G2'])
                STT('dve', AP(G1, 0, 128, 0, [[1, 4096]]), AP(G2, 0, 128, 0, [[1, 4096]]), AP(SGNI, 0, 128, 0, [[1, 1]]),
                    AP(G1, 0, 128, 0, [[1, 4096]]), ALU.mult, ALU.add, ['G1', 'G2', 'SGNI'], ['G1'])
                if stop == 'p3a1':
                    return finish()
                for gl in range(32):
                    bi = (gl // 4) % 2
                    bk = BK[bi]
                    TR(AP(bk, 0, 128, (gl % 4) * 128, [[1, 128]]), AP(G1, 0, 128, gl * 128, [[1, 128]]), AP(ident_f, 0, 128, 0, [[1, 128]]),
                       ['G1', 'ident_f'], ['B%d' % bi])
                    KV = os.environ.get('KVAR', 'abc')
                    if gl % 4 == 3:
                        g0 = gl - 3
                        if 'a' in KV:
                            CP('dve', AP(OB1, 0, 128, g0 * 128, [[1, 512]]), AP(bk, 0, 128, 0, [[1, 512]]), ['B%d' % bi], ['OB1'])
                        if 'b' in KV:
                            CP('dve', AP(OB2, 0, 128, g0 * 128, [[128, 4], [1, 64]]), AP(bk, 0, 128, 64, [[128, 4], [1, 64]]), ['B%d' % bi], ['OB2'])
                        if 'c' in KV:
                            TS('dve', AP(OB2, 0, 128, g0 * 128 + 64, [[128, 4], [1, 64]]), AP(bk, 0, 128, 0, [[128, 4], [1, 64]]), -1.0, ALU.mult,
                               ['B%d' % bi], ['OB2'])
                if stop == 'p3a2':
                    return finish()
                for j in range(2):
                    q = 2 * gh + j
                    P.dma('sp', DR(SA_d, q * 128 * 4096, [4096, 128], [1, 2048]), AP(OB1, 0, 128, j * 2048, [[1, 2048]]), scw(), ['OB1'], ['SA'])
                    P.dma('sp', DR(SA_d, q * 128 * 4096 + 2048, [4096, 128], [1, 2048]), AP(OB2, 0, 128, j * 2048, [[1, 2048]]), scw(), ['OB2'], ['SA'])
                if stop == 'p3a':
                    return finish()
                c4 = [[144, 32], [16, 9], [1, 16]]
                cv = [[16, 32], [0, 9], [1, 16]]
                pv = [[9, 32], [1, 9], [0, 16]]
                TT('dve', AP(G1, 0, 128, 0, c4), AP(C_st, 0, 128, gh * 512, cv), AP(PWc, 0, 128, gh * 288, pv), ALU.mult, ['C_st', 'PW'], ['G1'])
                TT('dve', AP(G2, 0, 128, 0, c4), AP(C_sw, 0, 128, gh * 512, cv), AP(PWs, 0, 128, gh * 288, pv), ALU.mult, ['C_sw', 'PW'], ['G2'])
                STT('dve', AP(G1, 0, 128, 0, [[1, 4608]]), AP(G2, 0, 128, 0, [[1, 4608]]), AP(SGNI, 0, 128, 0, [[1, 1]]),
                    AP(G1, 0, 128, 0, [[1, 4608]]), ALU.mult, ALU.add, ['G1', 'G2', 'SGNI'], ['G1'])
                TT('dve', AP(G3, 0, 128, 0, c4), AP(C_sw, 0, 128, gh * 512, cv), AP(PWc, 0, 128, gh * 288, pv), ALU.mult, ['C_sw', 'PW'], ['G3'])
                TT('dve', AP(G2, 0, 128, 0, c4), AP(C_st, 0, 128, gh * 512, cv), AP(PWs, 0, 128, gh * 288, pv), ALU.mult, ['C_st', 'PW'], ['G2'])
                STT('dve', AP(G3, 0, 128, 0, [[1, 4608]]), AP(G2, 0, 128, 0, [[1, 4608]]), AP(SGN1, 0, 128, 0, [[1, 1]]),
                    AP(G3, 0, 128, 0, [[1, 4608]]), ALU.mult, ALU.add, ['G3', 'G2', 'SGN1'], ['G3'])
                TS('dve', AP(OB1, 0, 128, 0, [[128, 32], [1, 128]]), AP(G1, 0, 128, 16, [[144, 32], [1, 128]]), AP(SGN1, 0, 128, 0, [[1, 1]]), ALU.mult,
                   ['G1', 'SGN1'], ['OB1'])
                TS('dve', AP(OB2, 0, 128, 0, [[128, 32], [1, 128]]), AP(G3, 0, 128, 16, [[144, 32], [1, 128]]), -1.0, ALU.mult, ['G3'], ['OB2'])
                for j in range(2):
                    q = 2 * gh + j
                    P.dma('sp', DR(SB_d, q * 128 * 4096 + 2048, [4096, 128], [1, 2048]), AP(OB1, 0, 128, j * 2048, [[1, 2048]]), scw(), ['OB1'], ['SB'])
                    P.dma('sp', DR(SC_d, q * 128 * 2048, [2048, 128], [1, 2048]), AP(OB2, 0, 128, j * 2048, [[1, 2048]]), scw(), ['OB2'], ['SC'])
                if stop == 'p3b':
                    return finish()
                for gl in range(32):
                    g = gh * 32 + gl
                    bi = 2 + (gl // 4) % 2
                    bk = BK[bi]
                    MM(AP(bk, 0, 16, (gl % 4) * 128, [[1, 128]]), AP(bb_sg, 0, 128, g * 16, [[1, 16]]), AP(G1, 0, 128, gl * 144, [[1, 128]]), True, True,
                       ['bb_sg', 'G1'], ['B%d' % bi])
                    if gl % 4 == 3:
                        g0 = gl - 3
                        CP('act', AP(G2, 0, 16, g0 * 128, [[1, 512]]), AP(bk, 0, 16, 0, [[1, 512]]), ['B%d' % bi], ['G2'])
                TT('dve', AP(G2, 0, 16, 0, [[128, 32], [1, 16]]), AP(G2, 0, 16, 0, [[128, 32], [1, 16]]), AP(DTt, 0, 16, gh * 512, [[16, 32], [1, 16]]), ALU.add,
                   ['G2', 'DTt'], ['G2'])
                CP('dve', AP(M_bf, 0, 16, gh * 4096, [[1, 4096]]), AP(G2, 0, 16, 0, [[1, 4096]]), ['G2'], ['M_bf'])
            if stop == 'p3c':
                return finish()
            MS('dve', AP(G1b, 0, 128, 0, [[1, 8192]]), 0.0, ['G1'])
            for s in range(8):
                n = (8 - s) * 16
                P.dma('sp', AP(G1b, 16 * s, 16, s * 16, [[128, 64], [1, n]]), AP(M_bf, 0, 16, 0, [[128, 64], [1, n]]), 'kpl', ['M_bf'], ['G1'])
            for q in range(4):
                P.dma('sp', DR(SB_d, q * 128 * 4096, [4096, 128], [1, 2048]), AP(G1b, 0, 128, q * 2048, [[1, 2048]]), scw(), ['G1'], ['SB'])
            P.barrier()
            P.emit()
    if stop == 'p4':
        return finish()
    Ecur = sb("Ecur", [128, 2048], BF16)
    Eprev = sb("Eprev", [128, 2048], BF16)
    wfin = sb("wfin", [128, 1024], F32)
    P.dma('sp', AP(wfin, 0, 128, 0, [[1, 1024]]), DR(fnw_d, 0, [0, 128], [1, 1024]), 'pl', (), ['wfin'])
    with ExitStack() as ees:
        di = ees.enter_context(nc.sbuf_tensor("di", [128, 128], I32))
        df = ees.enter_context(nc.sbuf_tensor("df", [128, 128], F32))
        dc = ees.enter_context(nc.sbuf_tensor("dc", [128, 128], F32))
        dp = ees.enter_context(nc.sbuf_tensor("dp", [128, 128], F32))
        Ef = ees.enter_context(nc.sbuf_tensor("Ef", [128, 2048], F32))
        P.op('pool', lambda e: e.iota(AP(di, 0, 128, 0, [[1, 128]]), [[1, 128]], base=0, channel_multiplier=-1), (), ['di'])
        CP('dve', AP(df, 0, 128, 0, [[1, 128]]), AP(di, 0, 128, 0, [[1, 128]]), ['di'], ['df'])
        TS('dve', AP(dc, 0, 128, 0, [[1, 128]]), AP(df, 0, 128, 0, [[1, 128]]), 0.0, ALU.max, ['df'], ['dc'])
        TS('dve', AP(dp, 0, 128, 0, [[1, 128]]), AP(df, 0, 128, 0, [[1, 128]]), 128.0, ALU.add, ['df'], ['dp'])
        TS('dve', AP(dp, 0, 128, 0, [[1, 128]]), AP(dp, 0, 128, 0, [[1, 128]]), 0.0, ALU.max, ['dp'], ['dp'])
        for (src, dstE, pat, base, cm) in ((dc, Ecur, [[0, 16], [1, 128]], 0, -1), (dp, Eprev, [[0, 16], [-1, 128]], -1, 1)):
            for h in range(16):
                slope = float(2.0 ** (-(h + 1) / 2.0))
                ACT(AP(Ef, 0, 128, h * 128, [[1, 128]]), AP(src, 0, 128, 0, [[1, 128]]), AF.Exp, [src.name], ['Ef'], scale=-slope)
            P.op('pool', lambda e, pat=pat, base=base, cm=cm: e.affine_select(
                out=AP(Ef, 0, 128, 0, [[128, 16], [1, 128]]), in_=AP(Ef, 0, 128, 0, [[128, 16], [1, 128]]),
                pattern=pat, compare_op=ALU.is_ge, fill=0.0, base=base, channel_multiplier=cm), ['Ef'], ['Ef'])
            CP('dve', AP(dstE, 0, 128, 0, [[1, 2048]]), AP(Ef, 0, 128, 0, [[1, 2048]]), ['Ef'], [dstE.name])
        P.barrier()
        P.emit()

    if stop == 'p5':
        return finish()
    H8 = sb("H8", [128, 8192], F32)
    Hn8 = sb("Hn8", [128, 8192], BF16)
    hnT = sb("hnT", [128, 8192], BF16)
    RING = sb("RING", [128, NSLOT * SLOTE], BF16)
    RA = sb("RA", [128, 8192], F32)
    RAb = RA.bitcast(BF16)
    TP = sb("TP", [128, 2048], F32)
    PTB = sb("PTB", [128, 3072], BF16)
    Otok = sb("Otok", [128, 1024], BF16)
    QT0, KT0, VA0 = 0, 8192, 12800

    specs = []

    def mlp_specs(l):
        U = lambda qd: [('up', l, qd, 0), ('up', l, qd, 1)]
        Dn = lambda qd: [('dn', l, qd, 0), ('dn', l, qd, 1)]
        return U(0) + U(1) + Dn(0) + U(2) + Dn(1) + U(3) + Dn(2) + Dn(3)

    def tile_specs(meta):
        s = [('wkv',), ('wq', 0), ('wq', 1)]
        if not meta:
            s += [('wq', 0), ('wq', 1)]
        s += [('wo', 0), ('wo', 1)] + mlp_specs(0)
        if meta:
            s += [('A', q) for q in range(4)]
        elif debug_stage != 1:
            s += [('A', 0), ('A', 1), ('B', 0), ('C', 0), ('A', 2), ('B', 1), ('C', 1), ('A', 3), ('B', 2), ('C', 2), ('B', 3), ('C', 3)]
            s += [('glu', 0, 0), ('glu', 1, 0), ('glu', 0, 1), ('glu', 1, 1)] + mlp_specs(1)
        return s
    specs = tile_specs(True)
    ntiles = 8 if debug_stage != 9 else 1
    for _ in range(ntiles):
        specs += tile_specs(False)
    ring = {'next': 0, 'issued': 0, 'done': [False] * len(specs)}

    def issue(i):
        sp = specs[i]
        sl = i % NSLOT
        key = 'R%d' % sl
        base = sl * SLOTE
        kind = sp[0]
        if kind == 'wq':
            src = DR(wqkv_d, sp[1] * 512, [1536, 128], [128 * 1536, 8], [1, 512]); dims = [[512, 8], [1, 512]]; q = 'pool'
        elif kind == 'wkv':
            src = DR(wqkv_d, 1024, [1536, 128], [128 * 1536, 8], [1, 512]); dims = [[512, 8], [1, 512]]; q = 'pool'
        elif kind == 'wo':
            src = DR(wo_d, sp[1] * 512, [1024, 128], [128 * 1024, 8], [1, 512]); dims = [[512, 8], [1, 512]]; q = 'pool'
        elif kind == 'up':
            _, l, qd, j = sp
            src = DR(wup_d, l * D * 4096 + qd * 1024 + j * 512, [4096, 128], [128 * 4096, 8], [1, 512]); dims = [[512, 8], [1, 512]]; q = 'pool'
        elif kind == 'dn':
            _, l, qd, j = sp
            src = DR(wdn_d, l * 4096 * D + (qd * 1024 + j * 512) * D, [1024, 128], [128 * 1024, 4], [1, 1024]); dims = [[1024, 4], [1, 1024]]; q = 'pool'
        elif kind == 'glu':
            _, part, j = sp
            src = DR(wglu_d, part * 1024 + j * 512, [2048, 128], [128 * 2048, 8], [1, 512]); dims = [[512, 8], [1, 512]]; q = 'pool'
        elif kind == 'A':
            src = DR(SA_d, sp[1] * 128 * 4096, [4096, 128], [1, 4096]); dims = [[1, 4096]]; q = 'sp'
        elif kind == 'B':
            src = DR(SB_d, sp[1] * 128 * 4096, [4096, 128], [1, 4096]); dims = [[1, 4096]]; q = 'sp'
        elif kind == 'C':
            src = DR(SC_d, sp[1] * 128 * 2048, [2048, 128], [1, 2048]); dims = [[1, 2048]]; q = 'sp'
        P.dma(q, AP(RING, 0, 128, base, dims), src, key, (), [key])

    def pump():
        while ring['issued'] < len(specs) and ring['issued'] < ring['next'] + NSLOT:
            i = ring['issued']
            if i >= NSLOT and not ring['done'][i - NSLOT]:
                break
            issue(i)
            ring['issued'] += 1

    def rget(*sp):
        i = ring['next']
        assert specs[i] == tuple(sp), (i, specs[i], sp)
        ring['next'] += 1
        pump()
        assert ring['issued'] > i
        return i

    def rdone(i):
        ring['done'][i] = True
        pump()

    def rk(i):
        return 'R%d' % (i % NSLOT)

    def rb(i):
        return (i % NSLOT) * SLOTE

    ctr = {'t': 0, 'pt': 0, 'ev': 0, 'bk': 0}

    def tmp():
        i = ctr['t'] % 4
        ctr['t'] += 1
        return i

    def evq():
        ctr['ev'] += 1
        return 'act' if ctr['ev'] % 2 == 0 else 'dve'

    HK = ['H8.%d' % s for s in range(8)]
    HNK = ['Hn8.%d' % s for s in range(8)]
    TK = ['hnT.%d' % k for k in range(8)]

    def rmsnorm(nt, order):
        P.claim('Hn8', HNK)
        SSK = ['ss.%d' % s for s in range(8)]
        RSK = ['rstd.%d' % s for s in range(8)]

        def sq(s):
            ACT(AP(junk, 0, nt, 0, [[1, 1024]]), AP(H8, 0, nt, s * 1024, [[1, 1024]]), AF.Square, [HK[s], SSK[s]], [SSK[s]],
                accum_out=AP(ss_t, 0, nt, s, [[1, 1]]))

        def rs(s0, n):
            ACT(AP(rstd_t, 0, nt, s0, [[1, n]]), AP(ss_t, 0, nt, s0, [[1, n]]), AF.Sqrt, SSK[s0:s0 + n] + ['epsc'], RSK[s0:s0 + n],
                bias=AP(epsc, 0, nt, 0, [[1, 1]]), scale=float(1.0 / D))
            RCP(AP(rstd_t, 0, nt, s0, [[1, n]]), AP(rstd_t, 0, nt, s0, [[1, n]]), RSK[s0:s0 + n], RSK[s0:s0 + n])

        def scale(s):
            if order == 'std':
                o = AP(Hn8, 0, nt, s * 1024, [[1, 1024]])
                i = AP(H8, 0, nt, s * 1024, [[1, 1024]])
            else:
                o = AP(Hn8, 0, nt, s * 16, [[128, 64], [1, 16]])
                i = AP(H8, 0, nt, s * 1024, [[16, 64], [1, 16]])
            TS('dve', o, i, AP(rstd_t, 0, nt, s, [[1, 1]]), ALU.mult, [HK[s], RSK[s]], [HNK[s]])
        for (s0, n) in ((0, 4), (4, 3), (7, 1)):
            for s in range(s0, s0 + n):
                sq(s)
            rs(s0, n)
            for s in range(s0, s0 + n):
                scale(s)
        MS('dve', AP(ss_t, 0, 128, 0, [[1, 8]]), 0.0, SSK)

    def to_featmajor(nt, wj, srckeys, sn=False):
        P.claim('hnT', TK)
        for kc in range(8):
            bi = kc % 2
            for s in range(8):
                TR(AP(BKb[bi], 0, 128, s * 128, [[1, nt]]), AP(Hn8, 0, nt, s * 1024 + kc * 128, [[1, 128]]), AP(ident_b, 0, nt, 0, [[1, nt]]),
                   srckeys + ['ident_b'], ['B%d' % bi], ms=(s == 7))
            if sn:
                o = AP(hnT, 0, 128, kc * 1024, [[1, 1024]])
                i = AP(BKb[bi], 0, 128, 0, [[1, 1024]])
            else:
                o = AP(hnT, 0, 128, kc * 1024, [[1, 8], [8, nt]])
                i = AP(BKb[bi], 0, 128, 0, [[128, 8], [1, nt]])
            if wj is None:
                CP(evq(), o, i, ['B%d' % bi], [TK[kc]])
            else:
                e = evq()
                wc = AP(wfm, 0, 128, wj * 8 + kc, [[1, 1]])
                if e == 'dve':
                    TS('dve', o, i, wc, ALU.mult, ['B%d' % bi, 'wfm'], [TK[kc]])
                else:
                    ACT(o, i, AF.Copy, ['B%d' % bi, 'wfm'], [TK[kc]], scale=wc)

    def resid_add(nt, s, half, bi):
        hs = AP(H8, 0, nt, s * 1024 + half * 512, [[1, 512]])
        TT('dve', hs, AP(BK[bi], 0, nt, 0, [[1, 512]]), hs, ALU.add, ['B%d' % bi, HK[s]], [HK[s]])

    def attention(nt, meta, first):
        ntok = nt * 8
        nblk = max(1, ntok // 128)
        QB = min(128, ntok)
        nth = max(1, ntok // 512)
        tn = min(512, ntok)
        P.claim('RA', ['qT', 'kT', 'Vatt'])
        P.claim('Hn8', ['OT'])
        MS('dve', AP(RAb, 0, 128, VA0, [[1, 2340]]), 1.0, ['Vatt'])
        if not meta and not first:
            CP('dve', AP(RAb, 0, 128, VA0, [[1, 260]]), AP(Vp, 0, 128, 0, [[1, 260]]), ['Vp'], ['Vatt'])
            CP('dve', AP(RAb, 0, 64, KT0, [[1152, 4], [1, 128]]), AP(kTp, 0, 64, 0, [[128, 4], [1, 128]]), ['kTp'], ['kT'])
        iw = rget('wkv')
        for kv in range(4):
            for th in range(nth):
                bi = 2 + (ctr['bk'] % 2); ctr['bk'] += 1
                for kc in range(8):
                    MM(AP(BK[bi], 0, 64, 0, [[1, tn]]), AP(RING, 0, 128, rb(iw) + kc * 512 + kv * 64, [[1, 64]]),
                       AP(hnT, 0, 128, kc * 1024 + th * 512, [[1, tn]]), kc == 0, kc == 7, [rk(iw), TK[kc]], ['B%d' % bi])
                CP(evq(), AP(RAb, 0, 64, KT0 + kv * 1152 + 128 + th * 512, [[1, tn]]), AP(BK[bi], 0, 64, 0, [[1, tn]]), ['B%d' % bi], ['kT'])
        for blk in range(nblk):
            bi = 2 + (ctr['bk'] % 2); ctr['bk'] += 1
            for kc in range(8):
                MM(AP(BK[bi], 0, QB, 0, [[1, 256]]), AP(hnT, 0, 128, kc * 1024 + blk * 128, [[1, QB]]),
                   AP(RING, 0, 128, rb(iw) + kc * 512 + 256, [[1, 256]]), kc == 0, kc == 7, [rk(iw), TK[kc]], ['B%d' % bi])
            CP(evq(), AP(RAb, 0, QB, VA0 + (blk + 1) * 260, [[65, 4], [1, 64]]), AP(BK[bi], 0, QB, 0, [[64, 4], [1, 64]]), ['B%d' % bi], ['Vatt'])
        rdone(iw)
        if meta:
            CP('dve', AP(kTm, 0, 64, 0, [[16, 4], [1, 16]]), AP(RAb, 0, 64, KT0 + 128, [[1152, 4], [1, 16]]), ['kT'], ['kTm'])
            CP('dve', AP(Vm, 0, 16, 0, [[1, 260]]), AP(RAb, 0, 16, VA0 + 260, [[1, 260]]), ['Vatt'], ['Vm'])
            MS('dve', AP(RAb, 0, 64, QT0, [[1, 8192]]), 0.0, ['qT'])
        nhalf = 1 if meta else 2
        for hf in range(nhalf):
            bl0 = hf * 4
            nb_h = min(4, nblk)
            iq = [rget('wq', 0), rget('wq', 1)]
            tnq = min(512, ntok)
            for h in range(16):
                bi = 2 + (ctr['bk'] % 2); ctr['bk'] += 1
                ip = iq[h // 8]
                for kc in range(8):
                    MM(AP(BK[bi], 0, 64, 0, [[1, tnq]]), AP(RING, 0, 128, rb(ip) + kc * 512 + (h % 8) * 64, [[1, 64]]),
                       AP(hnT, 0, 128, kc * 1024 + hf * 512, [[1, tnq]]), kc == 0, kc == 7, [rk(ip), TK[kc]], ['B%d' % bi])
                if meta:
                    CP(evq(), AP(RAb, 0, 64, QT0 + h * 128, [[1, tnq]]), AP(BK[bi], 0, 64, 0, [[1, tnq]]), ['B%d' % bi], ['qT'])
                else:
                    CP(evq(), AP(RAb, 0, 64, QT0 + h * 128, [[2048, 4], [1, 128]]), AP(BK[bi], 0, 64, 0, [[128, 4], [1, 128]]), ['B%d' % bi], ['qT'])
            rdone(iq[0]); rdone(iq[1])
            SETS = ((4, 5, 6), (2, 3, 1))
            its = [(bl, kv) for bl in range(nb_h) for kv in range(4)]

            def hp(bl):
                return (not meta) and ((bl0 + bl) > 0 or not first)

            def scores(i):
                bl, kv = its[i]
                b = bl0 + bl
                bc, bp, bm = SETS[i % 2]
                rq = AP(RAb, 0, 64, QT0 + bl * 2048 + kv * 512, [[1, 512]])
                if not meta:
                    MM(AP(BK[bc], 0, 128, 0, [[1, 512]]), AP(RAb, 0, 64, KT0 + kv * 1152 + 128 + b * 128, [[1, 128]]), rq, True, True, ['kT', 'qT'], ['B%d' % bc], ms=False)
                    if hp(bl):
                        MM(AP(BK[bp], 0, 128, 0, [[1, 512]]), AP(RAb, 0, 64, KT0 + kv * 1152 + b * 128, [[1, 128]]), rq, True, True, ['kT', 'qT'], ['B%d' % bp], ms=False)
                MM(AP(BK[bm], 0, 16, 0, [[1, 512]]), AP(kTm, 0, 64, kv * 16, [[1, 16]]), rq, True, True, ['kTm', 'qT'], ['B%d' % bm], ms=True)

            def ptinfo(i):
                pset = i % 2
                PTc = pset * 1536
                return PTc, PTc + 512, PTc + 1024, 'PTc%d' % pset, 'PTp%d' % pset, 'PTm%d' % pset

            def expmask(i):
                bl, kv = its[i]
                bc, bp, bm = SETS[i % 2]
                has_prev = hp(bl)
                PTc, PTp, PTm, kc_, kp_, km_ = ptinfo(i)
                if not meta:
                    t = tmp()
                    ACT(AP(TP, 0, 128, t * 512, [[1, 512]]), AP(BK[bc], 0, 128, 0, [[1, 512]]), AF.Exp, ['B%d' % bc], ['T%d' % t], scale=0.125)
                    TT('dve', AP(PTB, 0, 128, PTc, [[1, 512]]), AP(TP, 0, 128, t * 512, [[1, 512]]), AP(Ecur, 0, 128, kv * 512, [[1, 512]]), ALU.mult,
                       ['T%d' % t, 'Ecur'], [kc_])
                    if has_prev:
                        t = tmp()
                        ACT(AP(TP, 0, 128, t * 512, [[1, 512]]), AP(BK[bp], 0, 128, 0, [[1, 512]]), AF.Exp, ['B%d' % bp], ['T%d' % t], scale=0.125)
                        TT('dve', AP(PTB, 0, 128, PTp, [[1, 512]]), AP(TP, 0, 128, t * 512, [[1, 512]]), AP(Eprev, 0, 128, kv * 512, [[1, 512]]), ALU.mult,
                           ['T%d' % t, 'Eprev'], [kp_])
                    ACT(AP(PTB, 0, 16, PTm, [[1, 512]]), AP(BK[bm], 0, 16, 0, [[1, 512]]), AF.Exp, ['B%d' % bm], [km_], scale=0.125)
                else:
                    t = tmp()
                    ACT(AP(TP, 0, 16, t * 512, [[1, 512]]), AP(BK[bm], 0, 16, 0, [[1, 512]]), AF.Exp, ['B%d' % bm], ['T%d' % t], scale=0.125)
                    TT('dve', AP(PTB, 0, 16, PTm, [[128, 4], [1, 128]]), AP(TP, 0, 16, t * 512, [[128, 4], [1, 128]]),
                       AP(maskc, 0, 16, 0, [[0, 4], [1, 128]]), ALU.mult, ['T%d' % t, 'maskc'], [km_])
            def pv(i):
                bl, kv = its[i]
                b = bl0 + bl
                has_prev = hp(bl)
                PTc, PTp, PTm, kc_, kp_, km_ = ptinfo(i)
                for hl in range(4):
                    lst = []
                    if not meta:
                        lst.append((AP(PTB, 0, 128, PTc + hl * 128, [[1, QB]]), AP(RAb, 0, 128, VA0 + (b + 1) * 260 + kv * 65, [[1, 65]]), [kc_, 'Vatt']))
                        if has_prev:
                            lst.append((AP(PTB, 0, 128, PTp + hl * 128, [[1, QB]]), AP(RAb, 0, 128, VA0 + b * 260 + kv * 65, [[1, 65]]), [kp_, 'Vatt']))
                    lst.append((AP(PTB, 0, 16, PTm + hl * 128, [[1, QB]]), AP(Vm, 0, 16, kv * 65, [[1, 65]]), [km_, 'Vm']))
                    for ii, (l_, r_, ks) in enumerate(lst):
                        MM(AP(BK[7], 0, QB, hl * 128, [[1, 65]]), l_, r_, ii == 0, ii == len(lst) - 1, ks, ['B7'],
                           ms=(hl == 3 and ii == len(lst) - 1))
            def normalize(i):
                bl, kv = its[i]
                b = bl0 + bl
                dk = 'den%d' % (i % 2)
                dn = AP(den_t, 0, QB, (i % 2) * 4, [[1, 4]])
                rd = AP(rden_t, 0, QB, (i % 2) * 4, [[1, 4]])
                TT('dve', dn, AP(BK[7], 0, QB, 64, [[128, 4]]), AP(expsink, 0, QB, kv * 4, [[1, 4]]), ALU.add, ['B7', 'expsink'], [dk])
                RCP(rd, dn, [dk], [dk])
                TT('dve', AP(Otok, 0, QB, kv * 256, [[64, 4], [1, 64]]), AP(BK[7], 0, QB, 0, [[128, 4], [1, 64]]),
                   AP(rden_t, 0, QB, (i % 2) * 4, [[1, 4], [0, 64]]), ALU.mult, ['B7', dk], ['Otok'])
                if kv == 3:
                    for kc in range(8):
                        TR(AP(BKb[0], 0, 128, kc * 128, [[1, QB]]), AP(Otok, 0, QB, kc * 128, [[1, 128]]), AP(ident_b, 0, QB, 0, [[1, QB]]),
                           ['Otok', 'ident_b'], ['B0'], ms=(kc == 7))
                    CP(evq(), AP(Hn8, 0, 128, b * 128, [[1024, 8], [1, QB]]), AP(BKb[0], 0, 128, 0, [[128, 8], [1, QB]]), ['B0'], ['OT'])

            nit = len(its)
            scores(0)
            if nit > 1:
                scores(1)
            expmask(0)
            for i in range(nit):
                if i + 2 < nit:
                    scores(i + 2)
                if i >= 1:
                    normalize(i - 1)
                if i + 1 < nit:
                    expmask(i + 1)
                pv(i)
            normalize(nit - 1)
        if not meta:
            CP('dve', AP(kTp, 0, 64, 0, [[128, 4], [1, 128]]), AP(RAb, 0, 64, KT0 + 1024, [[1152, 4], [1, 128]]), ['kT'], ['kTp'])
            CP('dve', AP(Vp, 0, 128, 0, [[1, 260]]), AP(RAb, 0, 128, VA0 + 8 * 260, [[1, 260]]), ['Vatt'], ['Vp'])
        ios = [rget('wo', 0), rget('wo', 1)]
        for s in range(8):
            for half in range(2):
                io = ios[half]
                bi = 2 + (ctr['bk'] % 2); ctr['bk'] += 1
                for kc in range(8):
                    MM(AP(BK[bi], 0, nt, 0, [[1, 512]]), AP(Hn8, 0, 128, kc * 1024 + s, [[8, nt]]), AP(RING, 0, 128, rb(io) + kc * 512, [[1, 512]]),
                       kc == 0, kc == 7, ['OT', rk(io)], ['B%d' % bi])
                resid_add(nt, s, half, bi)
        rdone(ios[0]); rdone(ios[1])

    def mlp(nt, l, sn=False):
        ntok = nt * 8
        nth = max(1, ntok // 512)
        tn = min(512, ntok)
        aTk = [[['aT%d.%d.%d' % (b_, fc, th) for th in range(2)] for fc in range(8)] for b_ in range(2)]
        P.claim('RA', [k for a in aTk for bb_ in a for k in bb_])

        def up(qd):
            buf = qd % 2
            for j in range(2):
                iu = rget('up', l, qd, j)
                for fcl in range(4):
                    fc = j * 4 + fcl
                    for th in range(nth):
                        bi = 2 + (ctr['bk'] % 2); ctr['bk'] += 1
                        for kc in range(8):
                            MM(AP(BK[bi], 0, 128, 0, [[1, tn]]), AP(RING, 0, 128, rb(iu) + kc * 512 + fcl * 128, [[1, 128]]),
                               AP(hnT, 0, 128, kc * 1024 + th * 512, [[1, tn]]), kc == 0, kc == 7, [rk(iu), TK[kc]], ['B%d' % bi])
                        t = tmp()
                        tv = AP(TP, 0, 128, t * 512, [[1, tn]])
                        ACT(tv, AP(BK[bi], 0, 128, 0, [[1, tn]]), AF.Relu, ['B%d' % bi], ['T%d' % t])
                        TT('dve', AP(RAb, 0, 128, buf * 8192 + fc * 1024 + th * 512, [[1, tn]]), tv, tv, ALU.mult, ['T%d' % t], [aTk[buf][fc][th]])
                rdone(iu)

        def down(qd):
            buf = qd % 2
            idn = [rget('dn', l, qd, 0), rget('dn', l, qd, 1)]
            rkeys = [k for fc in range(8) for k in aTk[buf][fc][:nth]]
            for s in range(8):
                for half in range(2):
                    bi = 4 + (ctr['bk'] % 2); ctr['bk'] += 1
                    for fc in range(8):
                        ii = idn[fc // 4]
                        MM(AP(BK[bi], 0, nt, 0, [[1, 512]]), (AP(RAb, 0, 128, buf * 8192 + fc * 1024 + s * 128, [[1, nt]]) if sn else AP(RAb, 0, 128, buf * 8192 + fc * 1024 + s, [[8, nt]])),
                           AP(RING, 0, 128, rb(ii) + (fc % 4) * 1024 + half * 512, [[1, 512]]), fc == 0, fc == 7, rkeys + [rk(ii)], ['B%d' % bi])
                    resid_add(nt, s, half, bi)
            rdone(idn[0]); rdone(idn[1])
        up(0)
        up(1)
        down(0)
        up(2)
        down(1)
        up(3)
        down(2)
        down(3)

    def ssm(nt, meta):
        UK = ['U.%d' % i for i in range(8)]
        P.claim('hnT', UK)
        SKEYS = ['Wb0', 'Vb0', 'Vb1'] + ['P1b%d' % i for i in range(4)] + ['P2b%d' % i for i in range(4)]
        P.claim('RA', SKEYS)
        P.claim('Hn8', HNK)
        for g8 in range(8):
            bi = g8 % 2
            for gl in range(8):
                g = g8 * 8 + gl
                TR(AP(BKb[bi], 0, 128, gl * 128, [[1, nt]]), AP(Hn8, 0, nt, g * 128, [[1, 128]]), AP(ident_b, 0, nt, 0, [[1, nt]]), HNK + ['ident_b'], ['B%d' % bi], ms=(gl == 7))
            CP(evq(), AP(hnT, 0, 128, g8 * 1024, [[128, 8], [1, nt]]), AP(BKb[bi], 0, 128, 0, [[128, 8], [1, nt]]), ['B%d' % bi], [UK[g8]])
        if not meta:
            P.claim('Hn8', ['Y8.%d' % i for i in range(8)])
        WB = [0, 0]
        VB = [1024, 2056]
        P1B = [6176 + i * 1024 for i in range(4)]
        P2B = [6176 + 4096 + i * 1024 for i in range(4)]
        pend = {}

        def stage1(q, ia):
            for bb_ in range(2):
                gb = 2 * q + bb_
                i2 = gb % 2
                g0 = gb * 8
                for gl in range(8):
                    gq = bb_ * 8 + gl
                    b1, b2 = 4 + gl // 4, 6 + gl // 4
                    rhs = AP(hnT, 0, 128, (g0 + gl) * 128, [[1, nt]])
                    MM(AP(BK[b1], 0, 128, (gl % 4) * 128, [[1, nt]]), AP(RING, 0, 128, rb(ia) + gq * 128, [[1, 128]]), rhs, True, True, [rk(ia), UK[gb]], ['B%d' % b1], ms=False)
                    MM(AP(BK[b2], 0, 128, (gl % 4) * 128, [[1, nt]]), AP(RING, 0, 128, rb(ia) + 2048 + gq * 128, [[1, 128]]), rhs, True, True, [rk(ia), UK[gb]], ['B%d' % b2], ms=(gl == 7))
                i4 = gb % 4
                wk, vk, p1k, p2k = 'Wb0', 'Vb%d' % i2, 'P1b%d' % i4, 'P2b%d' % i4
                for hb in range(2):
                    b1, b2 = 4 + hb, 6 + hb
                    d3 = [[128, 4], [1, nt]]
                    cosv = AP(COSt, 0, 128, (g0 + hb * 4) * 129 + 1, [[129, 4], [1, nt]])
                    sinv = AP(SINt, 0, 128, (g0 + hb * 4) * 129 + 1, [[129, 4], [1, nt]])
                    wv = AP(RA, 0, 128, WB[i2] + hb * 512, d3)
                    t = tmp()
                    tv = AP(TP, 0, 128, t * 512, d3)
                    TT('dve', wv, AP(BK[b1], 0, 128, 0, d3), cosv, ALU.mult, ['B%d' % b1, 'COSt'], [wk])
                    TT('dve', tv, AP(BK[b2], 0, 128, 0, d3), sinv, ALU.mult, ['B%d' % b2, 'SINt'], ['T%d' % t])
                    TT('dve', wv, wv, tv, ALU.add, [wk, 'T%d' % t], [wk])
                CP('dve', AP(RA, 0, 128, VB[i2], [[129, 8]]), AP(Xcar, 0, 128, g0, [[1, 8]]), ['Xcar.%d' % gb], [vk])
                for gl in range(8):
                    P.op('dve', lambda e, gl=gl, i2=i2, g0=g0: e.tensor_tensor_scan(
                        out=AP(RA, 0, 128, VB[i2] + gl * 129 + 1, [[1, nt]]), data0=AP(R8d, 0, 128, g0 + gl, [[0, nt]]),
                        data1=AP(RA, 0, 128, WB[i2] + gl * 128, [[1, nt]]), initial=AP(RA, 0, 128, VB[i2] + gl * 129, [[1, 1]]),
                        op0=ALU.mult, op1=ALU.add), [wk, vk, 'R8d'], [vk])
                if not meta:
                    vv = AP(RA, 0, 128, VB[i2], [[129, 8], [1, nt]])
                    TT('dve', AP(RAb, 0, 128, P1B[i4], [[128, 8], [1, nt]]), vv, AP(COSt, 0, 128, g0 * 129, [[129, 8], [1, nt]]), ALU.mult, [vk, 'COSt'], [p1k])
                    TT('dve', AP(RAb, 0, 128, P2B[i4], [[128, 8], [1, nt]]), vv, AP(SINt, 0, 128, g0 * 129, [[129, 8], [1, nt]]), ALU.mult, [vk, 'SINt'], [p2k])
                vl = lambda p0: AP(RA, p0, 64, VB[i2] + nt, [[129, 8]])
                sk = 'vsw%d' % i2
                P.dma('sp', AP(vsw_t, 64, 64, i2 * 8, [[1, 8]]), vl(0), sk, [vk], [sk], allow_slow_non_contiguous=True)
                P.dma('sp', AP(vsw_t, 0, 64, i2 * 8, [[1, 8]]), vl(64), sk, [vk], [sk], allow_slow_non_contiguous=True)
                pend[gb] = (i2, g0, vk, sk)

        def carry(gb):
            i2, g0, vk, sk = pend.pop(gb)
            c1 = AP(ct_t, 0, 128, i2 * 16, [[1, 8]])
            c2 = AP(ct_t, 0, 128, i2 * 16 + 8, [[1, 8]])
            ck = 'ct%d' % i2
            TT('dve', c1, AP(RA, 0, 128, VB[i2] + nt, [[129, 8]]), AP(COSt, 0, 128, g0 * 129 + nt, [[129, 8]]), ALU.mult, [vk, 'COSt'], [ck])
            TT('dve', c2, AP(vsw_t, 0, 128, i2 * 8, [[1, 8]]), AP(SINt, 0, 128, g0 * 129 + nt, [[129, 8]]), ALU.mult, [sk, 'SINt'], [ck])
            STT('dve', AP(Xcar, 0, 128, g0, [[1, 8]]), c2, AP(SGNI, 0, 128, 0, [[1, 1]]), c1, ALU.mult, ALU.add, [ck, 'SGNI'], ['Xcar.%d' % gb])

        def stage2(q, ib, ic):
            for bb_ in range(2):
                gb = 2 * q + bb_
                i2 = gb % 2
                g0 = gb * 8
                i4 = gb % 4
                p1k, p2k = 'P1b%d' % i4, 'P2b%d' % i4
                for hb in range(2):
                    bi = 2 + (ctr['bk'] % 2); ctr['bk'] += 1
                    for g4 in range(4):
                        gl = hb * 4 + g4
                        gq = bb_ * 8 + gl
                        o = AP(BK[bi], 0, nt, g4 * 128, [[1, 128]])
                        MM(o, AP(hnT, 0, 128, (g0 + gl) * 128, [[1, nt]]), AP(RING, 0, 128, rb(ib) + gq * 128, [[1, 128]]), True, False, ['U.%d' % gb, rk(ib)], ['B%d' % bi])
                        MM(o, AP(RAb, 0, 128, P1B[i4] + gl * 128, [[1, nt]]), AP(RING, 0, 128, rb(ib) + 2048 + gq * 128, [[1, 128]]), False, False, [p1k, rk(ib)], ['B%d' % bi])
                        MM(o, AP(RAb, 0, 128, P2B[i4] + gl * 128, [[1, nt]]), AP(RING, 0, 128, rb(ic) + gq * 128, [[1, 128]]), False, True, [p2k, rk(ic)], ['B%d' % bi], ms=(g4 == 3))
                    ACT(AP(Hn8, 0, nt, (g0 + hb * 4) * 16, [[16, 4], [1024, 8], [1, 16]]), AP(BK[bi], 0, nt, 0, [[128, 4], [16, 8], [1, 16]]), AF.Gelu_apprx_tanh,
                        ['B%d' % bi], ['Y8.%d' % gb])

        if meta:
            ias = []
            for q in range(4):
                ia = rget('A', q)
                stage1(q, ia)
                rdone(ia)
                carry(2 * q)
                carry(2 * q + 1)
            return
        ia = rget('A', 0)
        stage1(0, ia)
        rdone(ia)
        for q in range(4):
            carry(2 * q)
            carry(2 * q + 1)
            if q + 1 < 4:
                ia = rget('A', q + 1)
                stage1(q + 1, ia)
                rdone(ia)
            ib = rget('B', q)
            ic = rget('C', q)
            stage2(q, ib, ic)
            rdone(ib); rdone(ic)
        YK = ['Y8.%d' % i for i in range(8)]
        to_featmajor(nt, None, YK, sn=True)
        gl_iv = [None, None]
        gl_ig = [None, None]
        for half in range(2):
            gl_iv[half] = rget('glu', 0, half)
            gl_ig[half] = rget('glu', 1, half)
        for s in range(8):
            for half in range(2):
                iv, ig = gl_iv[half], gl_ig[half]
                bv = 2 + (ctr['bk'] % 2); ctr['bk'] += 1
                bg = 4 + (ctr['bk'] % 2)
                for kc in range(8):
                    lt = AP(hnT, 0, 128, kc * 1024 + s * 128, [[1, nt]])
                    MM(AP(BK[bv], 0, nt, 0, [[1, 512]]), lt, AP(RING, 0, 128, rb(iv) + kc * 512, [[1, 512]]), kc == 0, kc == 7, [TK[kc], rk(iv)], ['B%d' % bv])
                for kc in range(8):
                    lt = AP(hnT, 0, 128, kc * 1024 + s * 128, [[1, nt]])
                    MM(AP(BK[bg], 0, nt, 0, [[1, 512]]), lt, AP(RING, 0, 128, rb(ig) + kc * 512, [[1, 512]]), kc == 0, kc == 7, [TK[kc], rk(ig)], ['B%d' % bg])
                t = tmp()
                tv = AP(TP, 0, nt, t * 512, [[1, 512]])
                ACT(tv, AP(BK[bg], 0, nt, 0, [[1, 512]]), AF.Sigmoid, ['B%d' % bg], ['T%d' % t])
                TT('dve', tv, AP(BK[bv], 0, nt, 0, [[1, 512]]), tv, ALU.mult, ['B%d' % bv, 'T%d' % t], ['T%d' % t])
                hs = AP(H8, 0, nt, s * 1024 + half * 512, [[1, 512]])
                TT('dve', hs, hs, tv, ALU.add, [HK[s], 'T%d' % t], [HK[s]])
        for half in range(2):
            rdone(gl_iv[half]); rdone(gl_ig[half])

    def layer0(nt, meta, first):
        rmsnorm(nt, 'std')
        to_featmajor(nt, 0, HNK)
        attention(nt, meta, first)
        rmsnorm(nt, 'std')
        to_featmajor(nt, 1, HNK, sn=(nt == 128))
        mlp(nt, 0, sn=(nt == 128))

    def layer1(nt):
        rmsnorm(nt, 'gsc')
        ssm(nt, False)
        rmsnorm(nt, 'std')
        to_featmajor(nt, 2, HNK, sn=True)
        mlp(nt, 1, sn=True)

    MS('dve', AP(Xcar, 0, 128, 0, [[1, 64]]), 0.0, ['Xcar.%d' % i for i in range(8)])
    MS('dve', AP(ss_t, 0, 128, 0, [[1, 8]]), 0.0, ['ss.%d' % s for s in range(8)])
    P.dma('sp', AP(H8, 0, 2, 0, [[1, 8192]]), DR(meta_d, 0, [8192, 2], [1, 8192]), 'xin', (), HK)
    layer0(2, True, True)
    rmsnorm(2, 'gsc')
    ssm(2, True)
    CP('dve', AP(Xmeta, 0, 128, 0, [[1, 64]]), AP(Xcar, 0, 128, 0, [[1, 64]]), ['Xcar.%d' % i for i in range(8)], ['Xmeta'])
    P.emit()

    if stop == 'meta':
        return finish()
    for ti in range(ntiles):
        sq, tl = ti // 4, ti % 4
        first = (tl == 0)
        off = (sq * SEQ + tl * 1024) * D
        for s in range(8):
            P.dma('sp', AP(H8, 0, 128, s * 1024, [[1, 1024]]), DR(x_d, off + s * 1024, [8192, 128], [1, 1024]), 'xin%d' % s, (), [HK[s]])
        if first:
            CP('dve', AP(Xcar, 0, 128, 0, [[1, 64]]), AP(Xmeta, 0, 128, 0, [[1, 64]]), ['Xmeta'], ['Xcar.%d' % i for i in range(8)])
        layer0(128, False, first)
        if debug_stage != 1:
            layer1(128)
            SSK = ['ss.%d' % s for s in range(8)]
            RSK = ['rstd.%d' % s for s in range(8)]

            def fsq(s):
                ACT(AP(junk, 0, 128, 0, [[1, 1024]]), AP(H8, 0, 128, s * 1024, [[1, 1024]]), AF.Square, [HK[s], SSK[s]], [SSK[s]],
                    accum_out=AP(ss_t, 0, 128, s, [[1, 1]]))

            def frs(s0, n):
                ACT(AP(rstd_t, 0, 128, s0, [[1, n]]), AP(ss_t, 0, 128, s0, [[1, n]]), AF.Sqrt, SSK[s0:s0 + n] + ['epsc'], RSK[s0:s0 + n],
                    bias=AP(epsc, 0, 128, 0, [[1, 1]]), scale=float(1.0 / D))
                RCP(AP(rstd_t, 0, 128, s0, [[1, n]]), AP(rstd_t, 0, 128, s0, [[1, n]]), RSK[s0:s0 + n], RSK[s0:s0 + n])

            def fsc(s):
                hs = AP(H8, 0, 128, s * 1024, [[1, 1024]])
                STT('dve', hs, hs, AP(rstd_t, 0, 128, s, [[1, 1]]), AP(wfin, 0, 128, 0, [[1, 1024]]), ALU.mult, ALU.mult, [HK[s], RSK[s], 'wfin'], [HK[s]])
            for (s0, n) in ((0, 4), (4, 3), (7, 1)):
                for s in range(s0, s0 + n):
                    fsq(s)
                frs(s0, n)
                for s in range(s0, s0 + n):
                    fsc(s)
            MS('dve', AP(ss_t, 0, 128, 0, [[1, 8]]), 0.0, SSK)
        for s in range(8):
            P.dma('sp', DR(y_d, off + s * 1024, [8192, 128], [1, 1024]), AP(H8, 0, 128, s * 1024, [[1, 1024]]), 'yout%d' % s, [HK[s]], ())
        P.emit()
    waits = P._waits('sp', (), HK)
    P.ops['sp'].append((waits, None, None, 0))
    P.emit()
    es.close()
    return nc


_INPUT_NAMES = ["meta_tokens", "attn_norm_w", "attn_w_qkv", "attn_sinks", "attn_w_o", "ssm_norm_w", "ssm_lambda_re",
                "ssm_lambda_im", "ssm_log_dt", "ssm_b_re", "ssm_b_im", "ssm_c_re", "ssm_c_im", "ssm_d", "ssm_w_glu",
                "mlp_norm_w", "mlp_w_up", "mlp_w_down", "final_norm_w"]


def kernel(**inputs):
    x = np.ascontiguousarray(np.asarray(inputs["x"], dtype=np.float32))
    shared = {k: np.ascontiguousarray(np.asarray(inputs[k], dtype=np.float32)) for k in _INPUT_NAMES}
    nc = build()
    in_maps = []
    for c in range(NCORE):
        m = dict(shared)
        m["x"] = x[2 * c:2 * c + 2]
        in_maps.append(m)
    res = run_bass_kernel_spmd(nc, in_maps, core_ids=list(range(NCORE)))
    out = np.concatenate([np.asarray(r["y"], dtype=np.float32) for r in res.results], axis=0)
    return out
```
